# Optimizing a Trainium2 kernel written in Bass

```python
import math
import jax
import jax.numpy as jnp
from jax import lax
import numpy as np


D_MODEL = 1024
BATCH = 16
SEQ = 2048
DEPTH = 2

GRID_W = 64
CTX_LEN = 256
RMS_EPS = 1e-6
N_MOD = 6
SHORT_W = 3

HY_WIDTH = 512
HY_EMB = 33
HY_FF = 64
HY_FAST = 0.3
HY_SLOW = 1.5
HY_TARGET = 1e-2
HY_SHIFT = 0.05

ML_HEADS = 4
ML_HEAD_DIM = 128
ML_WIDTH = ML_HEADS * ML_HEAD_DIM
ML_CHUNK = 64
IN0_COLS = 3 * HY_WIDTH + 4 * ML_WIDTH + 4 * ML_HEADS

MLA_HEADS = 8
MLA_NOPE = 128
MLA_ROPE = 64
MLA_V = 128
MLA_Q_RANK = 384
MLA_KV_RANK = 256
MLA_SCALE = (MLA_NOPE + MLA_ROPE) ** -0.5
ROPE_AXIS = MLA_ROPE // 2
ROPE_BASE = 10000.0
Q_BLOCK = 128

D_FF = 2816

kernel_name = 'hybrid_hyena_mlstm_mla_prefix_dit'

F32 = jnp.float32


def rmsnorm(x, g):
    xf = x.astype(F32)
    y = xf * lax.rsqrt(jnp.mean(xf * xf, axis=-1, keepdims=True) + RMS_EPS)
    return (y * g.astype(F32)).astype(x.dtype)


def ada_mod(cv, w, b):
    m = jax.nn.silu(cv) @ w + b
    return jnp.split(m, N_MOD, axis=-1)


def dwconv(x, w):
    k = w.shape[0]
    return lax.conv_general_dilated(x, w[:, None, :].astype(x.dtype), (1,), [(k // 2, k // 2)],
                                    dimension_numbers=('NWC', 'WIO', 'NWC'),
                                    feature_group_count=x.shape[-1])


def hyena_filters(L, fw1, fb1, fw2, fb2, fw3, freq):
    pos = jnp.arange(L, dtype=F32)
    t = (pos / (L - 1))[:, None]
    bands = (HY_EMB - 1) // 2
    f = jnp.linspace(1e-4, bands - 1, bands, dtype=F32)
    ang = (2.0 * math.pi / L) * pos[:, None] * f[None, :]
    z = jnp.concatenate([t, jnp.cos(ang), -jnp.sin(ang)], axis=-1)
    fq = freq.astype(F32)
    h = jnp.sin(fq * (z @ fw1.astype(F32) + fb1.astype(F32)))
    h = jnp.sin(fq * (h @ fw2.astype(F32) + fb2.astype(F32)))
    h = h @ fw3.astype(F32)
    deltas = jnp.linspace(math.log(HY_TARGET) / HY_FAST, math.log(HY_TARGET) / HY_SLOW, HY_WIDTH, dtype=F32)
    decay = jnp.exp(-t * jnp.abs(jnp.tile(deltas, 2))[None, :])
    return h * (decay + HY_SHIFT)


def hyena_mix(u, conv_w, conv_b, filt, bias):
    u = dwconv(u, conv_w) + conv_b
    x0, x1, v = jnp.split(u, 3, axis=-1)
    L = u.shape[1]
    g = (v * x1).astype(F32)
    h = hyena_filters(L, *filt)
    hf, hb = h[:, :HY_WIDTH], h[:, HY_WIDTH:]
    k2 = jnp.concatenate([hf, jnp.zeros((1, HY_WIDTH), F32), hb[:0:-1]], axis=0)
    n = 2 * L
    y = jnp.fft.irfft(jnp.fft.rfft(g, n=n, axis=1) * jnp.fft.rfft(k2, axis=0)[None], n=n, axis=1)[:, :L]
    y = y + g * bias.astype(F32)
    return y.astype(u.dtype) * x0


def mlstm_chunkwise(q, k, v, ig, lf, state):
    B, H, T, d = q.shape
    nc = T // ML_CHUNK

    def chunks(a):
        return jnp.moveaxis(a.reshape((B, H, nc, ML_CHUNK) + a.shape[3:]), 2, 0)

    causal = jnp.tril(jnp.ones((ML_CHUNK, ML_CHUNK), dtype=bool))

    def step(carry, inp):
        C, nv, m = carry
        qc, kc, vc, ic, fc = inp
        b = jnp.cumsum(fc, axis=-1)
        dlog = jnp.where(causal, b[..., :, None] - b[..., None, :] + ic[..., None, :], -jnp.inf)
        inter = b + m[..., None]
        m_t = jnp.maximum(inter, jnp.max(dlog, axis=-1))
        w_in = jnp.exp(dlog - m_t[..., None])
        w_st = jnp.exp(inter - m_t)
        s = jnp.einsum('bhtd,bhsd->bhts', qc, kc) * w_in
        num = jnp.einsum('bhts,bhsd->bhtd', s, vc) + w_st[..., None] * jnp.einsum('bhve,bhte->bhtv', C, qc)
        den = jnp.sum(s, axis=-1) + w_st * jnp.einsum('bhe,bhte->bht', nv, qc)
        h = num / jnp.maximum(jnp.abs(den), jnp.exp(-m_t))[..., None]
        bl = b[..., -1]
        gs = bl[..., None] - b + ic
        m_new = jnp.maximum(bl + m, jnp.max(gs, axis=-1))
        a = jnp.exp(bl + m - m_new)
        ws = jnp.exp(gs - m_new[..., None])
        C_new = a[..., None, None] * C + jnp.einsum('bhs,bhsv,bhse->bhve', ws, vc, kc)
        n_new = a[..., None] * nv + jnp.einsum('bhs,bhse->bhe', ws, kc)
        return (C_new, n_new, m_new), h

    state, hs = lax.scan(step, state, (chunks(q), chunks(k), chunks(v), chunks(ig), chunks(lf)))
    return jnp.moveaxis(hs, 0, 2).reshape(B, H, T, d), state


def mlstm_direction(q, k, v, ig, f_pre, state, reverse):
    lf = jax.nn.log_sigmoid(f_pre)
    if reverse:
        q, k, v = jnp.flip(q, 2), jnp.flip(k, 2), jnp.flip(v, 2)
        ig, lf = jnp.flip(ig, -1), jnp.flip(lf, -1)
    h, state = mlstm_chunkwise(q, k, v, ig, lf, state)
    if reverse:
        h = jnp.flip(h, 2)
    return h, state


def mlstm_inputs(u, conv_w, gate_b):
    B, L, _ = u.shape
    off = 3 * HY_WIDTH
    qk = jax.nn.silu(dwconv(u[..., off:off + 2 * ML_WIDTH], conv_w))
    q, k = jnp.split(qk, 2, axis=-1)
    v = u[..., off + 2 * ML_WIDTH:off + 3 * ML_WIDTH]
    og = u[..., off + 3 * ML_WIDTH:off + 4 * ML_WIDTH]
    gates = u[..., off + 4 * ML_WIDTH:].astype(F32) + gate_b.astype(F32)
    gates = jnp.transpose(gates.reshape(B, L, 4, ML_HEADS), (2, 0, 3, 1))

    def heads(a):
        return jnp.transpose(a.reshape(B, L, ML_HEADS, ML_HEAD_DIM), (0, 2, 1, 3)).astype(F32)

    return heads(q) * ML_HEAD_DIM ** -0.5, heads(k), heads(v), og, gates


def mlstm_out(h, og, norm_g):
    B, H, L, d = h.shape
    t = jnp.swapaxes(h, 1, 2)
    t = t * lax.rsqrt(jnp.mean(t * t, axis=-1, keepdims=True) + RMS_EPS)
    t = t.reshape(B, L, H * d) * norm_g.astype(F32)
    return (t * jax.nn.sigmoid(og.astype(F32))).astype(og.dtype)


def mlstm_mix(u_ctx, u_lat, conv_w, gate_b, norm_g):
    qc, kc, vc, oc, gc = mlstm_inputs(u_ctx, conv_w, gate_b)
    ql, kl, vl, ol, gl = mlstm_inputs(u_lat, conv_w, gate_b)
    B = u_ctx.shape[0]
    zero = (jnp.zeros((B, ML_HEADS, ML_HEAD_DIM, ML_HEAD_DIM), F32),
            jnp.zeros((B, ML_HEADS, ML_HEAD_DIM), F32),
            jnp.zeros((B, ML_HEADS), F32))
    hc_f, st_f = mlstm_direction(qc, kc, vc, gc[0], gc[1], zero, False)
    hl_f, _ = mlstm_direction(ql, kl, vl, gl[0], gl[1], st_f, False)
    hc_b, st_b = mlstm_direction(qc, kc, vc, gc[2], gc[3], zero, True)
    hl_b, _ = mlstm_direction(ql, kl, vl, gl[2], gl[3], st_b, True)
    return mlstm_out(hc_f + hc_b, oc, norm_g), mlstm_out(hl_f + hl_b, ol, norm_g)


def hyena_mlstm_mixer(h_ctx, h_lat, w_in, hy_conv_w, hy_conv_b, hy_filter, hy_bias,
                      ml_conv_w, ml_gate_b, ml_norm_g, w_out):
    u_ctx = h_ctx @ w_in
    u_lat = h_lat @ w_in
    hy_c = hyena_mix(u_ctx[..., :3 * HY_WIDTH], hy_conv_w, hy_conv_b, hy_filter, hy_bias)
    hy_l = hyena_mix(u_lat[..., :3 * HY_WIDTH], hy_conv_w, hy_conv_b, hy_filter, hy_bias)
    ml_c, ml_l = mlstm_mix(u_ctx, u_lat, ml_conv_w, ml_gate_b, ml_norm_g)
    y_ctx = jnp.concatenate([hy_c, ml_c], axis=-1) @ w_out
    y_lat = jnp.concatenate([hy_l, ml_l], axis=-1) @ w_out
    return y_ctx, y_lat


def rope_tables(T):
    n_rows = T // GRID_W
    row = jnp.repeat(jnp.arange(n_rows, dtype=F32), GRID_W)
    col = jnp.tile(jnp.arange(GRID_W, dtype=F32), n_rows)
    inv = ROPE_BASE ** (-jnp.arange(0, ROPE_AXIS, 2, dtype=F32) / ROPE_AXIS)
    ang = jnp.stack([row[:, None] * inv, col[:, None] * inv], axis=1)
    return jnp.cos(ang), jnp.sin(ang)


def apply_rope2d(x, cos, sin):
    xa = x.reshape(x.shape[:-1] + (2, ROPE_AXIS))
    x1, x2 = jnp.split(xa, 2, axis=-1)
    c = cos[None, :, None].astype(x.dtype)
    s = sin[None, :, None].astype(x.dtype)
    return jnp.concatenate([x1 * c - x2 * s, x2 * c + x1 * s], axis=-1).reshape(x.shape)


def mla_down(h, w_down):
    return jnp.split(h @ w_down, [MLA_Q_RANK, MLA_Q_RANK + MLA_KV_RANK], axis=-1)


def mla_q(q_a, q_norm, w_uq, rope):
    B, L, _ = q_a.shape
    q = (rmsnorm(q_a, q_norm) @ w_uq).reshape(B, L, MLA_HEADS, MLA_NOPE + MLA_ROPE)
    if rope is None:
        return q
    q_n, q_r = jnp.split(q, [MLA_NOPE], axis=-1)
    return jnp.concatenate([q_n, apply_rope2d(q_r, *rope)], axis=-1)


def mla_kv(kv_a, k_r, kv_norm, w_ukv, rope):
    B, L, _ = kv_a.shape
    kv = (rmsnorm(kv_a, kv_norm) @ w_ukv).reshape(B, L, MLA_HEADS, MLA_NOPE + MLA_V)
    k_n, v = jnp.split(kv, [MLA_NOPE], axis=-1)
    k_r = k_r[:, :, None, :]
    if rope is not None:
        k_r = apply_rope2d(k_r, *rope)
    k = jnp.concatenate([k_n, jnp.broadcast_to(k_r, (B, L, MLA_HEADS, MLA_ROPE))], axis=-1)
    return k, v


def attend(q, k, v):
    s = jnp.einsum('bqhd,bkhd->bhqk', q, k).astype(F32) * MLA_SCALE
    p = jax.nn.softmax(s, axis=-1).astype(v.dtype)
    return jnp.einsum('bhqk,bkhd->bqhd', p, v)


def mla_mixer(h_ctx, h_lat, ctx_queries, w_down, q_norm, kv_norm, w_uq, w_ukv, w_o):
    B, S, _ = h_lat.shape
    rope = rope_tables(S)
    qa_c, kva_c, kr_c = mla_down(h_ctx, w_down)
    qa_l, kva_l, kr_l = mla_down(h_lat, w_down)
    k_c, v_c = mla_kv(kva_c, kr_c, kv_norm, w_ukv, None)
    k_l, v_l = mla_kv(kva_l, kr_l, kv_norm, w_ukv, rope)
    q_l = mla_q(qa_l, q_norm, w_uq, rope)
    k_all = jnp.concatenate([k_c, k_l], axis=1)
    v_all = jnp.concatenate([v_c, v_l], axis=1)
    nb = S // Q_BLOCK
    qb = jnp.moveaxis(q_l.reshape(B, nb, Q_BLOCK, MLA_HEADS, MLA_NOPE + MLA_ROPE), 1, 0)
    o_l = lax.map(lambda blk: attend(blk, k_all, v_all), qb)
    y_lat = jnp.moveaxis(o_l, 0, 1).reshape(B, S, MLA_HEADS * MLA_V) @ w_o
    y_ctx = None
    if ctx_queries:
        q_c = mla_q(qa_c, q_norm, w_uq, None)
        y_ctx = attend(q_c, k_c, v_c).reshape(B, h_ctx.shape[1], MLA_HEADS * MLA_V) @ w_o
    return y_ctx, y_lat


def conv_ffn(h, w_up, conv_w, conv_b, w_down):
    g, u = jnp.split(h @ w_up, 2, axis=-1)
    g = dwconv(g, conv_w) + conv_b
    return (jax.nn.silu(g) * u) @ w_down


def setup_inputs(seed: int = 0) -> dict:
    key = jax.random.key(seed)
    keys = iter(jax.random.split(key, 64))

    def nrm(shape, scale):
        return jax.random.normal(next(keys), shape, F32) * scale

    def gain(n):
        return 1.0 + nrm((n,), 0.02)

    def ffn():
        return (nrm((D_MODEL, 2 * D_FF), D_MODEL ** -0.5), nrm((SHORT_W, D_FF), SHORT_W ** -0.5),
                nrm((D_FF,), 0.02), nrm((D_FF, D_MODEL), D_FF ** -0.5))

    x = nrm((BATCH, SEQ, D_MODEL), 1.0)
    c = nrm((BATCH, D_MODEL), 1.0)
    ctx = nrm((BATCH, CTX_LEN, D_MODEL), 1.0)
    c_ctx = nrm((D_MODEL,), 1.0)
    forget_b = jnp.linspace(3.0, 6.0, ML_HEADS, dtype=F32)
    ml_gate_b = jnp.concatenate([nrm((ML_HEADS,), 0.1), forget_b + nrm((ML_HEADS,), 0.01),
                                 nrm((ML_HEADS,), 0.1), forget_b + nrm((ML_HEADS,), 0.01)])
    f0 = ffn()
    f1 = ffn()
    mix_w = HY_WIDTH + ML_WIDTH
    return {
        'x': x, 'c': c, 'ctx': ctx, 'c_ctx': c_ctx,
        'ada_w_0': nrm((D_MODEL, N_MOD * D_MODEL), 0.5 * D_MODEL ** -0.5),
        'ada_b_0': nrm((N_MOD * D_MODEL,), 0.02),
        'norm_mix_0': gain(D_MODEL), 'norm_ffn_0': gain(D_MODEL),
        'w_in_0': nrm((D_MODEL, IN0_COLS), D_MODEL ** -0.5),
        'hy_conv_w': nrm((SHORT_W, 3 * HY_WIDTH), SHORT_W ** -0.5),
        'hy_conv_b': nrm((3 * HY_WIDTH,), 0.02),
        'hy_fw1': nrm((HY_EMB, HY_FF), HY_EMB ** -0.5),
        'hy_fb1': nrm((HY_FF,), 0.02),
        'hy_fw2': nrm((HY_FF, HY_FF), HY_FF ** -0.5),
        'hy_fb2': nrm((HY_FF,), 0.02),
        'hy_fw3': nrm((HY_FF, 2 * HY_WIDTH), 0.05 * HY_FF ** -0.5),
        'hy_freq': gain(HY_FF),
        'hy_bias': nrm((HY_WIDTH,), 0.1),
        'ml_conv_w': nrm((SHORT_W, 2 * ML_WIDTH), SHORT_W ** -0.5),
        'ml_gate_b': ml_gate_b,
        'ml_norm_g': gain(ML_WIDTH),
        'w_out_0': nrm((mix_w, D_MODEL), mix_w ** -0.5),
        'ffn_up_0': f0[0], 'ffn_conv_w_0': f0[1], 'ffn_conv_b_0': f0[2], 'ffn_down_0': f0[3],
        'ada_w_1': nrm((D_MODEL, N_MOD * D_MODEL), 0.5 * D_MODEL ** -0.5),
        'ada_b_1': nrm((N_MOD * D_MODEL,), 0.02),
        'norm_mix_1': gain(D_MODEL), 'norm_ffn_1': gain(D_MODEL),
        'mla_w_down': nrm((D_MODEL, MLA_Q_RANK + MLA_KV_RANK + MLA_ROPE), D_MODEL ** -0.5),
        'mla_q_norm': gain(MLA_Q_RANK), 'mla_kv_norm': gain(MLA_KV_RANK),
        'mla_w_uq': nrm((MLA_Q_RANK, MLA_HEADS * (MLA_NOPE + MLA_ROPE)), MLA_Q_RANK ** -0.5),
        'mla_w_ukv': nrm((MLA_KV_RANK, MLA_HEADS * (MLA_NOPE + MLA_V)), MLA_KV_RANK ** -0.5),
        'mla_w_o': nrm((MLA_HEADS * MLA_V, D_MODEL), (MLA_HEADS * MLA_V) ** -0.5),
        'ffn_up_1': f1[0], 'ffn_conv_w_1': f1[1], 'ffn_conv_b_1': f1[2], 'ffn_down_1': f1[3],
        'final_norm': gain(D_MODEL),
    }


def reference(x, c, ctx, c_ctx, ada_w_0, ada_b_0, norm_mix_0, norm_ffn_0, w_in_0, hy_conv_w, hy_conv_b,
              hy_fw1, hy_fb1, hy_fw2, hy_fb2, hy_fw3, hy_freq, hy_bias, ml_conv_w, ml_gate_b, ml_norm_g,
              w_out_0, ffn_up_0, ffn_conv_w_0, ffn_conv_b_0, ffn_down_0, ada_w_1, ada_b_1, norm_mix_1,
              norm_ffn_1, mla_w_down, mla_q_norm, mla_kv_norm, mla_w_uq, mla_w_ukv, mla_w_o, ffn_up_1,
              ffn_conv_w_1, ffn_conv_b_1, ffn_down_1, final_norm):
    ada_w = (ada_w_0, ada_w_1)
    ada_b = (ada_b_0, ada_b_1)
    norm_mix = (norm_mix_0, norm_mix_1)
    norm_ffn = (norm_ffn_0, norm_ffn_1)
    ffn_p = ((ffn_up_0, ffn_conv_w_0, ffn_conv_b_0, ffn_down_0),
             (ffn_up_1, ffn_conv_w_1, ffn_conv_b_1, ffn_down_1))
    hy_filter = (hy_fw1, hy_fb1, hy_fw2, hy_fb2, hy_fw3, hy_freq)
    x_lat, x_ctx = x, ctx
    for layer in range(DEPTH):
        last = layer == DEPTH - 1
        sh1, sc1, g1, sh2, sc2, g2 = ada_mod(c[:, None, :], ada_w[layer], ada_b[layer])
        csh1, csc1, cg1, csh2, csc2, cg2 = ada_mod(c_ctx, ada_w[layer], ada_b[layer])
        h_lat = rmsnorm(x_lat, norm_mix[layer]) * (1 + sc1) + sh1
        h_ctx = rmsnorm(x_ctx, norm_mix[layer]) * (1 + csc1) + csh1
        if layer % 2 == 0:
            y_ctx, y_lat = hyena_mlstm_mixer(h_ctx, h_lat, w_in_0, hy_conv_w, hy_conv_b, hy_filter, hy_bias,
                                             ml_conv_w, ml_gate_b, ml_norm_g, w_out_0)
        else:
            y_ctx, y_lat = mla_mixer(h_ctx, h_lat, not last, mla_w_down, mla_q_norm, mla_kv_norm,
                                     mla_w_uq, mla_w_ukv, mla_w_o)
        x_lat = x_lat + g1 * y_lat
        x_lat = x_lat + g2 * conv_ffn(rmsnorm(x_lat, norm_ffn[layer]) * (1 + sc2) + sh2, *ffn_p[layer])
        if not last:
            x_ctx = x_ctx + cg1 * y_ctx
            x_ctx = x_ctx + cg2 * conv_ffn(rmsnorm(x_ctx, norm_ffn[layer]) * (1 + csc2) + csh2, *ffn_p[layer])
    return rmsnorm(x_lat, final_norm)
```

```python
import contextlib
import math
import numpy as np
import ml_dtypes
import concourse.bass as bass
import concourse.mybir as mybir
from concourse.bass_utils import run_bass_kernel_spmd

F32 = mybir.dt.float32
BF16 = mybir.dt.bfloat16
AF = mybir.ActivationFunctionType
ALU = mybir.AluOpType
AX = mybir.AxisListType

NCORES = 8
D = 1024
KT = 8
LC = 256
LL = 2048
T = LC + LL
NB = 2
EPS = 1e-6
DFF = 2816
FT = 22
CHUNKS = [(0, 256)] + [(256 + 512 * i, 512) for i in range(4)]


class Tok:
    __slots__ = ("name", "last_w", "readers", "excl")

    def __init__(self, name="", excl=False):
        self.name = name
        self.last_w = None
        self.readers = []
        self.excl = excl


class Op:
    __slots__ = ("eng", "fn", "deps", "needs_inc", "is_dma", "sem", "semval")

    def __init__(self, eng, fn, deps, is_dma):
        self.eng = eng
        self.fn = fn
        self.deps = deps
        self.needs_inc = False
        self.is_dma = is_dma
        self.sem = None
        self.semval = None


class Prog:
    def __init__(self, nc, n_dma_sems=8):
        self.nc = nc
        self.ops = []
        self.n_dma_sems = n_dma_sems
        self.barrier_op = None
        self.bar_dram = None
        self.since_barrier = []

    def op(self, eng, fn, reads=(), writes=(), dma=False):
        deps = []
        seen = set()

        def add(o):
            if o is None or id(o) in seen:
                return
            if (not dma) and eng == "pe" and o.eng == "pe" and not o.is_dma:
                return
            seen.add(id(o))
            deps.append(o)

        add(self.barrier_op)
        for t in reads:
            add(t.last_w)
            if t.excl:
                for r in t.readers:
                    add(r)
        for t in writes:
            add(t.last_w)
            for r in t.readers:
                add(r)
        o = Op(eng, fn, deps, dma)
        for d in deps:
            d.needs_inc = True
        for t in reads:
            if t.excl:
                t.last_w = o
                t.readers = []
            else:
                t.readers.append(o)
        for t in writes:
            t.last_w = o
            t.readers = []
        self.ops.append(o)
        self.since_barrier.append(o)
        return o

    def dma(self, eng, fn, reads=(), writes=()):
        return self.op(eng, fn, reads, writes, dma=True)

    def barrier(self):
        nc = self.nc
        last = {}
        deps = []
        for o in self.since_barrier:
            if o.is_dma:
                deps.append(o)
            else:
                last[o.eng] = o
        deps.extend(last.values())
        if self.barrier_op is not None:
            deps.append(self.barrier_op)
        if self.bar_dram is None:
            self.bar_dram = nc.dram_tensor("bar_scratch", [2, 16], F32, kind="Internal").ap()
        bd = self.bar_dram
        bsrc = self.bar_src
        b = Op("sp", lambda: nc.sync.dma_start(out=bd[0:1, :], in_=bsrc), deps, True)
        for d in deps:
            d.needs_inc = True
        b.needs_inc = True
        self.ops.append(b)
        self.barrier_op = b
        self.since_barrier = []
        return b

    def emit(self, final_ops):
        nc = self.nc
        final_ops = list(final_ops) + ([self.barrier_op] if self.barrier_op is not None else [])
        for o in final_ops:
            o.needs_inc = True
        engs = {"pe": nc.tensor, "act": nc.scalar, "dve": nc.vector, "pool": nc.gpsimd, "sp": nc.sync}
        with contextlib.ExitStack() as st:
            csem = {e: st.enter_context(nc.semaphore("c_" + e)) for e in engs}
            dsem = {e: [st.enter_context(nc.semaphore("d_%s_%d" % (e, i))) for i in range(self.n_dma_sems)]
                    for e in ("sp", "act", "pool")}
            ccount = {e: 0 for e in engs}
            dcount = {e: [0] * self.n_dma_sems for e in dsem}
            drr = {e: 0 for e in dsem}
            waited = {e: {} for e in engs}
            n_wait = 0
            for o in self.ops:
                h = engs[o.eng]
                w = waited[o.eng]
                for d in o.deps:
                    key = d.sem
                    if w.get(id(key), 0) < d.semval:
                        h.wait_ge(key, d.semval)
                        w[id(key)] = d.semval
                        n_wait += 1
                if o.is_dma:
                    j = drr[o.eng]
                    drr[o.eng] = (j + 1) % self.n_dma_sems
                    s = dsem[o.eng][j]
                    prev = dcount[o.eng][j]
                    if prev > 0 and w.get(id(s), 0) < prev:
                        h.wait_ge(s, prev)
                        w[id(s)] = prev
                        n_wait += 1
                    ins = o.fn()
                    dcount[o.eng][j] = prev + 16
                    ins.then_inc(s, 16)
                    o.sem = s
                    o.semval = prev + 16
                else:
                    ins = o.fn()
                    if o.needs_inc:
                        ccount[o.eng] += 1
                        ins.then_inc(csem[o.eng], 1)
                        o.sem = csem[o.eng]
                        o.semval = ccount[o.eng]
            for o in final_ops:
                nc.sync.wait_ge(o.sem, o.semval)
            self.stats = dict(n_ops=len(self.ops), n_wait=n_wait, ccount=dict(ccount),
                              dcount={e: max(v) for e, v in dcount.items()})


class K:
    def __init__(self, stop_after=None, dbg=()):
        self.stop_after = stop_after
        self.stop = set(stop_after or ())
        self.dbg = set(dbg)
        self.nc = bass.Bass("TRN2", target_bir_lowering=False)
        self.P = Prog(self.nc)
        self.din = {}
        self.scr = {}
        self.rr = 0

    def inp(self, name, shape, dt=F32):
        self.din[name] = self.nc.dram_tensor(name, list(shape), dt, kind="ExternalInput").ap()
        return self.din[name]

    def scratch(self, name, shape, dt=F32):
        kind = "ExternalOutput" if name in self.dbg else "Internal"
        self.scr[name] = self.nc.dram_tensor(name, list(shape), dt, kind=kind).ap()
        return self.scr[name]

    def mm(self, out, lhsT, rhs, start, stop, reads, writes, **kw):
        nc = self.nc
        return self.P.op("pe", lambda: nc.tensor.matmul(out, lhsT=lhsT, rhs=rhs, start=start, stop=stop, **kw),
                         reads, writes)

    def tr(self, out, in_, ident, reads, writes):
        nc = self.nc
        return self.P.op("pe", lambda: nc.tensor.transpose(out, in_, ident), reads, writes)

    def act(self, out, in_, func, reads, writes, **kw):
        nc = self.nc
        return self.P.op("act", lambda: nc.scalar.activation(out=out, in_=in_, func=func, **kw), reads, writes)

    def v(self, eng, method, reads, writes, *a, **kw):
        nc = self.nc
        h = nc.vector if eng == "dve" else nc.gpsimd
        return self.P.op(eng, lambda: getattr(h, method)(*a, **kw), reads, writes)

    def load(self, out, in_, reads, writes, q="sp"):
        nc = self.nc
        h = {"sp": nc.sync, "act": nc.scalar, "pool": nc.gpsimd}[q]
        return self.P.dma(q, lambda: h.dma_start(out=out, in_=in_), reads, writes)


def build(stop_after=None, dbg=(), skip=()):
    k = K(stop_after, dbg)
    skip = set(skip)
    nc, P = k.nc, k.P

    x_in = k.inp("x", [NB, LL, D])
    ctx_in = k.inp("ctx", [NB, LC, D])
    cm_in = k.inp("cm", [128, KT, 3])
    ident_in = k.inp("ident", [128, 128])
    ada_w = [k.inp("ada_w_%d" % l, [D, 6 * D]) for l in range(2)]
    ada_b = [k.inp("ada_b_%d_fm" % l, [128, 48]) for l in range(2)]
    nmix = [k.inp("norm_mix_%d_fm" % l, [128, KT]) for l in range(2)]
    nffn = [k.inp("norm_ffn_%d_fm" % l, [128, KT]) for l in range(2)]
    fnorm_in = k.inp("final_norm", [1, D])
    out = nc.dram_tensor("out", [NB, LL, D], F32, kind="ExternalOutput").ap()
    P.bar_src = ident_in[0:1, 0:16]

    xT = k.scratch("xT", [NB, 128, KT, T])
    t_xT = [[Tok("xT%d_%d" % (b, c)) for c in range(len(CHUNKS))] for b in range(NB)]

    finals = []
    with contextlib.ExitStack() as gst:
        sbn = [0]

        def sb(name, shape, dt=F32, st=gst):
            sbn[0] += 1
            return st.enter_context(nc.sbuf_tensor("%s_%d" % (name, sbn[0]), list(shape), dt))

        ident = sb("ident_sb", [128, 128])
        identb = sb("identb_sb", [128, 128], BF16)
        onesb = sb("onesb", [128, 128], BF16)
        onesf = sb("onesf", [128, 128])
        mods = [sb("mods%d" % l, [128, 48, 3]) for l in range(2)]
        gs1 = [sb("gs1_%d" % l, [128, KT, 3]) for l in range(2)]
        gs2 = [sb("gs2_%d" % l, [128, KT, 3]) for l in range(2)]
        t_const = Tok("const")
        t_mods = [Tok("mods0"), Tok("mods1")]
        ps = [gst.enter_context(nc.psum_tensor("ps%d" % i, [128, 512], F32)) for i in range(8)]
        t_ps = [Tok("ps%d" % i, excl=True) for i in range(8)]

        k.load(ident[:], ident_in[:, :], [], [t_const])
        k.v("dve", "tensor_copy", [t_const], [t_const], out=identb[:], in_=ident[:])
        k.v("pool", "memset", [], [t_const], onesb[:], 1.0)
        k.v("pool", "memset", [], [t_const], onesf[:], 1.0)

        st01 = contextlib.ExitStack()
        st01.__enter__()
        with contextlib.nullcontext(st01) as st:
            xin = [sb("ph0_xin%d" % i, [128, D], st=st) for i in range(2)]
            t_xin = [Tok(), Tok()]
            stg = [sb("ph0_stg%d" % i, [128, KT, 512], st=st) for i in range(2)]
            t_stg = [Tok(), Tok()]
            n = 0
            for b in range(NB):
                for ci, (c0, cw) in enumerate(CHUNKS):
                    sg = ci % 2
                    for tt in range(cw // 128):
                        t0 = c0 + tt * 128
                        src = ctx_in[b, t0:t0 + 128, :] if t0 < LC else x_in[b, t0 - LC:t0 - LC + 128, :]
                        xi = n % 2
                        k.load(xin[xi][:], src, [], [t_xin[xi]], q="sp" if n % 2 == 0 else "act")
                        for half in range(2):
                            pb = (2 * n + half) % 8
                            for q4 in range(4):
                                kt = half * 4 + q4
                                k.tr(ps[pb][:, q4 * 128:(q4 + 1) * 128], xin[xi][:, kt * 128:(kt + 1) * 128],
                                     ident[:], [t_xin[xi], t_const], [t_ps[pb]])
                            dst = stg[sg][:, half * 4:half * 4 + 4, tt * 128:(tt + 1) * 128]
                            srcp = ps[pb][:].rearrange("p (a b) -> p a b", a=4)
                            if half == 0:
                                k.v("dve", "tensor_copy", [t_ps[pb]], [t_stg[sg]], out=dst, in_=srcp)
                            else:
                                k.act(dst, srcp, AF.Copy, [t_ps[pb]], [t_stg[sg]])
                        n += 1
                    k.load(xT[b, :, :, c0:c0 + cw], stg[sg][:, :, 0:cw], [t_stg[sg]], [t_xT[b][ci]], q="sp")

        with contextlib.nullcontext(st01) as st:
          if 'ph1' not in skip:
              cm = sb("ph1_cm", [128, KT, 3], st=st)
              scb = sb("ph1_scb", [128, KT, 3], BF16, st=st)
              abt = sb("ph1_ab", [128, 48], st=st)
              nm = sb("ph1_nm", [128, KT], st=st)
              nf = sb("ph1_nf", [128, KT], st=st)
              wt = [sb("ph1_w%d" % i, [128, KT, 1024], BF16, st=st) for i in range(2)]
              t_w = [Tok(), Tok()]
              t_cm = Tok()
              t_ab = Tok()
              k.load(cm[:], cm_in[:, :, :], [], [t_cm])
              k.act(scb[:], cm[:], AF.Silu, [t_cm], [t_cm])
              n = 0
              for l in range(2):
                  k.load(abt[:], ada_b[l][:, :], [], [t_ab])
                  k.load(nm[:], nmix[l][:, :], [], [t_ab])
                  k.load(nf[:], nffn[l][:, :], [], [t_ab])
                  pm = ps[l]
                  for cc in range(6):
                      wi = n % 2
                      n += 1
                      k.load(wt[wi][:], ada_w[l][:, cc * 1024:(cc + 1) * 1024].rearrange("(kt p) c -> p kt c", p=128),
                             [], [t_w[wi]], q="pool")
                      for jj in range(8):
                          j = cc * 8 + jj
                          for kt in range(KT):
                              k.mm(pm[:, j * 3:(j + 1) * 3], wt[wi][:, kt, jj * 128:(jj + 1) * 128], scb[:, kt, :],
                                   kt == 0, kt == KT - 1, [t_w[wi], t_cm], [t_ps[l]])
                  k.v("dve", "tensor_tensor", [t_ps[l], t_ab], [t_mods[l]], out=mods[l][:],
                      in0=pm[:, 0:144].rearrange("p (j v) -> p j v", v=3),
                      in1=abt[:].unsqueeze(2).to_broadcast([128, 48, 3]), op=ALU.add)
                  for (g, nn_, off) in ((gs1[l], nm, 8), (gs2[l], nf, 32)):
                      k.v("dve", "scalar_tensor_tensor", [t_mods[l], t_ab], [t_mods[l]], out=g[:],
                          in0=mods[l][:, off:off + 8, :], scalar=1.0,
                          in1=nn_[:].unsqueeze(2).to_broadcast([128, KT, 3]), op0=ALU.add, op1=ALU.mult)
              P.barrier()

        st01.__exit__(None, None, None)
        if "mods" in k.dbg and "ph1" not in skip:
            md = nc.dram_tensor("mods", [2, 128, 48, 3], F32, kind="ExternalOutput").ap()
            for l in range(2):
                finals.append(k.load(md[l], mods[l][:], [t_mods[l]], []))

        cur = {"x": xT, "t": t_xT}
        xT2 = k.scratch("xT2", [NB, 128, KT, T])
        t_xT2 = [[Tok() for c in range(len(CHUNKS))] for b in range(NB)]
        bankc = [0]

        def nb():
            bankc[0] = (bankc[0] + 1) % 8
            return bankc[0]

        def chunk_of(t0):
            return 0 if t0 < LC else 1 + (t0 - LC) // 512

        def toks_for(tl, b, a0, a1):
            return [tl[b][c] for c in range(chunk_of(a0), chunk_of(a1 - 1) + 1)]

        def normmod(st_bufs, xw, t_xw, W, gs, sh_off, l, v, hT, t_hT):
            sq, tmp, rs, t_tmp = st_bufs
            k.act(sq[:, :, 0:W], xw[:, :, 0:W], AF.Square, [t_xw], [t_tmp])
            pb = nb()
            for kt in range(KT):
                k.mm(ps[pb][:, 0:W], onesb[:], sq[:, kt, 0:W], kt == 0, kt == KT - 1, [t_tmp, t_const], [t_ps[pb]])
            k.act(rs[:, 0:W], ps[pb][:, 0:W], AF.Sqrt, [t_ps[pb]], [t_tmp], scale=1.0 / D, bias=EPS)
            k.v("dve", "reciprocal", [t_tmp], [t_tmp], out=rs[:, 0:W], in_=rs[:, 0:W])
            k.v("dve", "tensor_tensor", [t_xw, t_tmp], [t_tmp], out=tmp[:, :, 0:W], in0=xw[:, :, 0:W],
                in1=rs[:, 0:W].unsqueeze(1).to_broadcast([128, KT, W]), op=ALU.mult)
            for kt in range(KT):
                k.act(hT[:, kt, 0:W], tmp[:, kt, 0:W], AF.Identity, [t_tmp, t_mods[l]], [t_hT],
                      scale=gs[:, kt, v:v + 1], bias=mods[l][:, sh_off + kt, v:v + 1])

        def seg_windows(L, nw):
            res = []
            step = (L + nw - 1) // nw
            o0 = 0
            while o0 < L:
                o1 = min(o0 + step, L)
                res.append((max(o0 - 1, 0), min(o1 + 1, L), o0, o1))
                o0 = o1
            return res

        ffn_pref = {}

        def ffn_weights(l, st, issue=True):
            wu_in, wd_in = din["ffn_up_%d" % l], din["ffn_down_%d" % l]
            cw_in, cb_in = din["ffn_conv_w_%d_fm" % l], din["ffn_conv_b_%d_fm" % l]
            wup = sb("f_wup", [128, KT, 2 * DFF], BF16, st=st)
            wdn = sb("f_wdn", [128, FT, D], BF16, st=st)
            cwf = sb("f_cw", [128, FT, 3], st=st)
            cbf = sb("f_cb", [128, FT], st=st)
            t_w = {"c": Tok(), "d": Tok(), "g": [Tok() for _ in range(11)]}

            def do_loads():
                k.load(cwf[:], cw_in[:, :, :], [], [t_w["c"]])
                k.load(cbf[:], cb_in[:, :], [], [t_w["c"]])
                for cg in range(11):
                    for off in (0, DFF):
                        c0_ = off + cg * 256
                        k.load(wup[:, :, c0_:c0_ + 256], wu_in[:, c0_:c0_ + 256].rearrange("(kt p) c -> p kt c", p=128), [], [t_w["g"][cg]], q="pool")
                for j in range(FT):
                    k.load(wdn[:, j, :], wd_in[j * 128:(j + 1) * 128, :], [], [t_w["d"]], q="pool")
            if issue:
                do_loads()
            return (wup, wdn, cwf, cbf, t_w), do_loads

        def ffn_prefetch(l):
            stw = contextlib.ExitStack()
            stw.__enter__()
            tiles, do_loads = ffn_weights(l, stw, issue=False)
            ffn_pref[l] = (stw, tiles)
            return do_loads

        def ffn_phase(l, xS, tS, xD, tD, wu_in, wd_in, cw_in, cb_in):
            with contextlib.ExitStack() as st:
                if l in ffn_pref:
                    stw, (wup, wdn, cwf, cbf, t_w) = ffn_pref.pop(l)
                    st.enter_context(stw)
                else:
                    (wup, wdn, cwf, cbf, t_w), _ = ffn_weights(l, st)
                WM = 260
                xws = [sb("f_xw%d" % i, [128, KT, WM], st=st) for i in range(2)]; t_xws = [Tok(), Tok()]
                sq = sb("f_sq", [128, KT, WM], BF16, st=st)
                tmp = sb("f_tmp", [128, KT, WM], st=st)
                rs = sb("f_rs", [128, WM], st=st); t_tmp = Tok()
                hTs = [sb("f_hT%d" % i, [128, KT, WM], BF16, st=st) for i in range(2)]; t_hTs = [Tok(), Tok()]
                aT = sb("f_aT", [128, FT, WM], BF16, st=st); t_aT = Tok()
                gc = [sb("f_gc%d" % i, [128, WM], st=st) for i in range(2)]
                sg = [sb("f_sg%d" % i, [128, WM], st=st) for i in range(2)]
                t_gc = [Tok(), Tok()]
                wins = []
                for b in range(NB):
                    segs = ([(0, LC, 2, 1)] if l == 0 else []) + [(LC, LL, b, 8)]
                    for (base, L, v, nw) in segs:
                        for (i0, i1, o0, o1) in seg_windows(L, nw):
                            wins.append((b, base, v, i0, i1, o0, o1))

                def prep(wi):
                    b, base, v, i0, i1, o0, o1 = wins[wi]
                    W = i1 - i0
                    xw, t_xw, hT, t_hT = xws[wi % 2], t_xws[wi % 2], hTs[wi % 2], t_hTs[wi % 2]
                    k.load(xw[:, :, 0:W], xS[b, :, :, base + i0:base + i1], toks_for(tS, b, base + i0, base + i1), [t_xw])
                    normmod((sq, tmp, rs, t_tmp), xw, t_xw, W, gs2[l], 24, l, v, hT, t_hT)

                n = 0
                prep(0)
                for wi, (b, base, v, i0, i1, o0, o1) in enumerate(wins):
                    W = i1 - i0
                    xw, t_xw, hT, t_hT = xws[wi % 2], t_xws[wi % 2], hTs[wi % 2], t_hTs[wi % 2]
                    for j in range(FT):
                        if j == 8 and wi + 1 < len(wins):
                            prep(wi + 1)
                        pg, pu = nb(), nb()
                        for kt in range(KT):
                            k.mm(ps[pg][:, 0:W], wup[:, kt, j * 128:(j + 1) * 128], hT[:, kt, 0:W], kt == 0, kt == KT - 1, [t_w["g"][j // 2], t_hT], [t_ps[pg]])
                        for kt in range(KT):
                            k.mm(ps[pu][:, 0:W], wup[:, kt, DFF + j * 128:DFF + (j + 1) * 128], hT[:, kt, 0:W], kt == 0, kt == KT - 1, [t_w["g"][j // 2], t_hT], [t_ps[pu]])
                        g2i = n % 2; n += 1
                        k.act(gc[g2i][:, 0:W], ps[pg][:, 0:W], AF.Identity, [t_ps[pg], t_w["c"]], [t_gc[g2i]],
                              scale=cwf[:, j, 1:2], bias=cbf[:, j:j + 1])
                        k.v("dve", "scalar_tensor_tensor", [t_ps[pg], t_w["c"]], [t_gc[g2i]], out=gc[g2i][:, 1:W], in0=ps[pg][:, 0:W - 1],
                            scalar=cwf[:, j, 0:1], in1=gc[g2i][:, 1:W], op0=ALU.mult, op1=ALU.add)
                        k.v("dve", "scalar_tensor_tensor", [t_ps[pg], t_w["c"]], [t_gc[g2i]], out=gc[g2i][:, 0:W - 1], in0=ps[pg][:, 1:W],
                            scalar=cwf[:, j, 2:3], in1=gc[g2i][:, 0:W - 1], op0=ALU.mult, op1=ALU.add)
                        k.act(sg[g2i][:, 0:W], gc[g2i][:, 0:W], AF.Silu, [t_gc[g2i]], [t_gc[g2i]])
                        k.v("dve", "tensor_tensor", [t_ps[pu], t_gc[g2i]], [t_aT], out=aT[:, j, 0:W], in0=ps[pu][:, 0:W],
                            in1=sg[g2i][:, 0:W], op=ALU.mult)
                    a0, a1 = o0 - i0, o1 - i0
                    for kt in range(KT):
                        pd = nb()
                        for j in range(FT):
                            k.mm(ps[pd][:, 0:a1 - a0], wdn[:, j, kt * 128:(kt + 1) * 128], aT[:, j, a0:a1], j == 0, j == FT - 1, [t_w["d"], t_aT], [t_ps[pd]])
                        k.v("dve", "scalar_tensor_tensor", [t_ps[pd], t_mods[l]], [t_xw], out=xw[:, kt, a0:a1], in0=ps[pd][:, 0:a1 - a0],
                            scalar=mods[l][:, 40 + kt, v:v + 1], in1=xw[:, kt, a0:a1], op0=ALU.mult, op1=ALU.add)
                    k.load(xD[b, :, :, base + o0:base + o1], xw[:, :, a0:a1], [t_xw], [Tok()])
                P.barrier()

        def mla_phase(l, xS, tS, xD, tD):
            knD = k.scratch("knD", [NB, 128, 8, T], BF16)
            vD = k.scratch("vD", [NB, T, D], BF16)
            krD = k.scratch("krD", [NB, 64, T], BF16)
            qD = k.scratch("qD", [NB, 8, 128, LL], BF16)
            qrD = k.scratch("qrD", [NB, 8, 64, LL], BF16)
            t_kv = [Tok() for b in range(NB)]
            t_q = [Tok() for b in range(NB)]
            SC = 192.0 ** -0.5
            with contextlib.ExitStack() as st:
                wdn = sb("m_wdn", [128, KT, 768], BF16, st=st)
                wuq = sb("m_wuq", [128, 3, 2048], BF16, st=st)
                wukv = sb("m_wukv", [128, 2, 2048], BF16, st=st)
                qng = sb("m_qng", [128, 3], st=st)
                kvg = sb("m_kvg", [128, 2], st=st)
                rc = sb("m_rc", [64, LL], st=st)
                rsn = sb("m_rs", [64, LL], st=st)
                t_w = Tok()
                k.load(wdn[:], din["mla_w_down_ext"].rearrange("(kt p) c -> p kt c", p=128), [], [t_w], q="pool")
                k.load(wuq[:], din["mla_w_uq_ext"].rearrange("(kt p) c -> p kt c", p=128), [], [t_w], q="pool")
                k.load(wukv[:], din["mla_w_ukv"].rearrange("(kt p) c -> p kt c", p=128), [], [t_w], q="pool")
                k.load(qng[:], din["mla_q_norm_fm"][:, :], [], [t_w])
                k.load(kvg[:], din["mla_kv_norm_fm"][:, :], [], [t_w])
                k.load(rc[:], din["ropeC"][:, :], [], [t_w])
                k.load(rsn[:], din["ropeS"][:, :], [], [t_w])
                xws = [sb("m_xw%d" % i, [128, KT, 512], st=st) for i in range(2)]; t_xws = [Tok(), Tok()]
                sq = sb("m_sq", [128, KT, 512], BF16, st=st)
                tmp = sb("m_tmp", [128, KT, 512], st=st)
                rs = sb("m_rsd", [128, 512], st=st); t_tmp = Tok()
                hTs = [sb("m_hT%d" % i, [128, KT, 512], BF16, st=st) for i in range(2)]; t_hTs = [Tok(), Tok()]
                lat_a = sb("m_lata", [128, 3, 512], st=st); t_la = Tok()
                lat_n = sb("m_latn", [128, 3, 512], BF16, st=st); t_ln = Tok()
                kv_a = sb("m_kva", [128, 2, 512], st=st); t_ka = Tok()
                kv_n = sb("m_kvn", [128, 2, 512], BF16, st=st); t_kn = Tok()
                sq2 = sb("m_sq2", [128, 3, 512], BF16, st=st)
                rs2 = sb("m_rs2", [128, 512], st=st); t_s2 = Tok()
                r1 = sb("m_r1", [64, 512], st=st)
                r2 = sb("m_r2", [64, 512], st=st); t_r = Tok()
                krr = sb("m_krr", [64, 512], BF16, st=st); t_krr = Tok()
                knT = sb("m_knT", [128, 8, 512], BF16, st=st); t_knT = Tok()
                vt = [sb("m_vt%d" % i, [128, D], BF16, st=st) for i in range(2)]; t_vt = [Tok(), Tok()]
                qo = [sb("m_qo%d" % i, [128, 512], BF16, st=st) for i in range(2)]; t_qo = [Tok(), Tok()]
                qro = [sb("m_qro%d" % i, [64, 512], BF16, st=st) for i in range(2)]; t_qro = [Tok(), Tok()]

                def lownorm(src, t_src, nt, dst, t_dst, g, W):
                    k.act(sq2[:, 0:nt, 0:W], src[:, 0:nt, 0:W], AF.Square, [t_src], [t_s2])
                    pb = nb()
                    for i in range(nt):
                        k.mm(ps[pb][:, 0:W], onesb[:], sq2[:, i, 0:W], i == 0, i == nt - 1, [t_s2, t_const], [t_ps[pb]])
                    k.act(rs2[:, 0:W], ps[pb][:, 0:W], AF.Sqrt, [t_ps[pb]], [t_s2], scale=1.0 / (nt * 128), bias=EPS)
                    k.v("dve", "reciprocal", [t_s2], [t_s2], out=rs2[:, 0:W], in_=rs2[:, 0:W])
                    for i in range(nt):
                        k.v("dve", "scalar_tensor_tensor", [t_src, t_s2, t_w], [t_dst], out=dst[:, i, 0:W], in0=src[:, i, 0:W],
                            scalar=g[:, i:i + 1], in1=rs2[:, 0:W], op0=ALU.mult, op1=ALU.mult)

                def rope(pa, pb2, p0, W, dst, t_dst):
                    k.v("dve", "tensor_tensor", [t_ps[pa], t_w], [t_r], out=r1[:, 0:W], in0=ps[pa][0:64, 0:W], in1=rc[:, p0:p0 + W], op=ALU.mult)
                    k.v("dve", "tensor_tensor", [t_ps[pb2], t_w], [t_r], out=r2[:, 0:W], in0=ps[pb2][0:64, 0:W], in1=rsn[:, p0:p0 + W], op=ALU.mult)
                    k.v("dve", "tensor_tensor", [t_r], [t_dst], out=dst[:, 0:W], in0=r1[:, 0:W], in1=r2[:, 0:W], op=ALU.add)

                n = 0
                m1items = [(b, ci, c0, W) for b in range(NB) for ci, (c0, W) in enumerate(CHUNKS)]

                def m1prep(i):
                    b_, ci_, c0_, W_ = m1items[i]
                    v_ = b_ if c0_ >= LC else 2
                    k.load(xws[i % 2][:, :, 0:W_], xS[b_, :, :, c0_:c0_ + W_], [tS[b_][ci_]], [t_xws[i % 2]])
                    normmod((sq, tmp, rs, t_tmp), xws[i % 2], t_xws[i % 2], W_, gs1[l], 0, l, v_, hTs[i % 2], t_hTs[i % 2])

                m1prep(0)
                for mi, (b, ci, c0, W) in enumerate(m1items):
                    if True:
                        is_lat = c0 >= LC
                        v = b if is_lat else 2
                        hT, t_hT = hTs[mi % 2], t_hTs[mi % 2]
                        for i in range(2):
                            pb = nb()
                            for kt in range(KT):
                                k.mm(ps[pb][:, 0:W], wdn[:, kt, 384 + i * 128:384 + (i + 1) * 128], hT[:, kt, 0:W], kt == 0, kt == KT - 1, [t_w, t_hT], [t_ps[pb]])
                            k.act(kv_a[:, i, 0:W], ps[pb][:, 0:W], AF.Copy, [t_ps[pb]], [t_ka])
                        lownorm(kv_a, t_ka, 2, kv_n, t_kn, kvg, W)
                        pa = nb()
                        for kt in range(KT):
                            k.mm(ps[pa][0:64, 0:W], wdn[:, kt, 640:704], hT[:, kt, 0:W], kt == 0, kt == KT - 1, [t_w, t_hT], [t_ps[pa]])
                        if is_lat:
                            pb2 = nb()
                            for kt in range(KT):
                                k.mm(ps[pb2][0:64, 0:W], wdn[:, kt, 704:768], hT[:, kt, 0:W], kt == 0, kt == KT - 1, [t_w, t_hT], [t_ps[pb2]])
                            rope(pa, pb2, c0 - LC, W, krr, t_krr)
                        else:
                            k.act(krr[:, 0:W], ps[pa][0:64, 0:W], AF.Copy, [t_ps[pa]], [t_krr])
                        k.load(krD[b, :, c0:c0 + W], krr[:, 0:W], [t_krr], [Tok()])
                        if mi + 1 < len(m1items):
                            m1prep(mi + 1)
                        for h in range(8):
                            pb = nb()
                            for i in range(2):
                                k.mm(ps[pb][:, 0:W], wukv[:, i, h * 256:h * 256 + 128], kv_n[:, i, 0:W], i == 0, i == 1, [t_w, t_kn], [t_ps[pb]])
                            if h % 2 == 0:
                                k.act(knT[:, h, 0:W], ps[pb][:, 0:W], AF.Copy, [t_ps[pb]], [t_knT])
                            else:
                                k.v("dve", "tensor_copy", [t_ps[pb]], [t_knT], out=knT[:, h, 0:W], in_=ps[pb][:, 0:W])
                        k.load(knD[b, :, :, c0:c0 + W], knT[:, :, 0:W], [t_knT], [Tok()])
                        wv = wukv[:].rearrange("p i (h c) -> p i h c", c=256)
                        for tt in range(W // 128):
                            vi = n % 2; n += 1
                            for half in range(2):
                                pb = nb()
                                for i in range(2):
                                    k.mm(ps[pb][:].rearrange("p (h c) -> p h c", c=128), kv_n[:, i, tt * 128:(tt + 1) * 128],
                                         wv[:, i, half * 4:half * 4 + 4, 128:256], i == 0, i == 1, [t_w, t_kn], [t_ps[pb]])
                                if half == 0:
                                    k.act(vt[vi][:, 0:512], ps[pb][:], AF.Copy, [t_ps[pb]], [t_vt[vi]])
                                else:
                                    k.v("dve", "tensor_copy", [t_ps[pb]], [t_vt[vi]], out=vt[vi][:, 512:1024], in_=ps[pb][:])
                            k.load(vD[b, c0 + tt * 128:c0 + (tt + 1) * 128, :], vt[vi][:], [t_vt[vi]], [Tok()], q="act")
                        if not is_lat:
                            continue
                        for i in range(3):
                            pb = nb()
                            for kt in range(KT):
                                k.mm(ps[pb][:, 0:W], wdn[:, kt, i * 128:(i + 1) * 128], hT[:, kt, 0:W], kt == 0, kt == KT - 1, [t_w, t_hT], [t_ps[pb]])
                            k.act(lat_a[:, i, 0:W], ps[pb][:, 0:W], AF.Copy, [t_ps[pb]], [t_la])
                        lownorm(lat_a, t_la, 3, lat_n, t_ln, qng, W)
                        for h in range(8):
                            qi = h % 2
                            pb = nb()
                            for i in range(3):
                                k.mm(ps[pb][:, 0:W], wuq[:, i, h * 256:h * 256 + 128], lat_n[:, i, 0:W], i == 0, i == 2, [t_w, t_ln], [t_ps[pb]])
                            k.act(qo[qi][:, 0:W], ps[pb][:, 0:W], AF.Copy, [t_ps[pb]], [t_qo[qi]])
                            k.load(qD[b, h, :, c0 - LC:c0 - LC + W], qo[qi][:, 0:W], [t_qo[qi]], [Tok()])
                            pa, pb2 = nb(), nb()
                            for i in range(3):
                                k.mm(ps[pa][0:64, 0:W], wuq[:, i, h * 256 + 128:h * 256 + 192], lat_n[:, i, 0:W], i == 0, i == 2, [t_w, t_ln], [t_ps[pa]])
                            for i in range(3):
                                k.mm(ps[pb2][0:64, 0:W], wuq[:, i, h * 256 + 192:h * 256 + 256], lat_n[:, i, 0:W], i == 0, i == 2, [t_w, t_ln], [t_ps[pb2]])
                            rope(pa, pb2, c0 - LC, W, qro[qi], t_qro[qi])
                            k.load(qrD[b, h, :, c0 - LC:c0 - LC + W], qro[qi][:, 0:W], [t_qro[qi]], [Tok()])
                P.barrier()
            with contextlib.ExitStack() as st:
                wo = sb("a_wo", [128, 8, D], BF16, st=st); t_w = Tok()
                k.load(wo[:], din["mla_w_o"].rearrange("(h p) c -> p h c", p=128), [], [t_w], q="pool")
                knT = sb("a_knT", [128, 8, T], BF16, st=st)
                vv = sb("a_v", [128, 18, D], BF16, st=st)
                krr = sb("a_krr", [64, T], BF16, st=st); t_kvs = Tok()
                qt = [sb("a_q%d" % i, [128, 512], BF16, st=st) for i in range(2)]
                qrt = [sb("a_qr%d" % i, [64, 512], BF16, st=st) for i in range(2)]; t_qt = [Tok(), Tok()]
                Pm = [sb("a_P%d" % i, [128, 18, 512], BF16, st=st) for i in range(2)]; t_P = [Tok(), Tok()]
                rden = sb("a_rden", [128, 512], st=st); t_rd = Tok()
                accP = [sb("a_acc%d" % i, [128, 512], st=st) for i in range(2)]; t_acc = [Tok(), Tok()]
                attnT = sb("a_attn", [128, 8, 512], BF16, st=st); t_at = Tok()
                xw = sb("a_xw", [128, KT, 512], st=st); t_xw = Tok()
                t_Pj = [[Tok() for j in range(18)] for i in range(2)]
                accB = [sb("a_accB%d" % i, [128, 512], st=st) for i in range(2)]; t_accB = [Tok(), Tok()]
                attn2 = [attnT, sb("a_attn2", [128, 8, 512], BF16, st=st)]; t_at2 = [t_at, Tok()]
                items = [(b, qc, h) for b in range(NB) for qc in range(4) for h in range(8)]

                def stage1(idx):
                    b, qc, h = items[idx]
                    i2 = idx % 2
                    q0 = qc * 512
                    k.load(qt[i2][:], qD[b, h, :, q0:q0 + 512], [], [t_qt[i2]])
                    k.load(qrt[i2][:], qrD[b, h, :, q0:q0 + 512], [], [t_qt[i2]], q="act")
                    for j in range(18):
                        pS = nb()
                        k.mm(ps[pS][:], knT[:, h, j * 128:(j + 1) * 128], qt[i2][:], True, False, [t_kvs, t_qt[i2]], [t_ps[pS]])
                        k.mm(ps[pS][:], krr[:, j * 128:(j + 1) * 128], qrt[i2][:], False, True, [t_kvs, t_qt[i2]], [t_ps[pS]])
                        k.act(Pm[i2][:, j, :], ps[pS][:], AF.Exp, [t_ps[pS]], [t_Pj[i2][j]], scale=SC)
                        eng, acc, t_a = ("pool", accP[i2], t_acc[i2]) if j % 2 == 0 else ("dve", accB[i2], t_accB[i2])
                        if j < 2:
                            k.v(eng, "tensor_copy", [t_Pj[i2][j]], [t_a], out=acc[:], in_=Pm[i2][:, j, :])
                        else:
                            k.v(eng, "tensor_tensor", [t_Pj[i2][j]], [t_a], out=acc[:], in0=acc[:], in1=Pm[i2][:, j, :], op=ALU.add)

                def stage2(idx):
                    b, qc, h = items[idx]
                    i2 = idx % 2
                    a2 = qc % 2
                    pO, pDn = nb(), nb()
                    for j in range(18):
                        k.mm(ps[pO][:], vv[:, j, h * 128:(h + 1) * 128], Pm[i2][:, j, :], j == 0, j == 17, [t_kvs, t_Pj[i2][j]], [t_ps[pO]])
                    k.mm(ps[pDn][:], onesf[:], accP[i2][:], True, False, [t_const, t_acc[i2]], [t_ps[pDn]])
                    k.mm(ps[pDn][:], onesf[:], accB[i2][:], False, True, [t_const, t_accB[i2]], [t_ps[pDn]])
                    k.v("dve", "reciprocal", [t_ps[pDn]], [t_rd], out=rden[:], in_=ps[pDn][:])
                    k.v("dve", "tensor_tensor", [t_ps[pO], t_rd], [t_at2[a2]], out=attn2[a2][:, h, :], in0=ps[pO][:], in1=rden[:], op=ALU.mult)
                    if h == 7:
                        q0 = qc * 512
                        c0 = LC + q0
                        ci = chunk_of(c0)
                        k.load(xw[:], xS[b, :, :, c0:c0 + 512], [tS[b][ci]], [t_xw])
                        for kt in range(KT):
                            pY = nb()
                            for hh in range(8):
                                k.mm(ps[pY][:], wo[:, hh, kt * 128:(kt + 1) * 128], attn2[a2][:, hh, :], hh == 0, hh == 7, [t_w, t_at2[a2]], [t_ps[pY]])
                            k.v("dve", "scalar_tensor_tensor", [t_ps[pY], t_mods[l]], [t_xw], out=xw[:, kt, :], in0=ps[pY][:],
                                scalar=mods[l][:, 16 + kt, b:b + 1], in1=xw[:, kt, :], op0=ALU.mult, op1=ALU.add)
                        k.load(xD[b, :, :, c0:c0 + 512], xw[:], [t_xw], [tD[b][ci]])

                for idx, (b, qc, h) in enumerate(items):
                    if qc == 0 and h == 0:
                        k.load(knT[:], knD[b], [], [t_kvs])
                        k.load(vv[:], vD[b].rearrange("(j p) c -> p j c", p=128), [], [t_kvs], q="act")
                        k.load(krr[:], krD[b], [], [t_kvs])
                        stage1(idx)
                    if idx + 1 < len(items) and items[idx + 1][0] == b:
                        stage1(idx + 1)
                    stage2(idx)
                P.barrier()

        def colof(t):
            return t + 1 if t < LC else t + 2

        def mix0_phase(l, xS, tS, xD, tD):
            hx0 = k.scratch("hx0", [NB, 4, 128, T])
            hg = k.scratch("hg", [NB, 4, 128, T])
            hgT = k.scratch("hgT", [NB, T, 512], BF16)
            mqT = k.scratch("mqT", [NB, 4, 128, T], BF16)
            mkT = k.scratch("mkT", [NB, 4, 128, T], BF16)
            mvT = k.scratch("mvT", [NB, T, 512], BF16)
            mog = k.scratch("mog", [NB, 4, 128, T])
            mgr = k.scratch("mgr", [NB, 4, 4, T])
            mixD = k.scratch("mixD", [NB, 8, 128, T], BF16)
            t_pr = [Tok() for b in range(NB)]
            t_mixh = [Tok() for b in range(NB)]
            t_mixm = [Tok() for b in range(NB)]
            WP = T + 3
            with contextlib.ExitStack() as st:
                win = sb("p_win", [128, KT, 3600], BF16, st=st); t_w = Tok()
                for kt in range(KT):
                    k.load(win[:, kt, :], din["w_in_0"][kt * 128:(kt + 1) * 128, :], [], [t_w], q="pool")
                hcw = sb("p_hcw", [128, 12, 3], st=st); hcb = sb("p_hcb", [128, 12], st=st)
                mcw = sb("p_mcw", [128, 8, 3], st=st); mgb = sb("p_mgb", [4, 4], st=st); mng = sb("p_mng", [128, 4], st=st)
                k.load(hcw[:], din["hy_conv_w_fm"][:, :, :], [], [t_w]); k.load(hcb[:], din["hy_conv_b_fm"][:, :], [], [t_w])
                k.load(mcw[:], din["ml_conv_w_fm"][:, :, :], [], [t_w]); k.load(mgb[:], din["ml_gate_b_fm"][:, :], [], [t_w])
                k.load(mng[:], din["ml_norm_g_fm"][:, :], [], [t_w])
                hT = sb("p_hT", [128, KT, T], BF16, st=st); t_hT = Tok()
                xw = sb("p_xw", [128, KT, 256], st=st); t_xw = Tok()
                sq = sb("p_sq", [128, KT, 256], BF16, st=st)
                tmp = sb("p_tmp", [128, KT, 256], st=st)
                rs = sb("p_rs", [128, 256], st=st); t_tmp = Tok()
                ub = [sb("p_ub%d" % i, [128, WP], st=st) for i in range(3)]; t_ub = [Tok() for i in range(3)]
                cv = [sb("p_cv%d" % i, [128, WP], st=st) for i in range(3)]; t_cv = [Tok() for i in range(3)]
                ob = [sb("p_ob%d" % i, [128, WP], BF16, st=st) for i in range(1)] * 2; t_ob = [Tok()] * 2
                gT = sb("p_gT", [128, 18, 128], BF16, st=st); t_gT = Tok()
                vt = [sb("p_vt%d" % i, [128, 512], BF16, st=st) for i in range(2)]; t_vt = [Tok(), Tok()]
                grow = sb("p_grow", [4, 1, T], st=st); t_grow = Tok()
                for i in range(3):
                    k.v("pool", "memset", [], [t_ub[i]], ub[i][:], 0.0)
                    k.v("pool", "memset", [], [t_cv[i]], cv[i][:], 0.0)
                evn = [0]

                def proj_fm(col0, M, dst_fn, t_dst):
                    for (c0, W) in CHUNKS:
                        pb = nb()
                        for kt in range(KT):
                            k.mm(ps[pb][0:M, 0:W], win[:, kt, col0:col0 + M], hT[:, kt, c0:c0 + W], kt == 0, kt == KT - 1, [t_w, t_hT], [t_ps[pb]])
                        evn[0] += 1
                        if evn[0] % 2:
                            k.act(dst_fn(c0, W), ps[pb][0:M, 0:W], AF.Copy, [t_ps[pb]], [t_dst])
                        else:
                            k.v("dve", "tensor_copy", [t_ps[pb]], [t_dst], out=dst_fn(c0, W), in_=ps[pb][0:M, 0:W])

                def conv3(i, wt, widx, bias):
                    u_, c_ = ub[i], cv[i]
                    if bias is None:
                        k.act(c_[:, 1:WP - 1], u_[:, 1:WP - 1], AF.Identity, [t_ub[i], t_w], [t_cv[i]], scale=wt[:, widx, 1:2])
                    else:
                        k.act(c_[:, 1:WP - 1], u_[:, 1:WP - 1], AF.Identity, [t_ub[i], t_w], [t_cv[i]], scale=wt[:, widx, 1:2], bias=bias)
                    k.v("dve", "scalar_tensor_tensor", [t_ub[i], t_w], [t_cv[i]], out=c_[:, 1:WP - 1], in0=u_[:, 0:WP - 2],
                        scalar=wt[:, widx, 0:1], in1=c_[:, 1:WP - 1], op0=ALU.mult, op1=ALU.add)
                    k.v("dve", "scalar_tensor_tensor", [t_ub[i], t_w], [t_cv[i]], out=c_[:, 1:WP - 1], in0=u_[:, 2:WP],
                        scalar=wt[:, widx, 2:3], in1=c_[:, 1:WP - 1], op0=ALU.mult, op1=ALU.add)

                def store_fm(dst, src, t_src, t_dst, q="sp"):
                    k.load(dst[:, 0:LC], src[:, 1:1 + LC], [t_src], [t_dst], q=q)
                    k.load(dst[:, LC:T], src[:, LC + 2:LC + 2 + LL], [t_src], [t_dst], q=q)

                n = 0
                for b in range(NB):
                    for c0 in range(0, T, 256):
                        W = 256
                        v = b if c0 >= LC else 2
                        k.load(xw[:, :, 0:W], xS[b, :, :, c0:c0 + W], [tS[b][chunk_of(c0)]], [t_xw])
                        normmod((sq, tmp, rs, t_tmp), xw, t_xw, W, gs1[l], 0, l, v, hT[:, :, c0:c0 + W], t_hT)
                    for j in range(4):
                        for part in range(3):
                            m = part * 4 + j
                            proj_fm(m * 128, 128, lambda c0, W, u_=ub[part]: u_[:, colof(c0):colof(c0) + W], t_ub[part])
                            conv3(part, hcw, m, hcb[:, m:m + 1])
                        store_fm(hx0[b, j], cv[0], t_cv[0], Tok())
                        k.v("dve", "tensor_tensor", [t_cv[1], t_cv[2]], [t_cv[2]], out=cv[2][:], in0=cv[2][:], in1=cv[1][:], op=ALU.mult)
                        store_fm(hg[b, j], cv[2], t_cv[2], Tok(), q="act")
                        for g0 in range(0, 18, 4):
                            ng = min(4, 18 - g0)
                            pb = nb()
                            for i in range(ng):
                                cc = colof((g0 + i) * 128)
                                k.tr(ps[pb][:, i * 128:(i + 1) * 128], cv[2][:, cc:cc + 128], ident[:], [t_cv[2], t_const], [t_ps[pb]])
                            k.act(gT[:, g0:g0 + ng, :], ps[pb][:, 0:ng * 128].rearrange("p (a c) -> p a c", c=128),
                                  AF.Copy, [t_ps[pb]], [t_gT])
                        k.load(hgT[b, :, j * 128:(j + 1) * 128].rearrange("(tt p) c -> p tt c", p=128), gT[:], [t_gT], [Tok()])
                    for h in range(4):
                        for (qk, mt, dstD) in ((0, 12 + h, mqT), (1, 16 + h, mkT)):
                            proj_fm(mt * 128, 128, lambda c0, W, u_=ub[qk]: u_[:, colof(c0):colof(c0) + W], t_ub[qk])
                            conv3(qk, mcw, qk * 4 + h, None)
                            oi = n % 2; n += 1
                            if qk == 0:
                                k.act(ob[oi][:, 1:WP - 1], cv[qk][:, 1:WP - 1], AF.Silu, [t_cv[qk]], [t_ob[oi]])
                            else:
                                k.act(cv[qk][:, 1:WP - 1], cv[qk][:, 1:WP - 1], AF.Silu, [t_cv[qk]], [t_cv[qk]])
                                k.v("dve", "tensor_scalar", [t_cv[qk]], [t_ob[oi]], out=ob[oi][:, 1:WP - 1], in0=cv[qk][:, 1:WP - 1],
                                    scalar1=128.0 ** -0.5, scalar2=None, op0=ALU.mult)
                            store_fm(dstD[b, h], ob[oi], t_ob[oi], Tok())
                        proj_fm((24 + h) * 128, 128, lambda c0, W: ub[2][:, c0:c0 + W], t_ub[2])
                        k.act(cv[2][:, 0:T], ub[2][:, 0:T], AF.Sigmoid, [t_ub[2]], [t_cv[2]])
                        k.v("dve", "tensor_scalar", [t_cv[2], t_w], [t_cv[2]], out=cv[2][:, 0:T], in0=cv[2][:, 0:T],
                            scalar1=mng[:, h:h + 1], scalar2=None, op0=ALU.mult)
                        k.load(mog[b, h], cv[2][:, 0:T], [t_cv[2]], [Tok()])
                    k.v("pool", "memset", [t_ub[2]], [t_ub[2]], ub[2][:], 0.0)
                    k.v("pool", "memset", [t_cv[2]], [t_cv[2]], cv[2][:], 0.0)
                    for tt in range(18):
                        vi = tt % 2
                        pb = nb()
                        for kt in range(KT):
                            k.mm(ps[pb][:], hT[:, kt, tt * 128:(tt + 1) * 128], win[:, kt, 2560:3072], kt == 0, kt == KT - 1, [t_w, t_hT], [t_ps[pb]])
                        k.act(vt[vi][:], ps[pb][:], AF.Copy, [t_ps[pb]], [t_vt[vi]])
                        k.load(mvT[b, tt * 128:(tt + 1) * 128, :], vt[vi][:], [t_vt[vi]], [Tok()], q="act")
                    for kind in range(4):
                        for (c0, W) in CHUNKS:
                            pb = nb()
                            for kt in range(KT):
                                k.mm(ps[pb][0:4, 0:W], win[:, kt, 3584 + kind * 4:3588 + kind * 4], hT[:, kt, c0:c0 + W], kt == 0, kt == KT - 1, [t_w, t_hT], [t_ps[pb]])
                            k.act(grow[:, 0, c0:c0 + W], ps[pb][0:4, 0:W], AF.Identity, [t_ps[pb], t_w], [t_grow], bias=mgb[:, kind:kind + 1])
                        if kind % 2 == 1:
                            k.act(grow[:, 0, :], grow[:, 0, :], AF.Exp, [t_grow], [t_grow], scale=-1.0)
                            k.act(grow[:, 0, :], grow[:, 0, :], AF.Ln, [t_grow], [t_grow], bias=1.0)
                            k.v("dve", "tensor_scalar", [t_grow], [t_grow], out=grow[:, 0, :], in0=grow[:, 0, :], scalar1=-1.0, scalar2=None, op0=ALU.mult)
                        k.load(mgr[b, kind], grow[:, 0, :], [t_grow], [Tok()])
                P.barrier()
            if "mixA" in k.stop:
                return

            with contextlib.ExitStack() as st:
                maskF = sb("l_mF", [128, 4, 512], st=st); maskB = sb("l_mB", [128, 4, 512], st=st)
                sel = sb("l_sel", [4, 4, 128], st=st); t_c = Tok()
                k.load(maskF[:], din["maskF"][:, :, :], [], [t_c]); k.load(maskB[:], din["maskB"][:, :, :], [], [t_c], q="act")
                k.load(sel[:], din["sel4"][:, :, :], [], [t_c])
                arow = [sb("l_arow%d" % d_, [4, T], st=st) for d_ in range(2)]
                t_rows = Tok()
                colsb = sb("l_cols", [128, 18, 24], st=st); emn = sb("l_emn", [128, 18, 8], st=st); t_cols = Tok()
                n = 0
                for b in range(NB):
                    with contextlib.ExitStack() as st1:
                        gr = sb("l_gr", [4, 4, T], st=st1); t_gr = Tok()
                        zr = sb("l_zr", [4, T], st=st1)
                        k.v("pool", "memset", [], [t_c], zr[:], 0.0)
                        rows = [[arow[d_]] + [sb("l_row%d%d" % (d_, q_), [4, T], st=st1) for q_ in range(1, 3)] for d_ in range(2)]
                        mrow = [sb("l_m%d" % d_, [4, T], st=st1) for d_ in range(2)]
                        brow = [sb("l_b%d" % d_, [4, T], st=st1) for d_ in range(2)]
                        k.load(gr[:], mgr[b].rearrange("k h t -> h k t"), [], [t_gr])
                        S = "tensor_tensor_scan"
                        k.v("dve", S, [t_gr], [t_rows], out=mrow[0][:, :], data0=gr[:, 1, :], data1=gr[:, 0, :], initial=0.0, op0=ALU.add, op1=ALU.max)
                        k.v("dve", S, [t_gr, t_c], [t_rows], out=brow[0][:, :], data0=gr[:, 1, :], data1=zr[:, :], initial=0.0, op0=ALU.add, op1=ALU.add)
                        rv = lambda ap_: ap_[:, ::-1]
                        k.v("dve", S, [t_gr], [t_rows], out=rv(mrow[1][:, 0:LC]), data0=rv(gr[:, 3, 0:LC]), data1=rv(gr[:, 2, 0:LC]), initial=0.0, op0=ALU.add, op1=ALU.max)
                        k.v("dve", S, [t_gr, t_c], [t_rows], out=rv(brow[1][:, 0:LC]), data0=rv(gr[:, 3, 0:LC]), data1=rv(zr[:, 0:LC]), initial=0.0, op0=ALU.add, op1=ALU.add)
                        k.v("dve", S, [t_gr, t_rows], [t_rows], out=rv(mrow[1][:, LC:T]), data0=rv(gr[:, 3, LC:T]), data1=rv(gr[:, 2, LC:T]), initial=mrow[1][:, 0:1], op0=ALU.add, op1=ALU.max)
                        k.v("dve", S, [t_gr, t_c, t_rows], [t_rows], out=rv(brow[1][:, LC:T]), data0=rv(gr[:, 3, LC:T]), data1=rv(zr[:, LC:T]), initial=brow[1][:, 0:1], op0=ALU.add, op1=ALU.add)
                        for d_ in range(2):
                            k.v("dve", "tensor_tensor", [t_rows], [t_rows], out=rows[d_][0][:, :], in0=brow[d_][:, :], in1=mrow[d_][:, :], op=ALU.subtract)
                            k.v("dve", "tensor_tensor", [t_rows, t_gr], [t_rows], out=rows[d_][1][:, :], in0=gr[:, 2 * d_, :], in1=brow[d_][:, :], op=ALU.subtract)
                            k.v("dve", "tensor_scalar", [t_rows], [t_rows], out=rows[d_][2][:, :], in0=mrow[d_][:, :], scalar1=-1.0, scalar2=None, op0=ALU.mult)
                        pc = nb()
                        for tt in range(18):
                            for d_ in range(2):
                                for q_ in range(3):
                                    o0 = tt * 24 + d_ * 12 + q_ * 4
                                    k.mm(ps[pc][:, o0:o0 + 4], rows[d_][q_][0:4, tt * 128:(tt + 1) * 128], ident[0:4, 0:4], True, True, [t_rows, t_const], [t_ps[pc]])
                        k.v("dve", "tensor_copy", [t_ps[pc]], [t_cols], out=colsb[:], in_=ps[pc][:, 0:432].rearrange("p (a c) -> p a c", c=24))
                        for d_ in range(2):
                            k.act(emn[:, :, d_ * 4:(d_ + 1) * 4], colsb[:, :, d_ * 12 + 8:d_ * 12 + 12], AF.Exp, [t_cols], [t_cols])
                    P.barrier()
                    with contextlib.ExitStack() as st2:
                        kT = sb("l_kT", [128, 4, T], BF16, st=st2); qT = sb("l_qT", [128, 4, T], BF16, st=st2); t_qk = Tok()
                        vq = sb("l_vq", [128, 18, 4, 130], BF16, st=st2); t_vq = Tok()
                        hsum = sb("l_hsum", [128, 18, 512], st=st2); t_hs = Tok()
                        Pb = [sb("l_P%d" % i, [128, 18, 512], BF16, st=st2) for i in range(2)]; t_P = [Tok(), Tok()]
                        Wt = [sb("l_W%d" % i, [128, 512], st=st2) for i in range(2)]; t_W = [Tok(), Tok()]
                        Abc = [sb("l_Abc%d" % i, [128, 512], st=st2) for i in range(2)]; t_Abc = [Tok(), Tok()]
                        Am = [sb("l_Am%d" % i, [128, 512], st=st2) for i in range(2)]; t_Am = [Tok(), Tok()]
                        fin = sb("l_fin", [128, 8], st=st2); t_fin = Tok()
                        hn = sb("l_hn", [128, 512], st=st2); junk = sb("l_junk", [128, 128], st=st2); ss = sb("l_ss", [128, 8], st=st2); t_hn = Tok()
                        ogt = [sb("l_og%d" % i, [128, 4, 128], st=st2) for i in range(2)]; t_og = [Tok(), Tok()]
                        mo = [sb("l_mo%d" % i, [128, 4, 128], BF16, st=st2) for i in range(2)]; t_mo = [Tok(), Tok()]
                        k.load(kT[:], mkT[b].rearrange("h p t -> p h t"), [], [t_qk])
                        k.load(qT[:], mqT[b].rearrange("h p t -> p h t"), [], [t_qk], q="act")
                        k.v("pool", "memset", [], [t_vq], vq[:, :, :, 128:130], 1.0)
                        for tt in range(18):
                            k.load(vq[:, tt, :, 0:128], mvT[b, tt * 128:(tt + 1) * 128, :].rearrange("p (h d) -> p h d", d=128), [], [t_vq],
                                   q="sp" if tt % 2 else "act")
                        mitems = [(0, h, c0, W) for h in range(4) for (c0, W) in CHUNKS] + \
                                 [(1, h, c0, W) for (c0, W) in CHUNKS for h in range(4)]

                        def mjl(d_, c0, W):
                            j0 = c0 // 128
                            nj = W // 128
                            if d_ == 0:
                                return [(j, j - j0 if j >= j0 else None) for j in range(0, j0 + nj)]
                            if c0 < LC:
                                return [(j, j - j0) for j in range(j0, j0 + nj)]
                            return [(0, None), (1, None)] + [(j, j - j0 if j < j0 + nj else None) for j in range(j0, 18)]

                        def mstageA(idx):
                            d_, h, c0, W = mitems[idx]
                            mk = maskF if d_ == 0 else maskB
                            jl = mjl(d_, c0, W)
                            pi = idx % 2
                            pA = nb()
                            k.mm(ps[pA][:, 0:W], sel[:, h, :], arow[d_][0:4, c0:c0 + W], True, True, [t_c, t_rows], [t_ps[pA]])
                            k.act(Abc[pi][:, 0:W], ps[pA][:, 0:W], AF.Copy, [t_ps[pA]], [t_Abc[pi]])
                            for ji, (j, jj) in enumerate(jl):
                                pS = nb()
                                k.mm(ps[pS][:, 0:W], kT[:, h, j * 128:(j + 1) * 128], qT[:, h, c0:c0 + W], True, True, [t_qk], [t_ps[pS]])
                                wi = ji % 2
                                bcol = colsb[:, j, d_ * 12 + 4 + h:d_ * 12 + 5 + h]
                                if jj is None:
                                    k.act(Wt[wi][:, 0:W], Abc[pi][:, 0:W], AF.Exp, [t_Abc[pi], t_cols], [t_W[wi]], bias=bcol)
                                else:
                                    k.v("pool", "tensor_tensor", [t_Abc[pi], t_c], [t_Am[wi]], out=Am[wi][:, 0:W], in0=Abc[pi][:, 0:W], in1=mk[:, jj, 0:W], op=ALU.add)
                                    k.act(Wt[wi][:, 0:W], Am[wi][:, 0:W], AF.Exp, [t_Am[wi], t_cols], [t_W[wi]], bias=bcol)
                                k.v("dve", "tensor_tensor", [t_ps[pS], t_W[wi]], [t_P[pi]], out=Pb[pi][:, ji, 0:W], in0=ps[pS][:, 0:W], in1=Wt[wi][:, 0:W], op=ALU.mult)

                        def mstageB(idx):
                            d_, h, c0, W = mitems[idx]
                            jl = mjl(d_, c0, W)
                            pi = idx % 2
                            j0 = c0 // 128
                            for ts_ in range(W // 128):
                                tt = j0 + ts_
                                pa = nb()
                                for ji, (j, jj) in enumerate(jl):
                                    k.mm(ps[pa][:, 0:130], Pb[pi][:, ji, ts_ * 128:(ts_ + 1) * 128], vq[:, j, h, :], ji == 0, ji == len(jl) - 1, [t_P[pi], t_vq], [t_ps[pa]])
                                k.v("dve", "tensor_copy", [t_ps[pa]], [t_fin], out=fin[:, 3:4], in_=ps[pa][:, 128:129])
                                k.v("dve", "scalar_tensor_tensor", [t_fin], [t_fin], out=fin[:, 0:1], in0=fin[:, 3:4], scalar=-1.0, in1=fin[:, 3:4], op0=ALU.mult, op1=ALU.max)
                                k.v("dve", "tensor_tensor", [t_fin, t_cols], [t_fin], out=fin[:, 1:2], in0=fin[:, 0:1], in1=emn[:, tt, d_ * 4 + h:d_ * 4 + h + 1], op=ALU.max)
                                k.v("dve", "reciprocal", [t_fin], [t_fin], out=fin[:, 2:3], in_=fin[:, 1:2])
                                hs = hsum[:, tt, h * 128:(h + 1) * 128]
                                if d_ == 0:
                                    k.v("dve", "tensor_scalar", [t_ps[pa], t_fin], [t_hs], out=hs, in0=ps[pa][:, 0:128], scalar1=fin[:, 2:3], scalar2=None, op0=ALU.mult)
                                else:
                                    k.v("dve", "scalar_tensor_tensor", [t_ps[pa], t_fin], [t_hs], out=hs, in0=ps[pa][:, 0:128], scalar=fin[:, 2:3], in1=hs, op0=ALU.mult, op1=ALU.add)

                        def mpost(tt):
                            i2 = tt % 2
                            k.load(ogt[i2][:], mog[b, :, :, tt * 128:(tt + 1) * 128].rearrange("h p t -> p h t"), [], [t_og[i2]])
                            for h in range(4):
                                k.act(junk[:], hsum[:, tt, h * 128:(h + 1) * 128], AF.Square, [t_hs], [t_hn], accum_out=ss[:, h:h + 1])
                            k.act(ss[:, 4:8], ss[:, 0:4], AF.Sqrt, [t_hn], [t_hn], scale=1.0 / 128, bias=EPS)
                            k.v("dve", "reciprocal", [t_hn], [t_hn], out=ss[:, 4:8], in_=ss[:, 4:8])
                            pb = nb()
                            for h in range(4):
                                k.v("dve", "tensor_scalar", [t_hs, t_hn], [t_hn], out=hn[:, h * 128:(h + 1) * 128], in0=hsum[:, tt, h * 128:(h + 1) * 128],
                                    scalar1=ss[:, 4 + h:5 + h], scalar2=None, op0=ALU.mult)
                            for h in range(4):
                                k.tr(ps[pb][:, h * 128:(h + 1) * 128], hn[:, h * 128:(h + 1) * 128], ident[:], [t_hn, t_const], [t_ps[pb]])
                            k.v("dve", "tensor_tensor", [t_ps[pb], t_og[i2]], [t_mo[i2]], out=mo[i2][:], in0=ps[pb][:].rearrange("p (h t) -> p h t", t=128),
                                in1=ogt[i2][:], op=ALU.mult)
                            k.load(mixD[b, 4:8, :, tt * 128:(tt + 1) * 128].rearrange("r p t -> p r t"), mo[i2][:], [t_mo[i2]], [Tok()], q="act")

                        mstageA(0)
                        for idx in range(len(mitems)):
                            if idx + 1 < len(mitems):
                                mstageA(idx + 1)
                            mstageB(idx)
                            d_, h, c0, W = mitems[idx]
                            if d_ == 1 and h == 3:
                                for tt in range(c0 // 128, (c0 + W) // 128):
                                    mpost(tt)
                    P.barrier()
            if "mixB" in k.stop:
                return

            PI = math.pi

            def hyena_seg(L, tok0, sfx):
                NT = L // 128
                NN = 2 * L
                TC = min(512, L)
                with contextlib.ExitStack() as st:
                    FS = sb("y_FS", [128, NT, 512], BF16, st=st); FD = sb("y_FD", [128, NT, 512], BF16, st=st); t_F = Tok()
                    Z1 = [sb("y_Z1%d" % b, [128, NT, 512], BF16, st=st) for b in range(NB)]
                    Z2 = [sb("y_Z2%d" % b, [128, NT, 512], BF16, st=st) for b in range(NB)]; t_Z = Tok()
                    with contextlib.ExitStack() as st1:
                        zT = sb("y_zT", [33, L], st=st1); fw1 = sb("y_fw1", [33, 64], st=st1); fw2 = sb("y_fw2", [64, 64], st=st1)
                        fw3 = sb("y_fw3", [64, 1024], st=st1); fv = sb("y_fv", [64, 8], st=st1); t_c = Tok()
                        adl = sb("y_adl", [128, 1024], st=st1); negt = sb("y_negt", [128, NT], st=st1)
                        k.load(zT[:], din["hy_z" + sfx][:, :], [], [t_c]); k.load(fw1[:], din["hy_fw1"][:, :], [], [t_c])
                        k.load(fw2[:], din["hy_fw2"][:, :], [], [t_c]); k.load(fw3[:], din["hy_fw3"][:, :], [], [t_c])
                        k.load(fv[:, 0:3], din["hy_fvec"][:, :], [], [t_c])
                        k.load(adl[:], din["hy_absdelta"][:, :], [], [t_c]); k.load(negt[:], din["hy_negt" + sfx][:, :], [], [t_c])
                        k.v("dve", "tensor_tensor", [t_c], [t_c], out=fv[:, 3:4], in0=fv[:, 0:1], in1=fv[:, 1:2], op=ALU.mult)
                        k.v("dve", "tensor_tensor", [t_c], [t_c], out=fv[:, 4:5], in0=fv[:, 0:1], in1=fv[:, 2:3], op=ALU.mult)
                        h1 = sb("y_h1", [64, L], st=st1); h2 = sb("y_h2", [64, L], st=st1); t_h = Tok()
                        arg = sb("y_arg", [64, 512], st=st1); kk = sb("y_kk", [64, 512], st=st1); t_arg = Tok()
                        MAGIC = 12582912.0
                        for (w_, src, dst, bcol) in ((fw1, zT, h1, 3), (fw2, h1, h2, 4)):
                            for c0 in range(0, L, TC):
                                pb = nb()
                                k.mm(ps[pb][0:64, 0:TC], w_[:, :], src[:, c0:c0 + TC], True, True, [t_c, t_h], [t_ps[pb]])
                                k.act(arg[:, 0:TC], ps[pb][0:64, 0:TC], AF.Identity, [t_ps[pb], t_c], [t_arg], scale=fv[:, 0:1], bias=fv[:, bcol:bcol + 1])
                                k.v("dve", "tensor_scalar", [t_arg], [t_arg], out=kk[:, 0:TC], in0=arg[:, 0:TC], scalar1=1.0 / (2.0 * PI), scalar2=MAGIC, op0=ALU.mult, op1=ALU.add)
                                k.v("dve", "tensor_scalar", [t_arg], [t_arg], out=kk[:, 0:TC], in0=kk[:, 0:TC], scalar1=MAGIC, scalar2=None, op0=ALU.subtract)
                                k.v("dve", "scalar_tensor_tensor", [t_arg], [t_arg], out=arg[:, 0:TC], in0=kk[:, 0:TC], scalar=-2.0 * PI, in1=arg[:, 0:TC], op0=ALU.mult, op1=ALU.add)
                                k.v("dve", "tensor_scalar", [t_arg], [t_arg], out=arg[:, 0:TC], in0=arg[:, 0:TC], scalar1=-3.141592, scalar2=3.141592, op0=ALU.max, op1=ALU.min)
                                k.act(dst[:, c0:c0 + TC], arg[:, 0:TC], AF.Sin, [t_arg], [t_h])
                        dec = sb("y_dec", [128, 1024], st=st1); fl = sb("y_fl", [128, 1024], st=st1); t_fl = Tok()
                        for nt in range(NT):
                            k.act(dec[:], adl[:], AF.Exp, [t_c], [t_fl], scale=negt[:, nt:nt + 1])
                            for hh in range(2):
                                pb = nb()
                                k.mm(ps[pb][:], h2[:, nt * 128:(nt + 1) * 128], fw3[:, hh * 512:(hh + 1) * 512], True, True, [t_h, t_c], [t_ps[pb]])
                                k.v("dve", "scalar_tensor_tensor", [t_ps[pb], t_fl], [t_fl], out=fl[:, hh * 512:(hh + 1) * 512], in0=dec[:, hh * 512:(hh + 1) * 512],
                                    scalar=0.05, in1=ps[pb][:], op0=ALU.add, op1=ALU.mult)
                            k.v("dve", "tensor_tensor", [t_fl], [t_F], out=FS[:, nt, :], in0=fl[:, 0:512], in1=fl[:, 512:1024], op=ALU.add)
                            k.v("dve", "tensor_tensor", [t_fl], [t_F], out=FD[:, nt, :], in0=fl[:, 0:512], in1=fl[:, 512:1024], op=ALU.subtract)
                            if nt == 0:
                                k.v("dve", "tensor_copy", [t_fl], [t_F], out=FS[0:1, 0, :], in_=fl[0:1, 0:512])
                                k.v("dve", "tensor_copy", [t_fl], [t_F], out=FD[0:1, 0, :], in_=fl[0:1, 0:512])
                    P.barrier()
                    if "filt" in k.dbg:
                        fdb = nc.dram_tensor("filt" + sfx, [2, 128, NT, 512], BF16, kind="ExternalOutput").ap()
                        finals.append(k.load(fdb[0], FS[:], [t_F], [])); finals.append(k.load(fdb[1], FD[:], [t_F], []))
                    with contextlib.ExitStack() as st2:
                        gTs = [sb("y_gT%d" % b, [128, NT, 512], BF16, st=st2) for b in range(NB)]; t_g = Tok()
                        for b in range(NB):
                            k.load(gTs[b][:], hgT[b, tok0:tok0 + L, :].rearrange("(tt p) c -> p tt c", p=128), [], [t_g], q="act")
                        Cf = [sb("y_Cf%d" % i, [128, NT, 128], BF16, st=st2) for i in range(2)]
                        Sf = [sb("y_Sf%d" % i, [128, NT, 128], BF16, st=st2) for i in range(2)]; t_cs = [Tok(), Tok()]
                        KA = sb("y_KA", [128, 512], st=st2); KB = sb("y_KB", [128, 512], st=st2); t_K = Tok()
                        Asb = sb("y_A", [128, 512], st=st2); t_A = Tok()
                        tq = [sb("y_t%d" % i, [128, 512], st=st2) for i in range(4)]; t_tq = Tok()
                        for ft in range(NT):
                            i2 = ft % 2
                            k.load(Cf[i2][:], din["dftC" + sfx][ft], [], [t_cs[i2]], q="sp")
                            k.load(Sf[i2][:], din["dftS" + sfx][ft], [], [t_cs[i2]], q="act")
                            pKA, pKB = nb(), nb()
                            for tt in range(NT):
                                k.mm(ps[pKA][:], Cf[i2][:, tt, :], FS[:, tt, :], tt == 0, tt == NT - 1, [t_cs[i2], t_F], [t_ps[pKA]])
                            for tt in range(NT):
                                k.mm(ps[pKB][:], Sf[i2][:, tt, :], FD[:, tt, :], tt == 0, tt == NT - 1, [t_cs[i2], t_F], [t_ps[pKB]])
                            k.act(KA[:], ps[pKA][:], AF.Copy, [t_ps[pKA]], [t_K])
                            k.act(KB[:], ps[pKB][:], AF.Copy, [t_ps[pKB]], [t_K])
                            for b in range(NB):
                                pA, pB = nb(), nb()
                                for tt in range(NT):
                                    k.mm(ps[pA][:], Cf[i2][:, tt, :], gTs[b][:, tt, :], tt == 0, tt == NT - 1, [t_cs[i2], t_g], [t_ps[pA]])
                                for tt in range(NT):
                                    k.mm(ps[pB][:], Sf[i2][:, tt, :], gTs[b][:, tt, :], tt == 0, tt == NT - 1, [t_cs[i2], t_g], [t_ps[pB]])
                                k.act(Asb[:], ps[pA][:], AF.Copy, [t_ps[pA]], [t_A])
                                k.v("dve", "tensor_tensor", [t_A, t_K], [t_tq], out=tq[0][:], in0=Asb[:], in1=KA[:], op=ALU.mult)
                                k.v("dve", "tensor_tensor", [t_ps[pB], t_K], [t_tq], out=tq[1][:], in0=ps[pB][:], in1=KB[:], op=ALU.mult)
                                k.v("pool", "tensor_tensor", [t_tq], [t_Z], out=Z1[b][:, ft, :], in0=tq[0][:], in1=tq[1][:], op=ALU.subtract)
                                k.v("dve", "tensor_tensor", [t_A, t_K], [t_tq], out=tq[2][:], in0=Asb[:], in1=KB[:], op=ALU.mult)
                                k.v("dve", "tensor_tensor", [t_ps[pB], t_K], [t_tq], out=tq[3][:], in0=ps[pB][:], in1=KA[:], op=ALU.mult)
                                k.v("pool", "tensor_tensor", [t_tq], [t_Z], out=Z2[b][:, ft, :], in0=tq[2][:], in1=tq[3][:], op=ALU.add)
                    P.barrier()
                    with contextlib.ExitStack() as st3:
                        CT = [sb("y_CT%d" % i, [128, NT, TC], BF16, st=st3) for i in range(2)]
                        ST = [sb("y_ST%d" % i, [128, NT, TC], BF16, st=st3) for i in range(2)]; t_ct = [Tok(), Tok()]
                        gc = [sb("y_gc%d" % i, [128, TC], st=st3) for i in range(2)]
                        x0 = [sb("y_x0%d" % i, [128, TC], st=st3) for i in range(2)]; t_in = [Tok(), Tok()]
                        yo = [sb("y_yo%d" % i, [128, TC], BF16, st=st3) for i in range(2)]; t_yo = [Tok(), Tok()]
                        hb = sb("y_hb", [128, 4], st=st3); t_hb = Tok()
                        k.load(hb[:], din["hy_bias_fm"][:, :], [], [t_hb])
                        n = 0
                        for ci, c0 in enumerate(range(0, L, TC)):
                            i2 = ci % 2
                            k.load(CT[i2][:], din["dftCT" + sfx][ci], [], [t_ct[i2]], q="sp")
                            k.load(ST[i2][:], din["dftST" + sfx][ci], [], [t_ct[i2]], q="act")
                            for b in range(NB):
                                for j in range(4):
                                    u2 = n % 2; n += 1
                                    k.load(gc[u2][:], hg[b, j, :, tok0 + c0:tok0 + c0 + TC], [], [t_in[u2]])
                                    k.load(x0[u2][:], hx0[b, j, :, tok0 + c0:tok0 + c0 + TC], [], [t_in[u2]], q="act")
                                    pY = nb()
                                    for ft in range(NT):
                                        k.mm(ps[pY][:, 0:TC], Z1[b][:, ft, j * 128:(j + 1) * 128], CT[i2][:, ft, :], ft == 0, False, [t_Z, t_ct[i2]], [t_ps[pY]])
                                    for ft in range(NT):
                                        k.mm(ps[pY][:, 0:TC], Z2[b][:, ft, j * 128:(j + 1) * 128], ST[i2][:, ft, :], False, ft == NT - 1, [t_Z, t_ct[i2]], [t_ps[pY]])
                                    k.v("dve", "tensor_scalar", [t_in[u2], t_hb], [t_in[u2]], out=gc[u2][:], in0=gc[u2][:], scalar1=hb[:, j:j + 1], scalar2=None, op0=ALU.mult)
                                    k.v("dve", "scalar_tensor_tensor", [t_ps[pY], t_in[u2]], [t_in[u2]], out=gc[u2][:], in0=ps[pY][:, 0:TC], scalar=2.0 / NN, in1=gc[u2][:],
                                        op0=ALU.mult, op1=ALU.add)
                                    k.v("dve", "tensor_tensor", [t_in[u2]], [t_yo[u2]], out=yo[u2][:], in0=gc[u2][:], in1=x0[u2][:], op=ALU.mult)
                                    k.load(mixD[b, j, :, tok0 + c0:tok0 + c0 + TC], yo[u2][:], [t_yo[u2]], [Tok()], q="act")
                    P.barrier()

            hyena_seg(LL, LC, "")
            hyena_seg(LC, 0, "c")
            if "mixC" in k.stop:
                return
            ffn_issue = ffn_prefetch(0) if "ffn0" not in skip else (lambda: None)
            with contextlib.ExitStack() as st:
                wo = sb("o_wo", [128, 8, D], BF16, st=st); t_w = Tok()
                k.load(wo[:], din["w_out_0"].rearrange("(r p) c -> p r c", p=128), [], [t_w], q="pool")
                ffn_issue()
                mx = [sb("o_mx%d" % i, [128, 8, 512], BF16, st=st) for i in range(2)]; t_mx = [Tok(), Tok()]
                xw = [sb("o_xw%d" % i, [128, KT, 512], st=st) for i in range(2)]; t_xw = [Tok(), Tok()]
                n = 0
                for b in range(NB):
                    for ci, (c0, W) in enumerate(CHUNKS):
                        v = b if c0 >= LC else 2
                        i2 = n % 2; n += 1
                        k.load(mx[i2][:, :, 0:W], mixD[b, :, :, c0:c0 + W].rearrange("r p t -> p r t"), [], [t_mx[i2]], q="act")
                        k.load(xw[i2][:, :, 0:W], xS[b, :, :, c0:c0 + W], [tS[b][ci]], [t_xw[i2]])
                        for kt in range(KT):
                            pY = nb()
                            for r in range(8):
                                k.mm(ps[pY][:, 0:W], wo[:, r, kt * 128:(kt + 1) * 128], mx[i2][:, r, 0:W], r == 0, r == 7, [t_w, t_mx[i2]], [t_ps[pY]])
                            k.v("dve", "scalar_tensor_tensor", [t_ps[pY], t_mods[l]], [t_xw[i2]], out=xw[i2][:, kt, 0:W], in0=ps[pY][:, 0:W],
                                scalar=mods[l][:, 16 + kt, v:v + 1], in1=xw[i2][:, kt, 0:W], op0=ALU.mult, op1=ALU.add)
                        k.load(xD[b, :, :, c0:c0 + W], xw[i2][:, :, 0:W], [t_xw[i2]], [tD[b][ci]])
                P.barrier()

        din = k.din
        for nm_, shp in (("ffn_up_%d", [D, 2 * DFF]), ("ffn_down_%d", [DFF, D]), ("ffn_conv_w_%d_fm", [128, FT, 3]), ("ffn_conv_b_%d_fm", [128, FT])):
            for l in range(2):
                k.inp(nm_ % l, shp)
        k.inp("mla_w_down_ext", [D, 768]); k.inp("mla_w_uq_ext", [384, 2048]); k.inp("mla_w_ukv", [256, 2048])
        k.inp("mla_w_o", [D, D]); k.inp("mla_q_norm_fm", [128, 3]); k.inp("mla_kv_norm_fm", [128, 2])
        k.inp("ropeC", [64, LL]); k.inp("ropeS", [64, LL])
        k.inp("w_in_0", [D, 3600]); k.inp("w_out_0", [D, D])
        k.inp("hy_conv_w_fm", [128, 12, 3]); k.inp("hy_conv_b_fm", [128, 12]); k.inp("ml_conv_w_fm", [128, 8, 3])
        k.inp("ml_gate_b_fm", [4, 4]); k.inp("ml_norm_g_fm", [128, 4]); k.inp("hy_bias_fm", [128, 4])
        k.inp("maskF", [128, 4, 512]); k.inp("maskB", [128, 4, 512]); k.inp("sel4", [4, 4, 128])
        k.inp("hy_fw1", [33, 64]); k.inp("hy_fw2", [64, 64]); k.inp("hy_fw3", [64, 1024]); k.inp("hy_fvec", [64, 3])
        k.inp("hy_absdelta", [128, 1024])
        for sfx_, L_ in (("", LL), ("c", LC)):
            k.inp("hy_z" + sfx_, [33, L_]); k.inp("hy_negt" + sfx_, [128, L_ // 128])
            nt_ = L_ // 128
            tc_ = min(512, L_)
            for nm_ in ("dftC", "dftS"):
                k.inp(nm_ + sfx_, [nt_, 128, nt_, 128], BF16)
            for nm_ in ("dftCT", "dftST"):
                k.inp(nm_ + sfx_, [L_ // tc_, 128, nt_, tc_], BF16)

        A = (xT, t_xT)
        B = (xT2, t_xT2)
        if "mix0" not in skip:
            mix0_phase(0, A[0], A[1], B[0], B[1])
        else:
            A, B = B, A
        if "ffn0" not in skip:
            ffn_phase(0, B[0], B[1], A[0], A[1], din["ffn_up_0"], din["ffn_down_0"], din["ffn_conv_w_0_fm"], din["ffn_conv_b_0_fm"])
        else:
            A, B = B, A
        if "mix1" not in skip:
            mla_phase(1, A[0], A[1], B[0], B[1])
        else:
            A, B = B, A
        if "ffn1" not in skip:
            ffn_phase(1, B[0], B[1], A[0], A[1], din["ffn_up_1"], din["ffn_down_1"], din["ffn_conv_w_1_fm"], din["ffn_conv_b_1_fm"])
        else:
            A, B = B, A
        xT, t_xT = A

        with contextlib.ExitStack() as st:
          if "final" not in skip:
              gb = sb("fin_g", [128, D], st=st)
              t_gb = Tok()
              k.load(gb[:], fnorm_in[0:1, :].to_broadcast([128, D]), [], [t_gb])
              xc = [sb("fin_xc%d" % i, [128, KT, 512], st=st) for i in range(2)]
              t_xc = [Tok(), Tok()]
              yo = [sb("fin_y%d" % i, [128, D], st=st) for i in range(2)]
              t_yo = [Tok(), Tok()]
              junk = sb("fin_junk", [128, D], st=st)
              stat = sb("fin_stat", [128, 4], st=st)
              t_stat = Tok()
              n = 0
              for b in range(NB):
                  for ci, (c0, cw) in enumerate(CHUNKS):
                      if c0 < LC:
                          continue
                      c2 = ci % 2
                      k.load(xc[c2][:], xT[b, :, :, c0:c0 + 512], [t_xT[b][ci]], [t_xc[c2]], q="sp" if c2 == 0 else "act")
                      for tt in range(cw // 128):
                          t0 = c0 + tt * 128
                          i2 = n % 2
                          pa, pb = (2 * n) % 8, (2 * n + 1) % 8
                          n += 1
                          for kt in range(KT):
                              pp = pa if kt < 4 else pb
                              k.tr(ps[pp][:, (kt % 4) * 128:(kt % 4 + 1) * 128], xc[c2][:, kt, tt * 128:(tt + 1) * 128], ident[:],
                                   [t_xc[c2], t_const], [t_ps[pp]])
                          k.act(junk[:, 0:512], ps[pa][:], AF.Square, [t_ps[pa]], [t_stat], accum_out=stat[:, 0:1])
                          k.act(junk[:, 512:1024], ps[pb][:], AF.Square, [t_ps[pb]], [t_stat], accum_out=stat[:, 1:2])
                          k.v("dve", "tensor_tensor", [t_stat], [t_stat], out=stat[:, 2:3], in0=stat[:, 0:1],
                              in1=stat[:, 1:2], op=ALU.add)
                          k.act(stat[:, 3:4], stat[:, 2:3], AF.Sqrt, [t_stat], [t_stat], scale=1.0 / D, bias=EPS)
                          k.v("dve", "reciprocal", [t_stat], [t_stat], out=stat[:, 3:4], in_=stat[:, 3:4])
                          for hh, pp in ((0, pa), (1, pb)):
                              k.v("dve", "scalar_tensor_tensor", [t_ps[pp], t_stat, t_gb], [t_yo[i2]],
                                  out=yo[i2][:, hh * 512:(hh + 1) * 512], in0=ps[pp][:], scalar=stat[:, 3:4],
                                  in1=gb[:, hh * 512:(hh + 1) * 512], op0=ALU.mult, op1=ALU.mult)
                          finals.append(k.load(out[b, t0 - LC:t0 - LC + 128, :], yo[i2][:], [t_yo[i2]], []))
              P.barrier()

        P.emit(finals)
    return k


def _fm(v, nt):
    return np.ascontiguousarray(np.asarray(v, np.float32).reshape(nt, 128).T)


def make_in_maps(inp):
    shared = {
        "ident": np.eye(128, dtype=np.float32),
        "final_norm": np.asarray(inp["final_norm"], np.float32).reshape(1, D),
    }
    per_layer = (
        (inp["ada_w_0"], inp["ada_b_0"], inp["norm_mix_0"], inp["norm_ffn_0"],
         inp["ffn_up_0"], inp["ffn_down_0"], inp["ffn_conv_w_0"], inp["ffn_conv_b_0"]),
        (inp["ada_w_1"], inp["ada_b_1"], inp["norm_mix_1"], inp["norm_ffn_1"],
         inp["ffn_up_1"], inp["ffn_down_1"], inp["ffn_conv_w_1"], inp["ffn_conv_b_1"]),
    )
    for l, (aw_, ab_, nmx_, nff_, fu_, fd_, fcw_, fcb_) in enumerate(per_layer):
        shared["ada_w_%d" % l] = np.ascontiguousarray(aw_, dtype=np.float32)
        shared["ada_b_%d_fm" % l] = _fm(ab_, 48)
        shared["norm_mix_%d_fm" % l] = _fm(nmx_, KT)
        shared["norm_ffn_%d_fm" % l] = _fm(nff_, KT)
        shared["ffn_up_%d" % l] = np.ascontiguousarray(fu_, dtype=np.float32)
        shared["ffn_down_%d" % l] = np.ascontiguousarray(fd_, dtype=np.float32)
        cwv = np.asarray(fcw_, np.float32)
        shared["ffn_conv_w_%d_fm" % l] = np.ascontiguousarray(cwv.reshape(3, FT, 128).transpose(2, 1, 0))
        shared["ffn_conv_b_%d_fm" % l] = _fm(fcb_, FT)
    perm = np.arange(64).reshape(2, 2, 16)[:, ::-1, :].reshape(64)
    wd = np.asarray(inp["mla_w_down"], np.float32)
    shared["mla_w_down_ext"] = np.ascontiguousarray(np.concatenate([wd, wd[:, 640:704][:, perm]], axis=1))
    wq = np.asarray(inp["mla_w_uq"], np.float32).reshape(384, 8, 192)
    shared["mla_w_uq_ext"] = np.ascontiguousarray(np.concatenate([wq, wq[:, :, 128:192][:, :, perm]], axis=2).reshape(384, 2048))
    shared["mla_w_ukv"] = np.ascontiguousarray(inp["mla_w_ukv"], dtype=np.float32)
    shared["mla_w_o"] = np.ascontiguousarray(inp["mla_w_o"], dtype=np.float32)
    shared["mla_q_norm_fm"] = _fm(inp["mla_q_norm"], 3)
    shared["mla_kv_norm_fm"] = _fm(inp["mla_kv_norm"], 2)
    tpos = np.arange(LL)
    inv = (10000.0 ** (-np.arange(0, 32, 2, dtype=np.float32) / 32.0)).astype(np.float32)
    ang = np.stack([(tpos // 64).astype(np.float32)[:, None] * inv, (tpos % 64).astype(np.float32)[:, None] * inv], axis=1)
    cosT = np.zeros((2, 2, 16, LL), np.float32)
    sinT = np.zeros((2, 2, 16, LL), np.float32)
    for a_ in range(2):
        for hf in range(2):
            cosT[a_, hf] = np.cos(ang[:, a_, :]).T
            sinT[a_, hf] = np.sin(ang[:, a_, :]).T * (-1.0 if hf == 0 else 1.0)
    shared["ropeC"] = np.ascontiguousarray(cosT.reshape(64, LL))
    shared["ropeS"] = np.ascontiguousarray(sinT.reshape(64, LL))
    shared["w_in_0"] = np.ascontiguousarray(inp["w_in_0"], dtype=np.float32)
    shared["w_out_0"] = np.ascontiguousarray(inp["w_out_0"], dtype=np.float32)
    shared["hy_conv_w_fm"] = np.ascontiguousarray(np.asarray(inp["hy_conv_w"], np.float32).reshape(3, 12, 128).transpose(2, 1, 0))
    shared["hy_conv_b_fm"] = _fm(inp["hy_conv_b"], 12)
    shared["ml_conv_w_fm"] = np.ascontiguousarray(np.asarray(inp["ml_conv_w"], np.float32).reshape(3, 8, 128).transpose(2, 1, 0))
    shared["ml_gate_b_fm"] = np.ascontiguousarray(np.asarray(inp["ml_gate_b"], np.float32).reshape(4, 4).T)
    shared["ml_norm_g_fm"] = _fm(inp["ml_norm_g"], 4)
    shared["hy_bias_fm"] = _fm(inp["hy_bias"], 4)
    sidx = np.arange(128)[:, None, None] + 128 * np.arange(4)[None, :, None]
    tidx = np.arange(512)[None, None, :]
    shared["maskF"] = np.where(sidx > tidx, -30000.0, 0.0).astype(np.float32)
    shared["maskB"] = np.where(sidx < tidx, -30000.0, 0.0).astype(np.float32)
    sel = np.zeros((4, 4, 128), np.float32)
    for h_ in range(4):
        sel[h_, h_, :] = 1.0
    shared["sel4"] = sel
    shared["hy_fw1"] = np.ascontiguousarray(inp["hy_fw1"], dtype=np.float32)
    shared["hy_fw2"] = np.ascontiguousarray(inp["hy_fw2"], dtype=np.float32)
    shared["hy_fw3"] = np.ascontiguousarray(inp["hy_fw3"], dtype=np.float32)
    shared["hy_fvec"] = np.ascontiguousarray(np.stack([inp["hy_freq"], inp["hy_fb1"], inp["hy_fb2"]], axis=1), dtype=np.float32)
    deltas = np.linspace(math.log(1e-2) / 0.3, math.log(1e-2) / 1.5, 512, dtype=np.float32)
    shared["hy_absdelta"] = np.ascontiguousarray(np.broadcast_to(np.abs(np.tile(deltas, 2))[None, :], (128, 1024)), dtype=np.float32)
    for sfx_, L_ in (("", LL), ("c", LC)):
        pos = np.arange(L_, dtype=np.float32)
        tg = (pos / np.float32(L_ - 1)).astype(np.float32)
        fb = np.linspace(1e-4, 15.0, 16, dtype=np.float32)
        angz = (np.float32(2.0 * math.pi / L_) * pos[:, None] * fb[None, :]).astype(np.float32)
        zz = np.concatenate([tg[:, None], np.cos(angz), -np.sin(angz)], axis=-1).astype(np.float32)
        shared["hy_z" + sfx_] = np.ascontiguousarray(zz.T)
        shared["hy_negt" + sfx_] = np.ascontiguousarray((-tg).reshape(L_ // 128, 128).T)
        th = 2.0 * np.pi * np.outer(np.arange(L_, dtype=np.float64), np.arange(L_, dtype=np.float64) + 0.5) / (2.0 * L_)
        Cm = np.cos(th).astype(np.float32); Sm = np.sin(th).astype(np.float32)
        bf = ml_dtypes.bfloat16
        nt_ = L_ // 128
        tc_ = min(512, L_)
        for nm_, M_ in (("dftC", Cm), ("dftS", Sm)):
            shared[nm_ + sfx_] = np.ascontiguousarray(M_.reshape(nt_, 128, nt_, 128).transpose(2, 1, 0, 3)).astype(bf)
            shared[nm_ + "T" + sfx_] = np.ascontiguousarray(M_.T.reshape(nt_, 128, L_ // tc_, tc_).transpose(2, 1, 0, 3)).astype(bf)
    maps = []
    x = np.asarray(inp["x"], np.float32)
    ctx = np.asarray(inp["ctx"], np.float32)
    c = np.asarray(inp["c"], np.float32)
    c_ctx = np.asarray(inp["c_ctx"], np.float32)
    for core in range(NCORES):
        b0 = core * NB
        cm = np.stack([_fm(c[b0], KT), _fm(c[b0 + 1], KT), _fm(c_ctx, KT)], axis=-1)
        m = dict(shared)
        m["x"] = np.ascontiguousarray(x[b0:b0 + NB])
        m["ctx"] = np.ascontiguousarray(ctx[b0:b0 + NB])
        m["cm"] = np.ascontiguousarray(cm)
        maps.append(m)
    return maps


_CACHE = {}


def kernel(**inputs):
    if "k" not in _CACHE:
        _CACHE["k"] = build()
    k = _CACHE["k"]
    maps = make_in_maps(inputs)
    maps = [{n: m[n] for n in k.din} for m in maps]
    res = run_bass_kernel_spmd(k.nc, maps, core_ids=list(range(NCORES)))
    return np.concatenate([np.asarray(r["out"]) for r in res.results], axis=0).astype(np.float32)
```

```python
import contextlib
import math
import numpy as np
import ml_dtypes
import concourse.bass as bass
import concourse.mybir as mybir
from concourse.bass_utils import run_bass_kernel_spmd

F32 = mybir.dt.float32
BF16 = mybir.dt.bfloat16
AF = mybir.ActivationFunctionType
ALU = mybir.AluOpType
AX = mybir.AxisListType

NCORES = 8
D = 1024
KT = 8
LC = 256
LL = 2048
T = LC + LL
NB = 2
EPS = 1e-6
DFF = 2816
FT = 22
CHUNKS = [(0, 256)] + [(256 + 512 * i, 512) for i in range(4)]


class Tok:
    __slots__ = ("name", "last_w", "readers", "excl")

    def __init__(self, name="", excl=False):
        self.name = name
        self.last_w = None
        self.readers = []
        self.excl = excl


class Op:
    __slots__ = ("eng", "fn", "deps", "needs_inc", "is_dma", "sem", "semval")

    def __init__(self, eng, fn, deps, is_dma):
        self.eng = eng
        self.fn = fn
        self.deps = deps
        self.needs_inc = False
        self.is_dma = is_dma
        self.sem = None
        self.semval = None


class Prog:
    def __init__(self, nc, n_dma_sems=8):
        self.nc = nc
        self.ops = []
        self.n_dma_sems = n_dma_sems
        self.barrier_op = None
        self.bar_dram = None
        self.since_barrier = []

    def op(self, eng, fn, reads=(), writes=(), dma=False):
        deps = []
        seen = set()

        def add(o):
            if o is None or id(o) in seen:
                return
            if (not dma) and eng == "pe" and o.eng == "pe" and not o.is_dma:
                return
            seen.add(id(o))
            deps.append(o)

        add(self.barrier_op)
        for t in reads:
            add(t.last_w)
            if t.excl:
                for r in t.readers:
                    add(r)
        for t in writes:
            add(t.last_w)
            for r in t.readers:
                add(r)
        o = Op(eng, fn, deps, dma)
        for d in deps:
            d.needs_inc = True
        for t in reads:
            if t.excl:
                t.last_w = o
                t.readers = []
            else:
                t.readers.append(o)
        for t in writes:
            t.last_w = o
            t.readers = []
        self.ops.append(o)
        self.since_barrier.append(o)
        return o

    def dma(self, eng, fn, reads=(), writes=()):
        return self.op(eng, fn, reads, writes, dma=True)

    def barrier(self):
        nc = self.nc
        last = {}
        deps = []
        for o in self.since_barrier:
            if o.is_dma:
                deps.append(o)
            else:
                last[o.eng] = o
        deps.extend(last.values())
        if self.barrier_op is not None:
            deps.append(self.barrier_op)
        if self.bar_dram is None:
            self.bar_dram = nc.dram_tensor("bar_scratch", [2, 16], F32, kind="Internal").ap()
        bd = self.bar_dram
        bsrc = self.bar_src
        b = Op("sp", lambda: nc.sync.dma_start(out=bd[0:1, :], in_=bsrc), deps, True)
        for d in deps:
            d.needs_inc = True
        b.needs_inc = True
        self.ops.append(b)
        self.barrier_op = b
        self.since_barrier = []
        return b

    def emit(self, final_ops):
        nc = self.nc
        final_ops = list(final_ops) + ([self.barrier_op] if self.barrier_op is not None else [])
        for o in final_ops:
            o.needs_inc = True
        engs = {"pe": nc.tensor, "act": nc.scalar, "dve": nc.vector, "pool": nc.gpsimd, "sp": nc.sync}
        with contextlib.ExitStack() as st:
            csem = {e: st.enter_context(nc.semaphore("c_" + e)) for e in engs}
            dsem = {e: [st.enter_context(nc.semaphore("d_%s_%d" % (e, i))) for i in range(self.n_dma_sems)]
                    for e in ("sp", "act", "pool")}
            ccount = {e: 0 for e in engs}
            dcount = {e: [0] * self.n_dma_sems for e in dsem}
            drr = {e: 0 for e in dsem}
            waited = {e: {} for e in engs}
            n_wait = 0
            for o in self.ops:
                h = engs[o.eng]
                w = waited[o.eng]
                for d in o.deps:
                    key = d.sem
                    if w.get(id(key), 0) < d.semval:
                        h.wait_ge(key, d.semval)
                        w[id(key)] = d.semval
                        n_wait += 1
                if o.is_dma:
                    j = drr[o.eng]
                    drr[o.eng] = (j + 1) % self.n_dma_sems
                    s = dsem[o.eng][j]
                    prev = dcount[o.eng][j]
                    if prev > 0 and w.get(id(s), 0) < prev:
                        h.wait_ge(s, prev)
                        w[id(s)] = prev
                        n_wait += 1
                    ins = o.fn()
                    dcount[o.eng][j] = prev + 16
                    ins.then_inc(s, 16)
                    o.sem = s
                    o.semval = prev + 16
                else:
                    ins = o.fn()
                    if o.needs_inc:
                        ccount[o.eng] += 1
                        ins.then_inc(csem[o.eng], 1)
                        o.sem = csem[o.eng]
                        o.semval = ccount[o.eng]
            for o in final_ops:
                nc.sync.wait_ge(o.sem, o.semval)
            self.stats = dict(n_ops=len(self.ops), n_wait=n_wait, ccount=dict(ccount),
                              dcount={e: max(v) for e, v in dcount.items()})


class K:
    def __init__(self, stop_after=None, dbg=()):
        self.stop_after = stop_after
        self.stop = set(stop_after or ())
        self.dbg = set(dbg)
        self.nc = bass.Bass("TRN2", target_bir_lowering=False)
        self.P = Prog(self.nc)
        self.din = {}
        self.scr = {}
        self.rr = 0

    def inp(self, name, shape, dt=F32):
        self.din[name] = self.nc.dram_tensor(name, list(shape), dt, kind="ExternalInput").ap()
        return self.din[name]

    def scratch(self, name, shape, dt=F32):
        kind = "ExternalOutput" if name in self.dbg else "Internal"
        self.scr[name] = self.nc.dram_tensor(name, list(shape), dt, kind=kind).ap()
        return self.scr[name]

    def mm(self, out, lhsT, rhs, start, stop, reads, writes, **kw):
        nc = self.nc
        return self.P.op("pe", lambda: nc.tensor.matmul(out, lhsT=lhsT, rhs=rhs, start=start, stop=stop, **kw),
                         reads, writes)

    def tr(self, out, in_, ident, reads, writes):
        nc = self.nc
        return self.P.op("pe", lambda: nc.tensor.transpose(out, in_, ident), reads, writes)

    def act(self, out, in_, func, reads, writes, **kw):
        nc = self.nc
        return self.P.op("act", lambda: nc.scalar.activation(out=out, in_=in_, func=func, **kw), reads, writes)

    def v(self, eng, method, reads, writes, *a, **kw):
        nc = self.nc
        h = nc.vector if eng == "dve" else nc.gpsimd
        return self.P.op(eng, lambda: getattr(h, method)(*a, **kw), reads, writes)

    def load(self, out, in_, reads, writes, q="sp"):
        nc = self.nc
        h = {"sp": nc.sync, "act": nc.scalar, "pool": nc.gpsimd}[q]
        return self.P.dma(q, lambda: h.dma_start(out=out, in_=in_), reads, writes)


def build(stop_after=None, dbg=(), skip=()):
    k = K(stop_after, dbg)
    skip = set(skip)
    nc, P = k.nc, k.P

    x_in = k.inp("x", [NB, LL, D])
    ctx_in = k.inp("ctx", [NB, LC, D])
    cm_in = k.inp("cm", [128, KT, 3])
    ident_in = k.inp("ident", [128, 128])
    ada_w = [k.inp("ada_w_%d" % l, [D, 6 * D]) for l in range(2)]
    ada_b = [k.inp("ada_b_%d_fm" % l, [128, 48]) for l in range(2)]
    nmix = [k.inp("norm_mix_%d_fm" % l, [128, KT]) for l in range(2)]
    nffn = [k.inp("norm_ffn_%d_fm" % l, [128, KT]) for l in range(2)]
    fnorm_in = k.inp("final_norm", [1, D])
    out = nc.dram_tensor("out", [NB, LL, D], F32, kind="ExternalOutput").ap()
    P.bar_src = ident_in[0:1, 0:16]

    xT = k.scratch("xT", [NB, 128, KT, T])
    t_xT = [[Tok("xT%d_%d" % (b, c)) for c in range(len(CHUNKS))] for b in range(NB)]

    finals = []
    with contextlib.ExitStack() as gst:
        sbn = [0]

        def sb(name, shape, dt=F32, st=gst):
            sbn[0] += 1
            return st.enter_context(nc.sbuf_tensor("%s_%d" % (name, sbn[0]), list(shape), dt))

        ident = sb("ident_sb", [128, 128])
        identb = sb("identb_sb", [128, 128], BF16)
        onesb = sb("onesb", [128, 128], BF16)
        onesf = sb("onesf", [128, 128])
        mods = [sb("mods%d" % l, [128, 48, 3]) for l in range(2)]
        gs1 = [sb("gs1_%d" % l, [128, KT, 3]) for l in range(2)]
        gs2 = [sb("gs2_%d" % l, [128, KT, 3]) for l in range(2)]
        t_const = Tok("const")
        t_mods = [Tok("mods0"), Tok("mods1")]
        ps = [gst.enter_context(nc.psum_tensor("ps%d" % i, [128, 512], F32)) for i in range(8)]
        t_ps = [Tok("ps%d" % i, excl=True) for i in range(8)]

        k.load(ident[:], ident_in[:, :], [], [t_const])
        k.v("dve", "tensor_copy", [t_const], [t_const], out=identb[:], in_=ident[:])
        k.v("pool", "memset", [], [t_const], onesb[:], 1.0)
        k.v("pool", "memset", [], [t_const], onesf[:], 1.0)

        st01 = contextlib.ExitStack()
        st01.__enter__()
        with contextlib.nullcontext(st01) as st:
            xin = [sb("ph0_xin%d" % i, [128, D], st=st) for i in range(2)]
            t_xin = [Tok(), Tok()]
            stg = [sb("ph0_stg%d" % i, [128, KT, 512], st=st) for i in range(2)]
            t_stg = [Tok(), Tok()]
            n = 0
            for b in range(NB):
                for ci, (c0, cw) in enumerate(CHUNKS):
                    sg = ci % 2
                    for tt in range(cw // 128):
                        t0 = c0 + tt * 128
                        src = ctx_in[b, t0:t0 + 128, :] if t0 < LC else x_in[b, t0 - LC:t0 - LC + 128, :]
                        xi = n % 2
                        k.load(xin[xi][:], src, [], [t_xin[xi]], q="sp" if n % 2 == 0 else "act")
                        for half in range(2):
                            pb = (2 * n + half) % 8
                            for q4 in range(4):
                                kt = half * 4 + q4
                                k.tr(ps[pb][:, q4 * 128:(q4 + 1) * 128], xin[xi][:, kt * 128:(kt + 1) * 128],
                                     ident[:], [t_xin[xi], t_const], [t_ps[pb]])
                            dst = stg[sg][:, half * 4:half * 4 + 4, tt * 128:(tt + 1) * 128]
                            srcp = ps[pb][:].rearrange("p (a b) -> p a b", a=4)
                            if half == 0:
                                k.v("dve", "tensor_copy", [t_ps[pb]], [t_stg[sg]], out=dst, in_=srcp)
                            else:
                                k.act(dst, srcp, AF.Copy, [t_ps[pb]], [t_stg[sg]])
                        n += 1
                    k.load(xT[b, :, :, c0:c0 + cw], stg[sg][:, :, 0:cw], [t_stg[sg]], [t_xT[b][ci]], q="sp")

        with contextlib.nullcontext(st01) as st:
          if 'ph1' not in skip:
              cm = sb("ph1_cm", [128, KT, 3], st=st)
              scb = sb("ph1_scb", [128, KT, 3], BF16, st=st)
              abt = sb("ph1_ab", [128, 48], st=st)
              nm = sb("ph1_nm", [128, KT], st=st)
              nf = sb("ph1_nf", [128, KT], st=st)
              wt = [sb("ph1_w%d" % i, [128, KT, 1024], BF16, st=st) for i in range(2)]
              t_w = [Tok(), Tok()]
              t_cm = Tok()
              t_ab = Tok()
              k.load(cm[:], cm_in[:, :, :], [], [t_cm])
              k.act(scb[:], cm[:], AF.Silu, [t_cm], [t_cm])
              n = 0
              for l in range(2):
                  k.load(abt[:], ada_b[l][:, :], [], [t_ab])
                  k.load(nm[:], nmix[l][:, :], [], [t_ab])
                  k.load(nf[:], nffn[l][:, :], [], [t_ab])
                  pm = ps[l]
                  for cc in range(6):
                      wi = n % 2
                      n += 1
                      k.load(wt[wi][:], ada_w[l][:, cc * 1024:(cc + 1) * 1024].rearrange("(kt p) c -> p kt c", p=128),
                             [], [t_w[wi]], q="pool")
                      for jj in range(8):
                          j = cc * 8 + jj
                          for kt in range(KT):
                              k.mm(pm[:, j * 3:(j + 1) * 3], wt[wi][:, kt, jj * 128:(jj + 1) * 128], scb[:, kt, :],
                                   kt == 0, kt == KT - 1, [t_w[wi], t_cm], [t_ps[l]])
                  k.v("dve", "tensor_tensor", [t_ps[l], t_ab], [t_mods[l]], out=mods[l][:],
                      in0=pm[:, 0:144].rearrange("p (j v) -> p j v", v=3),
                      in1=abt[:].unsqueeze(2).to_broadcast([128, 48, 3]), op=ALU.add)
                  for (g, nn_, off) in ((gs1[l], nm, 8), (gs2[l], nf, 32)):
                      k.v("dve", "scalar_tensor_tensor", [t_mods[l], t_ab], [t_mods[l]], out=g[:],
                          in0=mods[l][:, off:off + 8, :], scalar=1.0,
                          in1=nn_[:].unsqueeze(2).to_broadcast([128, KT, 3]), op0=ALU.add, op1=ALU.mult)
              P.barrier()

        st01.__exit__(None, None, None)
        if "mods" in k.dbg and "ph1" not in skip:
            md = nc.dram_tensor("mods", [2, 128, 48, 3], F32, kind="ExternalOutput").ap()
            for l in range(2):
                finals.append(k.load(md[l], mods[l][:], [t_mods[l]], []))

        cur = {"x": xT, "t": t_xT}
        xT2 = k.scratch("xT2", [NB, 128, KT, T])
        t_xT2 = [[Tok() for c in range(len(CHUNKS))] for b in range(NB)]
        bankc = [0]

        def nb():
            bankc[0] = (bankc[0] + 1) % 8
            return bankc[0]

        def chunk_of(t0):
            return 0 if t0 < LC else 1 + (t0 - LC) // 512

        def toks_for(tl, b, a0, a1):
            return [tl[b][c] for c in range(chunk_of(a0), chunk_of(a1 - 1) + 1)]

        def normmod(st_bufs, xw, t_xw, W, gs, sh_off, l, v, hT, t_hT):
            sq, tmp, rs, t_tmp = st_bufs
            k.act(sq[:, :, 0:W], xw[:, :, 0:W], AF.Square, [t_xw], [t_tmp])
            pb = nb()
            for kt in range(KT):
                k.mm(ps[pb][:, 0:W], onesb[:], sq[:, kt, 0:W], kt == 0, kt == KT - 1, [t_tmp, t_const], [t_ps[pb]])
            k.act(rs[:, 0:W], ps[pb][:, 0:W], AF.Sqrt, [t_ps[pb]], [t_tmp], scale=1.0 / D, bias=EPS)
            k.v("dve", "reciprocal", [t_tmp], [t_tmp], out=rs[:, 0:W], in_=rs[:, 0:W])
            k.v("dve", "tensor_tensor", [t_xw, t_tmp], [t_tmp], out=tmp[:, :, 0:W], in0=xw[:, :, 0:W],
                in1=rs[:, 0:W].unsqueeze(1).to_broadcast([128, KT, W]), op=ALU.mult)
            for kt in range(KT):
                k.act(hT[:, kt, 0:W], tmp[:, kt, 0:W], AF.Identity, [t_tmp, t_mods[l]], [t_hT],
                      scale=gs[:, kt, v:v + 1], bias=mods[l][:, sh_off + kt, v:v + 1])

        def seg_windows(L, nw):
            res = []
            step = (L + nw - 1) // nw
            o0 = 0
            while o0 < L:
                o1 = min(o0 + step, L)
                res.append((max(o0 - 1, 0), min(o1 + 1, L), o0, o1))
                o0 = o1
            return res

        ffn_pref = {}

        def ffn_weights(l, st, issue=True):
            wu_in, wd_in = din["ffn_up_%d" % l], din["ffn_down_%d" % l]
            cw_in, cb_in = din["ffn_conv_w_%d_fm" % l], din["ffn_conv_b_%d_fm" % l]
            wup = sb("f_wup", [128, KT, 2 * DFF], BF16, st=st)
            wdn = sb("f_wdn", [128, FT, D], BF16, st=st)
            cwf = sb("f_cw", [128, FT, 3], st=st)
            cbf = sb("f_cb", [128, FT], st=st)
            t_w = {"c": Tok(), "d": Tok(), "g": [Tok() for _ in range(11)]}

            def do_loads():
                k.load(cwf[:], cw_in[:, :, :], [], [t_w["c"]])
                k.load(cbf[:], cb_in[:, :], [], [t_w["c"]])
                for cg in range(11):
                    for off in (0, DFF):
                        c0_ = off + cg * 256
                        k.load(wup[:, :, c0_:c0_ + 256], wu_in[:, c0_:c0_ + 256].rearrange("(kt p) c -> p kt c", p=128), [], [t_w["g"][cg]], q="pool")
                for j in range(FT):
                    k.load(wdn[:, j, :], wd_in[j * 128:(j + 1) * 128, :], [], [t_w["d"]], q="pool")
            if issue:
                do_loads()
            return (wup, wdn, cwf, cbf, t_w), do_loads

        def ffn_prefetch(l):
            stw = contextlib.ExitStack()
            stw.__enter__()
            tiles, do_loads = ffn_weights(l, stw, issue=False)
            ffn_pref[l] = (stw, tiles)
            return do_loads

        def ffn_phase(l, xS, tS, xD, tD, wu_in, wd_in, cw_in, cb_in):
            with contextlib.ExitStack() as st:
                if l in ffn_pref:
                    stw, (wup, wdn, cwf, cbf, t_w) = ffn_pref.pop(l)
                    st.enter_context(stw)
                else:
                    (wup, wdn, cwf, cbf, t_w), _ = ffn_weights(l, st)
                WM = 260
                xws = [sb("f_xw%d" % i, [128, KT, WM], st=st) for i in range(2)]; t_xws = [Tok(), Tok()]
                sq = sb("f_sq", [128, KT, WM], BF16, st=st)
                tmp = sb("f_tmp", [128, KT, WM], st=st)
                rs = sb("f_rs", [128, WM], st=st); t_tmp = Tok()
                hTs = [sb("f_hT%d" % i, [128, KT, WM], BF16, st=st) for i in range(2)]; t_hTs = [Tok(), Tok()]
                aT = sb("f_aT", [128, FT, WM], BF16, st=st); t_aT = Tok()
                gc = [sb("f_gc%d" % i, [128, WM], st=st) for i in range(2)]
                sg = [sb("f_sg%d" % i, [128, WM], st=st) for i in range(2)]
                t_gc = [Tok(), Tok()]
                wins = []
                for b in range(NB):
                    segs = ([(0, LC, 2, 1)] if l == 0 else []) + [(LC, LL, b, 8)]
                    for (base, L, v, nw) in segs:
                        for (i0, i1, o0, o1) in seg_windows(L, nw):
                            wins.append((b, base, v, i0, i1, o0, o1))

                def prep(wi):
                    b, base, v, i0, i1, o0, o1 = wins[wi]
                    W = i1 - i0
                    xw, t_xw, hT, t_hT = xws[wi % 2], t_xws[wi % 2], hTs[wi % 2], t_hTs[wi % 2]
                    k.load(xw[:, :, 0:W], xS[b, :, :, base + i0:base + i1], toks_for(tS, b, base + i0, base + i1), [t_xw])
                    normmod((sq, tmp, rs, t_tmp), xw, t_xw, W, gs2[l], 24, l, v, hT, t_hT)

                n = 0
                prep(0)
                for wi, (b, base, v, i0, i1, o0, o1) in enumerate(wins):
                    W = i1 - i0
                    xw, t_xw, hT, t_hT = xws[wi % 2], t_xws[wi % 2], hTs[wi % 2], t_hTs[wi % 2]
                    for j in range(FT):
                        if j == 8 and wi + 1 < len(wins):
                            prep(wi + 1)
                        pg, pu = nb(), nb()
                        for kt in range(KT):
                            k.mm(ps[pg][:, 0:W], wup[:, kt, j * 128:(j + 1) * 128], hT[:, kt, 0:W], kt == 0, kt == KT - 1, [t_w["g"][j // 2], t_hT], [t_ps[pg]])
                        for kt in range(KT):
                            k.mm(ps[pu][:, 0:W], wup[:, kt, DFF + j * 128:DFF + (j + 1) * 128], hT[:, kt, 0:W], kt == 0, kt == KT - 1, [t_w["g"][j // 2], t_hT], [t_ps[pu]])
                        g2i = n % 2; n += 1
                        k.act(gc[g2i][:, 0:W], ps[pg][:, 0:W], AF.Identity, [t_ps[pg], t_w["c"]], [t_gc[g2i]],
                              scale=cwf[:, j, 1:2], bias=cbf[:, j:j + 1])
                        k.v("dve", "scalar_tensor_tensor", [t_ps[pg], t_w["c"]], [t_gc[g2i]], out=gc[g2i][:, 1:W], in0=ps[pg][:, 0:W - 1],
                            scalar=cwf[:, j, 0:1], in1=gc[g2i][:, 1:W], op0=ALU.mult, op1=ALU.add)
                        k.v("dve", "scalar_tensor_tensor", [t_ps[pg], t_w["c"]], [t_gc[g2i]], out=gc[g2i][:, 0:W - 1], in0=ps[pg][:, 1:W],
                            scalar=cwf[:, j, 2:3], in1=gc[g2i][:, 0:W - 1], op0=ALU.mult, op1=ALU.add)
                        k.act(sg[g2i][:, 0:W], gc[g2i][:, 0:W], AF.Silu, [t_gc[g2i]], [t_gc[g2i]])
                        k.v("dve", "tensor_tensor", [t_ps[pu], t_gc[g2i]], [t_aT], out=aT[:, j, 0:W], in0=ps[pu][:, 0:W],
                            in1=sg[g2i][:, 0:W], op=ALU.mult)
                    a0, a1 = o0 - i0, o1 - i0
                    for kt in range(KT):
                        pd = nb()
                        for j in range(FT):
                            k.mm(ps[pd][:, 0:a1 - a0], wdn[:, j, kt * 128:(kt + 1) * 128], aT[:, j, a0:a1], j == 0, j == FT - 1, [t_w["d"], t_aT], [t_ps[pd]])
                        k.v("dve", "scalar_tensor_tensor", [t_ps[pd], t_mods[l]], [t_xw], out=xw[:, kt, a0:a1], in0=ps[pd][:, 0:a1 - a0],
                            scalar=mods[l][:, 40 + kt, v:v + 1], in1=xw[:, kt, a0:a1], op0=ALU.mult, op1=ALU.add)
                    k.load(xD[b, :, :, base + o0:base + o1], xw[:, :, a0:a1], [t_xw], [Tok()])
                P.barrier()

        def mla_phase(l, xS, tS, xD, tD):
            knD = k.scratch("knD", [NB, 128, 8, T], BF16)
            vD = k.scratch("vD", [NB, T, D], BF16)
            krD = k.scratch("krD", [NB, 64, T], BF16)
            qD = k.scratch("qD", [NB, 8, 128, LL], BF16)
            qrD = k.scratch("qrD", [NB, 8, 64, LL], BF16)
            t_kv = [Tok() for b in range(NB)]
            t_q = [Tok() for b in range(NB)]
            SC = 192.0 ** -0.5
            with contextlib.ExitStack() as st:
                wdn = sb("m_wdn", [128, KT, 768], BF16, st=st)
                wuq = sb("m_wuq", [128, 3, 2048], BF16, st=st)
                wukv = sb("m_wukv", [128, 2, 2048], BF16, st=st)
                qng = sb("m_qng", [128, 3], st=st)
                kvg = sb("m_kvg", [128, 2], st=st)
                rc = sb("m_rc", [64, LL], st=st)
                rsn = sb("m_rs", [64, LL], st=st)
                t_w = Tok()
                k.load(wdn[:], din["mla_w_down_ext"].rearrange("(kt p) c -> p kt c", p=128), [], [t_w], q="pool")
                k.load(wuq[:], din["mla_w_uq_ext"].rearrange("(kt p) c -> p kt c", p=128), [], [t_w], q="pool")
                k.load(wukv[:], din["mla_w_ukv"].rearrange("(kt p) c -> p kt c", p=128), [], [t_w], q="pool")
                k.load(qng[:], din["mla_q_norm_fm"][:, :], [], [t_w])
                k.load(kvg[:], din["mla_kv_norm_fm"][:, :], [], [t_w])
                k.load(rc[:], din["ropeC"][:, :], [], [t_w])
                k.load(rsn[:], din["ropeS"][:, :], [], [t_w])
                xws = [sb("m_xw%d" % i, [128, KT, 512], st=st) for i in range(2)]; t_xws = [Tok(), Tok()]
                sq = sb("m_sq", [128, KT, 512], BF16, st=st)
                tmp = sb("m_tmp", [128, KT, 512], st=st)
                rs = sb("m_rsd", [128, 512], st=st); t_tmp = Tok()
                hTs = [sb("m_hT%d" % i, [128, KT, 512], BF16, st=st) for i in range(2)]; t_hTs = [Tok(), Tok()]
                lat_a = sb("m_lata", [128, 3, 512], st=st); t_la = Tok()
                lat_n = sb("m_latn", [128, 3, 512], BF16, st=st); t_ln = Tok()
                kv_a = sb("m_kva", [128, 2, 512], st=st); t_ka = Tok()
                kv_n = sb("m_kvn", [128, 2, 512], BF16, st=st); t_kn = Tok()
                sq2 = sb("m_sq2", [128, 3, 512], BF16, st=st)
                rs2 = sb("m_rs2", [128, 512], st=st); t_s2 = Tok()
                r1 = sb("m_r1", [64, 512], st=st)
                r2 = sb("m_r2", [64, 512], st=st); t_r = Tok()
                krr = sb("m_krr", [64, 512], BF16, st=st); t_krr = Tok()
                knT = sb("m_knT", [128, 8, 512], BF16, st=st); t_knT = Tok()
                vt = [sb("m_vt%d" % i, [128, D], BF16, st=st) for i in range(2)]; t_vt = [Tok(), Tok()]
                qo = [sb("m_qo%d" % i, [128, 512], BF16, st=st) for i in range(2)]; t_qo = [Tok(), Tok()]
                qro = [sb("m_qro%d" % i, [64, 512], BF16, st=st) for i in range(2)]; t_qro = [Tok(), Tok()]

                def lownorm(src, t_src, nt, dst, t_dst, g, W):
                    k.act(sq2[:, 0:nt, 0:W], src[:, 0:nt, 0:W], AF.Square, [t_src], [t_s2])
                    pb = nb()
                    for i in range(nt):
                        k.mm(ps[pb][:, 0:W], onesb[:], sq2[:, i, 0:W], i == 0, i == nt - 1, [t_s2, t_const], [t_ps[pb]])
                    k.act(rs2[:, 0:W], ps[pb][:, 0:W], AF.Sqrt, [t_ps[pb]], [t_s2], scale=1.0 / (nt * 128), bias=EPS)
                    k.v("dve", "reciprocal", [t_s2], [t_s2], out=rs2[:, 0:W], in_=rs2[:, 0:W])
                    for i in range(nt):
                        k.v("dve", "scalar_tensor_tensor", [t_src, t_s2, t_w], [t_dst], out=dst[:, i, 0:W], in0=src[:, i, 0:W],
                            scalar=g[:, i:i + 1], in1=rs2[:, 0:W], op0=ALU.mult, op1=ALU.mult)

                def rope(pa, pb2, p0, W, dst, t_dst):
                    k.v("dve", "tensor_tensor", [t_ps[pa], t_w], [t_r], out=r1[:, 0:W], in0=ps[pa][0:64, 0:W], in1=rc[:, p0:p0 + W], op=ALU.mult)
                    k.v("dve", "tensor_tensor", [t_ps[pb2], t_w], [t_r], out=r2[:, 0:W], in0=ps[pb2][0:64, 0:W], in1=rsn[:, p0:p0 + W], op=ALU.mult)
                    k.v("dve", "tensor_tensor", [t_r], [t_dst], out=dst[:, 0:W], in0=r1[:, 0:W], in1=r2[:, 0:W], op=ALU.add)

                n = 0
                m1items = [(b, ci, c0, W) for b in range(NB) for ci, (c0, W) in enumerate(CHUNKS)]

                def m1prep(i):
                    b_, ci_, c0_, W_ = m1items[i]
                    v_ = b_ if c0_ >= LC else 2
                    k.load(xws[i % 2][:, :, 0:W_], xS[b_, :, :, c0_:c0_ + W_], [tS[b_][ci_]], [t_xws[i % 2]])
                    normmod((sq, tmp, rs, t_tmp), xws[i % 2], t_xws[i % 2], W_, gs1[l], 0, l, v_, hTs[i % 2], t_hTs[i % 2])

                m1prep(0)
                for mi, (b, ci, c0, W) in enumerate(m1items):
                    if True:
                        is_lat = c0 >= LC
                        v = b if is_lat else 2
                        hT, t_hT = hTs[mi % 2], t_hTs[mi % 2]
                        for i in range(2):
                            pb = nb()
                            for kt in range(KT):
                                k.mm(ps[pb][:, 0:W], wdn[:, kt, 384 + i * 128:384 + (i + 1) * 128], hT[:, kt, 0:W], kt == 0, kt == KT - 1, [t_w, t_hT], [t_ps[pb]])
                            k.act(kv_a[:, i, 0:W], ps[pb][:, 0:W], AF.Copy, [t_ps[pb]], [t_ka])
                        lownorm(kv_a, t_ka, 2, kv_n, t_kn, kvg, W)
                        pa = nb()
                        for kt in range(KT):
                            k.mm(ps[pa][0:64, 0:W], wdn[:, kt, 640:704], hT[:, kt, 0:W], kt == 0, kt == KT - 1, [t_w, t_hT], [t_ps[pa]])
                        if is_lat:
                            pb2 = nb()
                            for kt in range(KT):
                                k.mm(ps[pb2][0:64, 0:W], wdn[:, kt, 704:768], hT[:, kt, 0:W], kt == 0, kt == KT - 1, [t_w, t_hT], [t_ps[pb2]])
                            rope(pa, pb2, c0 - LC, W, krr, t_krr)
                        else:
                            k.act(krr[:, 0:W], ps[pa][0:64, 0:W], AF.Copy, [t_ps[pa]], [t_krr])
                        k.load(krD[b, :, c0:c0 + W], krr[:, 0:W], [t_krr], [Tok()])
                        if mi + 1 < len(m1items):
                            m1prep(mi + 1)
                        for h in range(8):
                            pb = nb()
                            for i in range(2):
                                k.mm(ps[pb][:, 0:W], wukv[:, i, h * 256:h * 256 + 128], kv_n[:, i, 0:W], i == 0, i == 1, [t_w, t_kn], [t_ps[pb]])
                            if h % 2 == 0:
                                k.act(knT[:, h, 0:W], ps[pb][:, 0:W], AF.Copy, [t_ps[pb]], [t_knT])
                            else:
                                k.v("dve", "tensor_copy", [t_ps[pb]], [t_knT], out=knT[:, h, 0:W], in_=ps[pb][:, 0:W])
                        k.load(knD[b, :, :, c0:c0 + W], knT[:, :, 0:W], [t_knT], [Tok()])
                        wv = wukv[:].rearrange("p i (h c) -> p i h c", c=256)
                        for tt in range(W // 128):
                            vi = n % 2; n += 1
                            for half in range(2):
                                pb = nb()
                                for i in range(2):
                                    k.mm(ps[pb][:].rearrange("p (h c) -> p h c", c=128), kv_n[:, i, tt * 128:(tt + 1) * 128],
                                         wv[:, i, half * 4:half * 4 + 4, 128:256], i == 0, i == 1, [t_w, t_kn], [t_ps[pb]])
                                if half == 0:
                                    k.act(vt[vi][:, 0:512], ps[pb][:], AF.Copy, [t_ps[pb]], [t_vt[vi]])
                                else:
                                    k.v("dve", "tensor_copy", [t_ps[pb]], [t_vt[vi]], out=vt[vi][:, 512:1024], in_=ps[pb][:])
                            k.load(vD[b, c0 + tt * 128:c0 + (tt + 1) * 128, :], vt[vi][:], [t_vt[vi]], [Tok()], q="act")
                        if not is_lat:
                            continue
                        for i in range(3):
                            pb = nb()
                            for kt in range(KT):
                                k.mm(ps[pb][:, 0:W], wdn[:, kt, i * 128:(i + 1) * 128], hT[:, kt, 0:W], kt == 0, kt == KT - 1, [t_w, t_hT], [t_ps[pb]])
                            k.act(lat_a[:, i, 0:W], ps[pb][:, 0:W], AF.Copy, [t_ps[pb]], [t_la])
                        lownorm(lat_a, t_la, 3, lat_n, t_ln, qng, W)
                        for h in range(8):
                            qi = h % 2
                            pb = nb()
                            for i in range(3):
                                k.mm(ps[pb][:, 0:W], wuq[:, i, h * 256:h * 256 + 128], lat_n[:, i, 0:W], i == 0, i == 2, [t_w, t_ln], [t_ps[pb]])
                            k.act(qo[qi][:, 0:W], ps[pb][:, 0:W], AF.Copy, [t_ps[pb]], [t_qo[qi]])
                            k.load(qD[b, h, :, c0 - LC:c0 - LC + W], qo[qi][:, 0:W], [t_qo[qi]], [Tok()])
                            pa, pb2 = nb(), nb()
                            for i in range(3):
                                k.mm(ps[pa][0:64, 0:W], wuq[:, i, h * 256 + 128:h * 256 + 192], lat_n[:, i, 0:W], i == 0, i == 2, [t_w, t_ln], [t_ps[pa]])
                            for i in range(3):
                                k.mm(ps[pb2][0:64, 0:W], wuq[:, i, h * 256 + 192:h * 256 + 256], lat_n[:, i, 0:W], i == 0, i == 2, [t_w, t_ln], [t_ps[pb2]])
                            rope(pa, pb2, c0 - LC, W, qro[qi], t_qro[qi])
                            k.load(qrD[b, h, :, c0 - LC:c0 - LC + W], qro[qi][:, 0:W], [t_qro[qi]], [Tok()])
                P.barrier()
            with contextlib.ExitStack() as st:
                wo = sb("a_wo", [128, 8, D], BF16, st=st); t_w = Tok()
                k.load(wo[:], din["mla_w_o"].rearrange("(h p) c -> p h c", p=128), [], [t_w], q="pool")
                knT = sb("a_knT", [128, 8, T], BF16, st=st)
                vv = sb("a_v", [128, 18, D], BF16, st=st)
                krr = sb("a_krr", [64, T], BF16, st=st); t_kvs = Tok()
                qt = [sb("a_q%d" % i, [128, 512], BF16, st=st) for i in range(2)]
                qrt = [sb("a_qr%d" % i, [64, 512], BF16, st=st) for i in range(2)]; t_qt = [Tok(), Tok()]
                Pm = [sb("a_P%d" % i, [128, 18, 512], BF16, st=st) for i in range(2)]; t_P = [Tok(), Tok()]
                rden = sb("a_rden", [128, 512], st=st); t_rd = Tok()
                accP = [sb("a_acc%d" % i, [128, 512], st=st) for i in range(2)]; t_acc = [Tok(), Tok()]
                attnT = sb("a_attn", [128, 8, 512], BF16, st=st); t_at = Tok()
                xw = sb("a_xw", [128, KT, 512], st=st); t_xw = Tok()
                t_Pj = [[Tok() for j in range(18)] for i in range(2)]
                accB = [sb("a_accB%d" % i, [128, 512], st=st) for i in range(2)]; t_accB = [Tok(), Tok()]
                attn2 = [attnT, sb("a_attn2", [128, 8, 512], BF16, st=st)]; t_at2 = [t_at, Tok()]
                items = [(b, qc, h) for b in range(NB) for qc in range(4) for h in range(8)]

                def stage1(idx):
                    b, qc, h = items[idx]
                    i2 = idx % 2
                    q0 = qc * 512
                    k.load(qt[i2][:], qD[b, h, :, q0:q0 + 512], [], [t_qt[i2]])
                    k.load(qrt[i2][:], qrD[b, h, :, q0:q0 + 512], [], [t_qt[i2]], q="act")
                    for j in range(18):
                        pS = nb()
                        k.mm(ps[pS][:], knT[:, h, j * 128:(j + 1) * 128], qt[i2][:], True, False, [t_kvs, t_qt[i2]], [t_ps[pS]])
                        k.mm(ps[pS][:], krr[:, j * 128:(j + 1) * 128], qrt[i2][:], False, True, [t_kvs, t_qt[i2]], [t_ps[pS]])
                        k.act(Pm[i2][:, j, :], ps[pS][:], AF.Exp, [t_ps[pS]], [t_Pj[i2][j]], scale=SC)
                        eng, acc, t_a = ("pool", accP[i2], t_acc[i2]) if j % 2 == 0 else ("dve", accB[i2], t_accB[i2])
                        if j < 2:
                            k.v(eng, "tensor_copy", [t_Pj[i2][j]], [t_a], out=acc[:], in_=Pm[i2][:, j, :])
                        else:
                            k.v(eng, "tensor_tensor", [t_Pj[i2][j]], [t_a], out=acc[:], in0=acc[:], in1=Pm[i2][:, j, :], op=ALU.add)

                def stage2(idx):
                    b, qc, h = items[idx]
                    i2 = idx % 2
                    a2 = qc % 2
                    pO, pDn = nb(), nb()
                    for j in range(18):
                        k.mm(ps[pO][:], vv[:, j, h * 128:(h + 1) * 128], Pm[i2][:, j, :], j == 0, j == 17, [t_kvs, t_Pj[i2][j]], [t_ps[pO]])
                    k.mm(ps[pDn][:], onesf[:], accP[i2][:], True, False, [t_const, t_acc[i2]], [t_ps[pDn]])
                    k.mm(ps[pDn][:], onesf[:], accB[i2][:], False, True, [t_const, t_accB[i2]], [t_ps[pDn]])
                    k.v("dve", "reciprocal", [t_ps[pDn]], [t_rd], out=rden[:], in_=ps[pDn][:])
                    k.v("dve", "tensor_tensor", [t_ps[pO], t_rd], [t_at2[a2]], out=attn2[a2][:, h, :], in0=ps[pO][:], in1=rden[:], op=ALU.mult)
                    if h == 7:
                        q0 = qc * 512
                        c0 = LC + q0
                        ci = chunk_of(c0)
                        k.load(xw[:], xS[b, :, :, c0:c0 + 512], [tS[b][ci]], [t_xw])
                        for kt in range(KT):
                            pY = nb()
                            for hh in range(8):
                                k.mm(ps[pY][:], wo[:, hh, kt * 128:(kt + 1) * 128], attn2[a2][:, hh, :], hh == 0, hh == 7, [t_w, t_at2[a2]], [t_ps[pY]])
                            k.v("dve", "scalar_tensor_tensor", [t_ps[pY], t_mods[l]], [t_xw], out=xw[:, kt, :], in0=ps[pY][:],
                                scalar=mods[l][:, 16 + kt, b:b + 1], in1=xw[:, kt, :], op0=ALU.mult, op1=ALU.add)
                        k.load(xD[b, :, :, c0:c0 + 512], xw[:], [t_xw], [tD[b][ci]])

                for idx, (b, qc, h) in enumerate(items):
                    if qc == 0 and h == 0:
                        k.load(knT[:], knD[b], [], [t_kvs])
                        k.load(vv[:], vD[b].rearrange("(j p) c -> p j c", p=128), [], [t_kvs], q="act")
                        k.load(krr[:], krD[b], [], [t_kvs])
                        stage1(idx)
                    if idx + 1 < len(items) and items[idx + 1][0] == b:
                        stage1(idx + 1)
                    stage2(idx)
                P.barrier()

        def colof(t):
            return t + 1 if t < LC else t + 2

        def mix0_phase(l, xS, tS, xD, tD):
            hx0 = k.scratch("hx0", [NB, 4, 128, T])
            hg = k.scratch("hg", [NB, 4, 128, T])
            hgT = k.scratch("hgT", [NB, T, 512], BF16)
            mqT = k.scratch("mqT", [NB, 4, 128, T], BF16)
            mkT = k.scratch("mkT", [NB, 4, 128, T], BF16)
            mvT = k.scratch("mvT", [NB, T, 512], BF16)
            mog = k.scratch("mog", [NB, 4, 128, T])
            mgr = k.scratch("mgr", [NB, 4, 4, T])
            mixD = k.scratch("mixD", [NB, 8, 128, T], BF16)
            t_pr = [Tok() for b in range(NB)]
            t_mixh = [Tok() for b in range(NB)]
            t_mixm = [Tok() for b in range(NB)]
            WP = T + 3
            with contextlib.ExitStack() as st:
                win = sb("p_win", [128, KT, 3600], BF16, st=st); t_w = Tok()
                for kt in range(KT):
                    k.load(win[:, kt, :], din["w_in_0"][kt * 128:(kt + 1) * 128, :], [], [t_w], q="pool")
                hcw = sb("p_hcw", [128, 12, 3], st=st); hcb = sb("p_hcb", [128, 12], st=st)
                mcw = sb("p_mcw", [128, 8, 3], st=st); mgb = sb("p_mgb", [4, 4], st=st); mng = sb("p_mng", [128, 4], st=st)
                k.load(hcw[:], din["hy_conv_w_fm"][:, :, :], [], [t_w]); k.load(hcb[:], din["hy_conv_b_fm"][:, :], [], [t_w])
                k.load(mcw[:], din["ml_conv_w_fm"][:, :, :], [], [t_w]); k.load(mgb[:], din["ml_gate_b_fm"][:, :], [], [t_w])
                k.load(mng[:], din["ml_norm_g_fm"][:, :], [], [t_w])
                hT = sb("p_hT", [128, KT, T], BF16, st=st); t_hT = Tok()
                xw = sb("p_xw", [128, KT, 256], st=st); t_xw = Tok()
                sq = sb("p_sq", [128, KT, 256], BF16, st=st)
                tmp = sb("p_tmp", [128, KT, 256], st=st)
                rs = sb("p_rs", [128, 256], st=st); t_tmp = Tok()
                ub = [sb("p_ub%d" % i, [128, WP], st=st) for i in range(3)]; t_ub = [Tok() for i in range(3)]
                cv = [sb("p_cv%d" % i, [128, WP], st=st) for i in range(3)]; t_cv = [Tok() for i in range(3)]
                ob = [sb("p_ob%d" % i, [128, WP], BF16, st=st) for i in range(1)] * 2; t_ob = [Tok()] * 2
                gT = sb("p_gT", [128, 18, 128], BF16, st=st); t_gT = Tok()
                vt = [sb("p_vt%d" % i, [128, 512], BF16, st=st) for i in range(2)]; t_vt = [Tok(), Tok()]
                grow = sb("p_grow", [4, 1, T], st=st); t_grow = Tok()
                for i in range(3):
                    k.v("pool", "memset", [], [t_ub[i]], ub[i][:], 0.0)
                    k.v("pool", "memset", [], [t_cv[i]], cv[i][:], 0.0)
                evn = [0]

                def proj_fm(col0, M, dst_fn, t_dst):
                    for (c0, W) in CHUNKS:
                        pb = nb()
                        for kt in range(KT):
                            k.mm(ps[pb][0:M, 0:W], win[:, kt, col0:col0 + M], hT[:, kt, c0:c0 + W], kt == 0, kt == KT - 1, [t_w, t_hT], [t_ps[pb]])
                        evn[0] += 1
                        if evn[0] % 2:
                            k.act(dst_fn(c0, W), ps[pb][0:M, 0:W], AF.Copy, [t_ps[pb]], [t_dst])
                        else:
                            k.v("dve", "tensor_copy", [t_ps[pb]], [t_dst], out=dst_fn(c0, W), in_=ps[pb][0:M, 0:W])

                def conv3(i, wt, widx, bias):
                    u_, c_ = ub[i], cv[i]
                    if bias is None:
                        k.act(c_[:, 1:WP - 1], u_[:, 1:WP - 1], AF.Identity, [t_ub[i], t_w], [t_cv[i]], scale=wt[:, widx, 1:2])
                    else:
                        k.act(c_[:, 1:WP - 1], u_[:, 1:WP - 1], AF.Identity, [t_ub[i], t_w], [t_cv[i]], scale=wt[:, widx, 1:2], bias=bias)
                    k.v("dve", "scalar_tensor_tensor", [t_ub[i], t_w], [t_cv[i]], out=c_[:, 1:WP - 1], in0=u_[:, 0:WP - 2],
                        scalar=wt[:, widx, 0:1], in1=c_[:, 1:WP - 1], op0=ALU.mult, op1=ALU.add)
                    k.v("dve", "scalar_tensor_tensor", [t_ub[i], t_w], [t_cv[i]], out=c_[:, 1:WP - 1], in0=u_[:, 2:WP],
                        scalar=wt[:, widx, 2:3], in1=c_[:, 1:WP - 1], op0=ALU.mult, op1=ALU.add)

                def store_fm(dst, src, t_src, t_dst, q="sp"):
                    k.load(dst[:, 0:LC], src[:, 1:1 + LC], [t_src], [t_dst], q=q)
                    k.load(dst[:, LC:T], src[:, LC + 2:LC + 2 + LL], [t_src], [t_dst], q=q)

                n = 0
                for b in range(NB):
                    for c0 in range(0, T, 256):
                        W = 256
                        v = b if c0 >= LC else 2
                        k.load(xw[:, :, 0:W], xS[b, :, :, c0:c0 + W], [tS[b][chunk_of(c0)]], [t_xw])
                        normmod((sq, tmp, rs, t_tmp), xw, t_xw, W, gs1[l], 0, l, v, hT[:, :, c0:c0 + W], t_hT)
                    for j in range(4):
                        for part in range(3):
                            m = part * 4 + j
                            proj_fm(m * 128, 128, lambda c0, W, u_=ub[part]: u_[:, colof(c0):colof(c0) + W], t_ub[part])
                            conv3(part, hcw, m, hcb[:, m:m + 1])
                        store_fm(hx0[b, j], cv[0], t_cv[0], Tok())
                        k.v("dve", "tensor_tensor", [t_cv[1], t_cv[2]], [t_cv[2]], out=cv[2][:], in0=cv[2][:], in1=cv[1][:], op=ALU.mult)
                        store_fm(hg[b, j], cv[2], t_cv[2], Tok(), q="act")
                        for g0 in range(0, 18, 4):
                            ng = min(4, 18 - g0)
                            pb = nb()
                            for i in range(ng):
                                cc = colof((g0 + i) * 128)
                                k.tr(ps[pb][:, i * 128:(i + 1) * 128], cv[2][:, cc:cc + 128], ident[:], [t_cv[2], t_const], [t_ps[pb]])
                            k.act(gT[:, g0:g0 + ng, :], ps[pb][:, 0:ng * 128].rearrange("p (a c) -> p a c", c=128),
                                  AF.Copy, [t_ps[pb]], [t_gT])
                        k.load(hgT[b, :, j * 128:(j + 1) * 128].rearrange("(tt p) c -> p tt c", p=128), gT[:], [t_gT], [Tok()])
                    for h in range(4):
                        for (qk, mt, dstD) in ((0, 12 + h, mqT), (1, 16 + h, mkT)):
                            proj_fm(mt * 128, 128, lambda c0, W, u_=ub[qk]: u_[:, colof(c0):colof(c0) + W], t_ub[qk])
                            conv3(qk, mcw, qk * 4 + h, None)
                            oi = n % 2; n += 1
                            if qk == 0:
                                k.act(ob[oi][:, 1:WP - 1], cv[qk][:, 1:WP - 1], AF.Silu, [t_cv[qk]], [t_ob[oi]])
                            else:
                                k.act(cv[qk][:, 1:WP - 1], cv[qk][:, 1:WP - 1], AF.Silu, [t_cv[qk]], [t_cv[qk]])
                                k.v("dve", "tensor_scalar", [t_cv[qk]], [t_ob[oi]], out=ob[oi][:, 1:WP - 1], in0=cv[qk][:, 1:WP - 1],
                                    scalar1=128.0 ** -0.5, scalar2=None, op0=ALU.mult)
                            store_fm(dstD[b, h], ob[oi], t_ob[oi], Tok())
                        proj_fm((24 + h) * 128, 128, lambda c0, W: ub[2][:, c0:c0 + W], t_ub[2])
                        k.act(cv[2][:, 0:T], ub[2][:, 0:T], AF.Sigmoid, [t_ub[2]], [t_cv[2]])
                        k.v("dve", "tensor_scalar", [t_cv[2], t_w], [t_cv[2]], out=cv[2][:, 0:T], in0=cv[2][:, 0:T],
                            scalar1=mng[:, h:h + 1], scalar2=None, op0=ALU.mult)
                        k.load(mog[b, h], cv[2][:, 0:T], [t_cv[2]], [Tok()])
                    k.v("pool", "memset", [t_ub[2]], [t_ub[2]], ub[2][:], 0.0)
                    k.v("pool", "memset", [t_cv[2]], [t_cv[2]], cv[2][:], 0.0)
                    for tt in range(18):
                        vi = tt % 2
                        pb = nb()
                        for kt in range(KT):
                            k.mm(ps[pb][:], hT[:, kt, tt * 128:(tt + 1) * 128], win[:, kt, 2560:3072], kt == 0, kt == KT - 1, [t_w, t_hT], [t_ps[pb]])
                        k.act(vt[vi][:], ps[pb][:], AF.Copy, [t_ps[pb]], [t_vt[vi]])
                        k.load(mvT[b, tt * 128:(tt + 1) * 128, :], vt[vi][:], [t_vt[vi]], [Tok()], q="act")
                    for kind in range(4):
                        for (c0, W) in CHUNKS:
                            pb = nb()
                            for kt in range(KT):
                                k.mm(ps[pb][0:4, 0:W], win[:, kt, 3584 + kind * 4:3588 + kind * 4], hT[:, kt, c0:c0 + W], kt == 0, kt == KT - 1, [t_w, t_hT], [t_ps[pb]])
                            k.act(grow[:, 0, c0:c0 + W], ps[pb][0:4, 0:W], AF.Identity, [t_ps[pb], t_w], [t_grow], bias=mgb[:, kind:kind + 1])
                        if kind % 2 == 1:
                            k.act(grow[:, 0, :], grow[:, 0, :], AF.Exp, [t_grow], [t_grow], scale=-1.0)
                            k.act(grow[:, 0, :], grow[:, 0, :], AF.Ln, [t_grow], [t_grow], bias=1.0)
                            k.v("dve", "tensor_scalar", [t_grow], [t_grow], out=grow[:, 0, :], in0=grow[:, 0, :], scalar1=-1.0, scalar2=None, op0=ALU.mult)
                        k.load(mgr[b, kind], grow[:, 0, :], [t_grow], [Tok()])
                P.barrier()
            if "mixA" in k.stop:
                return

            with contextlib.ExitStack() as st:
                maskF = sb("l_mF", [128, 4, 512], st=st); maskB = sb("l_mB", [128, 4, 512], st=st)
                sel = sb("l_sel", [4, 4, 128], st=st); t_c = Tok()
                k.load(maskF[:], din["maskF"][:, :, :], [], [t_c]); k.load(maskB[:], din["maskB"][:, :, :], [], [t_c], q="act")
                k.load(sel[:], din["sel4"][:, :, :], [], [t_c])
                arow = [sb("l_arow%d" % d_, [4, T], st=st) for d_ in range(2)]
                t_rows = Tok()
                colsb = sb("l_cols", [128, 18, 24], st=st); emn = sb("l_emn", [128, 18, 8], st=st); t_cols = Tok()
                n = 0
                for b in range(NB):
                    with contextlib.ExitStack() as st1:
                        gr = sb("l_gr", [4, 4, T], st=st1); t_gr = Tok()
                        zr = sb("l_zr", [4, T], st=st1)
                        k.v("pool", "memset", [], [t_c], zr[:], 0.0)
                        rows = [[arow[d_]] + [sb("l_row%d%d" % (d_, q_), [4, T], st=st1) for q_ in range(1, 3)] for d_ in range(2)]
                        mrow = [sb("l_m%d" % d_, [4, T], st=st1) for d_ in range(2)]
                        brow = [sb("l_b%d" % d_, [4, T], st=st1) for d_ in range(2)]
                        k.load(gr[:], mgr[b].rearrange("k h t -> h k t"), [], [t_gr])
                        S = "tensor_tensor_scan"
                        k.v("dve", S, [t_gr], [t_rows], out=mrow[0][:, :], data0=gr[:, 1, :], data1=gr[:, 0, :], initial=0.0, op0=ALU.add, op1=ALU.max)
                        k.v("dve", S, [t_gr, t_c], [t_rows], out=brow[0][:, :], data0=gr[:, 1, :], data1=zr[:, :], initial=0.0, op0=ALU.add, op1=ALU.add)
                        rv = lambda ap_: ap_[:, ::-1]
                        k.v("dve", S, [t_gr], [t_rows], out=rv(mrow[1][:, 0:LC]), data0=rv(gr[:, 3, 0:LC]), data1=rv(gr[:, 2, 0:LC]), initial=0.0, op0=ALU.add, op1=ALU.max)
                        k.v("dve", S, [t_gr, t_c], [t_rows], out=rv(brow[1][:, 0:LC]), data0=rv(gr[:, 3, 0:LC]), data1=rv(zr[:, 0:LC]), initial=0.0, op0=ALU.add, op1=ALU.add)
                        k.v("dve", S, [t_gr, t_rows], [t_rows], out=rv(mrow[1][:, LC:T]), data0=rv(gr[:, 3, LC:T]), data1=rv(gr[:, 2, LC:T]), initial=mrow[1][:, 0:1], op0=ALU.add, op1=ALU.max)
                        k.v("dve", S, [t_gr, t_c, t_rows], [t_rows], out=rv(brow[1][:, LC:T]), data0=rv(gr[:, 3, LC:T]), data1=rv(zr[:, LC:T]), initial=brow[1][:, 0:1], op0=ALU.add, op1=ALU.add)
                        for d_ in range(2):
                            k.v("dve", "tensor_tensor", [t_rows], [t_rows], out=rows[d_][0][:, :], in0=brow[d_][:, :], in1=mrow[d_][:, :], op=ALU.subtract)
                            k.v("dve", "tensor_tensor", [t_rows, t_gr], [t_rows], out=rows[d_][1][:, :], in0=gr[:, 2 * d_, :], in1=brow[d_][:, :], op=ALU.subtract)
                            k.v("dve", "tensor_scalar", [t_rows], [t_rows], out=rows[d_][2][:, :], in0=mrow[d_][:, :], scalar1=-1.0, scalar2=None, op0=ALU.mult)
                        pc = nb()
                        for tt in range(18):
                            for d_ in range(2):
                                for q_ in range(3):
                                    o0 = tt * 24 + d_ * 12 + q_ * 4
                                    k.mm(ps[pc][:, o0:o0 + 4], rows[d_][q_][0:4, tt * 128:(tt + 1) * 128], ident[0:4, 0:4], True, True, [t_rows, t_const], [t_ps[pc]])
                        k.v("dve", "tensor_copy", [t_ps[pc]], [t_cols], out=colsb[:], in_=ps[pc][:, 0:432].rearrange("p (a c) -> p a c", c=24))
                        for d_ in range(2):
                            k.act(emn[:, :, d_ * 4:(d_ + 1) * 4], colsb[:, :, d_ * 12 + 8:d_ * 12 + 12], AF.Exp, [t_cols], [t_cols])
                    P.barrier()
                    with contextlib.ExitStack() as st2:
                        kT = sb("l_kT", [128, 4, T], BF16, st=st2); qT = sb("l_qT", [128, 4, T], BF16, st=st2); t_qk = Tok()
                        vq = sb("l_vq", [128, 18, 4, 130], BF16, st=st2); t_vq = Tok()
                        hsum = sb("l_hsum", [128, 18, 512], st=st2); t_hs = Tok()
                        Pb = [sb("l_P%d" % i, [128, 18, 512], BF16, st=st2) for i in range(2)]; t_P = [Tok(), Tok()]
                        Wt = [sb("l_W%d" % i, [128, 512], st=st2) for i in range(2)]; t_W = [Tok(), Tok()]
                        Abc = [sb("l_Abc%d" % i, [128, 512], st=st2) for i in range(2)]; t_Abc = [Tok(), Tok()]
                        Am = [sb("l_Am%d" % i, [128, 512], st=st2) for i in range(2)]; t_Am = [Tok(), Tok()]
                        fin = sb("l_fin", [128, 8], st=st2); t_fin = Tok()
                        hn = sb("l_hn", [128, 512], st=st2); junk = sb("l_junk", [128, 128], st=st2); ss = sb("l_ss", [128, 8], st=st2); t_hn = Tok()
                        ogt = [sb("l_og%d" % i, [128, 4, 128], st=st2) for i in range(2)]; t_og = [Tok(), Tok()]
                        mo = [sb("l_mo%d" % i, [128, 4, 128], BF16, st=st2) for i in range(2)]; t_mo = [Tok(), Tok()]
                        k.load(kT[:], mkT[b].rearrange("h p t -> p h t"), [], [t_qk])
                        k.load(qT[:], mqT[b].rearrange("h p t -> p h t"), [], [t_qk], q="act")
                        k.v("pool", "memset", [], [t_vq], vq[:, :, :, 128:130], 1.0)
                        for tt in range(18):
                            k.load(vq[:, tt, :, 0:128], mvT[b, tt * 128:(tt + 1) * 128, :].rearrange("p (h d) -> p h d", d=128), [], [t_vq],
                                   q="sp" if tt % 2 else "act")
                        mitems = [(d_, h, c0, W) for d_ in range(2) for h in range(4) for (c0, W) in CHUNKS]

                        def mjl(d_, c0, W):
                            j0 = c0 // 128
                            nj = W // 128
                            if d_ == 0:
                                return [(j, j - j0 if j >= j0 else None) for j in range(0, j0 + nj)]
                            if c0 < LC:
                                return [(j, j - j0) for j in range(j0, j0 + nj)]
                            return [(0, None), (1, None)] + [(j, j - j0 if j < j0 + nj else None) for j in range(j0, 18)]

                        def mstageA(idx):
                            d_, h, c0, W = mitems[idx]
                            mk = maskF if d_ == 0 else maskB
                            jl = mjl(d_, c0, W)
                            pi = idx % 2
                            pA = nb()
                            k.mm(ps[pA][:, 0:W], sel[:, h, :], arow[d_][0:4, c0:c0 + W], True, True, [t_c, t_rows], [t_ps[pA]])
                            k.act(Abc[pi][:, 0:W], ps[pA][:, 0:W], AF.Copy, [t_ps[pA]], [t_Abc[pi]])
                            for ji, (j, jj) in enumerate(jl):
                                pS = nb()
                                k.mm(ps[pS][:, 0:W], kT[:, h, j * 128:(j + 1) * 128], qT[:, h, c0:c0 + W], True, True, [t_qk], [t_ps[pS]])
                                wi = ji % 2
                                bcol = colsb[:, j, d_ * 12 + 4 + h:d_ * 12 + 5 + h]
                                if jj is None:
                                    k.act(Wt[wi][:, 0:W], Abc[pi][:, 0:W], AF.Exp, [t_Abc[pi], t_cols], [t_W[wi]], bias=bcol)
                                else:
                                    k.v("pool", "tensor_tensor", [t_Abc[pi], t_c], [t_Am[wi]], out=Am[wi][:, 0:W], in0=Abc[pi][:, 0:W], in1=mk[:, jj, 0:W], op=ALU.add)
                                    k.act(Wt[wi][:, 0:W], Am[wi][:, 0:W], AF.Exp, [t_Am[wi], t_cols], [t_W[wi]], bias=bcol)
                                k.v("dve", "tensor_tensor", [t_ps[pS], t_W[wi]], [t_P[pi]], out=Pb[pi][:, ji, 0:W], in0=ps[pS][:, 0:W], in1=Wt[wi][:, 0:W], op=ALU.mult)

                        def mstageB(idx):
                            d_, h, c0, W = mitems[idx]
                            jl = mjl(d_, c0, W)
                            pi = idx % 2
                            j0 = c0 // 128
                            for ts_ in range(W // 128):
                                tt = j0 + ts_
                                pa = nb()
                                for ji, (j, jj) in enumerate(jl):
                                    k.mm(ps[pa][:, 0:130], Pb[pi][:, ji, ts_ * 128:(ts_ + 1) * 128], vq[:, j, h, :], ji == 0, ji == len(jl) - 1, [t_P[pi], t_vq], [t_ps[pa]])
                                k.act(fin[:, 3:4], ps[pa][:, 128:129], AF.Copy, [t_ps[pa]], [t_fin])
                                k.v("dve", "scalar_tensor_tensor", [t_fin], [t_fin], out=fin[:, 0:1], in0=fin[:, 3:4], scalar=-1.0, in1=fin[:, 3:4], op0=ALU.mult, op1=ALU.max)
                                k.v("dve", "tensor_tensor", [t_fin, t_cols], [t_fin], out=fin[:, 1:2], in0=fin[:, 0:1], in1=emn[:, tt, d_ * 4 + h:d_ * 4 + h + 1], op=ALU.max)
                                k.v("dve", "reciprocal", [t_fin], [t_fin], out=fin[:, 2:3], in_=fin[:, 1:2])
                                hs = hsum[:, tt, h * 128:(h + 1) * 128]
                                if d_ == 0:
                                    k.v("dve", "tensor_scalar", [t_ps[pa], t_fin], [t_hs], out=hs, in0=ps[pa][:, 0:128], scalar1=fin[:, 2:3], scalar2=None, op0=ALU.mult)
                                else:
                                    k.v("dve", "scalar_tensor_tensor", [t_ps[pa], t_fin], [t_hs], out=hs, in0=ps[pa][:, 0:128], scalar=fin[:, 2:3], in1=hs, op0=ALU.mult, op1=ALU.add)

                        mstageA(0)
                        for idx in range(len(mitems)):
                            if idx + 1 < len(mitems):
                                mstageA(idx + 1)
                            mstageB(idx)
                        for tt in range(18):
                            i2 = tt % 2
                            k.load(ogt[i2][:], mog[b, :, :, tt * 128:(tt + 1) * 128].rearrange("h p t -> p h t"), [], [t_og[i2]])
                            for h in range(4):
                                k.act(junk[:], hsum[:, tt, h * 128:(h + 1) * 128], AF.Square, [t_hs], [t_hn], accum_out=ss[:, h:h + 1])
                            k.act(ss[:, 4:8], ss[:, 0:4], AF.Sqrt, [t_hn], [t_hn], scale=1.0 / 128, bias=EPS)
                            k.v("dve", "reciprocal", [t_hn], [t_hn], out=ss[:, 4:8], in_=ss[:, 4:8])
                            pb = nb()
                            for h in range(4):
                                k.v("dve", "tensor_scalar", [t_hs, t_hn], [t_hn], out=hn[:, h * 128:(h + 1) * 128], in0=hsum[:, tt, h * 128:(h + 1) * 128],
                                    scalar1=ss[:, 4 + h:5 + h], scalar2=None, op0=ALU.mult)
                            for h in range(4):
                                k.tr(ps[pb][:, h * 128:(h + 1) * 128], hn[:, h * 128:(h + 1) * 128], ident[:], [t_hn, t_const], [t_ps[pb]])
                            k.v("dve", "tensor_tensor", [t_ps[pb], t_og[i2]], [t_mo[i2]], out=mo[i2][:], in0=ps[pb][:].rearrange("p (h t) -> p h t", t=128),
                                in1=ogt[i2][:], op=ALU.mult)
                            k.load(mixD[b, 4:8, :, tt * 128:(tt + 1) * 128].rearrange("r p t -> p r t"), mo[i2][:], [t_mo[i2]], [Tok()], q="act")
                    P.barrier()
            if "mixB" in k.stop:
                return

            PI = math.pi

            def hyena_seg(L, tok0, sfx):
                NT = L // 128
                NN = 2 * L
                TC = min(512, L)
                with contextlib.ExitStack() as st:
                    FS = sb("y_FS", [128, NT, 512], BF16, st=st); FD = sb("y_FD", [128, NT, 512], BF16, st=st); t_F = Tok()
                    Z1 = [sb("y_Z1%d" % b, [128, NT, 512], BF16, st=st) for b in range(NB)]
                    Z2 = [sb("y_Z2%d" % b, [128, NT, 512], BF16, st=st) for b in range(NB)]; t_Z = Tok()
                    with contextlib.ExitStack() as st1:
                        zT = sb("y_zT", [33, L], st=st1); fw1 = sb("y_fw1", [33, 64], st=st1); fw2 = sb("y_fw2", [64, 64], st=st1)
                        fw3 = sb("y_fw3", [64, 1024], st=st1); fv = sb("y_fv", [64, 8], st=st1); t_c = Tok()
                        adl = sb("y_adl", [128, 1024], st=st1); negt = sb("y_negt", [128, NT], st=st1)
                        k.load(zT[:], din["hy_z" + sfx][:, :], [], [t_c]); k.load(fw1[:], din["hy_fw1"][:, :], [], [t_c])
                        k.load(fw2[:], din["hy_fw2"][:, :], [], [t_c]); k.load(fw3[:], din["hy_fw3"][:, :], [], [t_c])
                        k.load(fv[:, 0:3], din["hy_fvec"][:, :], [], [t_c])
                        k.load(adl[:], din["hy_absdelta"][:, :], [], [t_c]); k.load(negt[:], din["hy_negt" + sfx][:, :], [], [t_c])
                        k.v("dve", "tensor_tensor", [t_c], [t_c], out=fv[:, 3:4], in0=fv[:, 0:1], in1=fv[:, 1:2], op=ALU.mult)
                        k.v("dve", "tensor_tensor", [t_c], [t_c], out=fv[:, 4:5], in0=fv[:, 0:1], in1=fv[:, 2:3], op=ALU.mult)
                        h1 = sb("y_h1", [64, L], st=st1); h2 = sb("y_h2", [64, L], st=st1); t_h = Tok()
                        arg = sb("y_arg", [64, 512], st=st1); kk = sb("y_kk", [64, 512], st=st1); t_arg = Tok()
                        MAGIC = 12582912.0
                        for (w_, src, dst, bcol) in ((fw1, zT, h1, 3), (fw2, h1, h2, 4)):
                            for c0 in range(0, L, TC):
                                pb = nb()
                                k.mm(ps[pb][0:64, 0:TC], w_[:, :], src[:, c0:c0 + TC], True, True, [t_c, t_h], [t_ps[pb]])
                                k.act(arg[:, 0:TC], ps[pb][0:64, 0:TC], AF.Identity, [t_ps[pb], t_c], [t_arg], scale=fv[:, 0:1], bias=fv[:, bcol:bcol + 1])
                                k.v("dve", "tensor_scalar", [t_arg], [t_arg], out=kk[:, 0:TC], in0=arg[:, 0:TC], scalar1=1.0 / (2.0 * PI), scalar2=MAGIC, op0=ALU.mult, op1=ALU.add)
                                k.v("dve", "tensor_scalar", [t_arg], [t_arg], out=kk[:, 0:TC], in0=kk[:, 0:TC], scalar1=MAGIC, scalar2=None, op0=ALU.subtract)
                                k.v("dve", "scalar_tensor_tensor", [t_arg], [t_arg], out=arg[:, 0:TC], in0=kk[:, 0:TC], scalar=-2.0 * PI, in1=arg[:, 0:TC], op0=ALU.mult, op1=ALU.add)
                                k.v("dve", "tensor_scalar", [t_arg], [t_arg], out=arg[:, 0:TC], in0=arg[:, 0:TC], scalar1=-3.141592, scalar2=3.141592, op0=ALU.max, op1=ALU.min)
                                k.act(dst[:, c0:c0 + TC], arg[:, 0:TC], AF.Sin, [t_arg], [t_h])
                        dec = sb("y_dec", [128, 1024], st=st1); fl = sb("y_fl", [128, 1024], st=st1); t_fl = Tok()
                        for nt in range(NT):
                            k.act(dec[:], adl[:], AF.Exp, [t_c], [t_fl], scale=negt[:, nt:nt + 1])
                            for hh in range(2):
                                pb = nb()
                                k.mm(ps[pb][:], h2[:, nt * 128:(nt + 1) * 128], fw3[:, hh * 512:(hh + 1) * 512], True, True, [t_h, t_c], [t_ps[pb]])
                                k.v("dve", "scalar_tensor_tensor", [t_ps[pb], t_fl], [t_fl], out=fl[:, hh * 512:(hh + 1) * 512], in0=dec[:, hh * 512:(hh + 1) * 512],
                                    scalar=0.05, in1=ps[pb][:], op0=ALU.add, op1=ALU.mult)
                            k.v("dve", "tensor_tensor", [t_fl], [t_F], out=FS[:, nt, :], in0=fl[:, 0:512], in1=fl[:, 512:1024], op=ALU.add)
                            k.v("dve", "tensor_tensor", [t_fl], [t_F], out=FD[:, nt, :], in0=fl[:, 0:512], in1=fl[:, 512:1024], op=ALU.subtract)
                            if nt == 0:
                                k.v("dve", "tensor_copy", [t_fl], [t_F], out=FS[0:1, 0, :], in_=fl[0:1, 0:512])
                                k.v("dve", "tensor_copy", [t_fl], [t_F], out=FD[0:1, 0, :], in_=fl[0:1, 0:512])
                    P.barrier()
                    if "filt" in k.dbg:
                        fdb = nc.dram_tensor("filt" + sfx, [2, 128, NT, 512], BF16, kind="ExternalOutput").ap()
                        finals.append(k.load(fdb[0], FS[:], [t_F], [])); finals.append(k.load(fdb[1], FD[:], [t_F], []))
                    with contextlib.ExitStack() as st2:
                        gTs = [sb("y_gT%d" % b, [128, NT, 512], BF16, st=st2) for b in range(NB)]; t_g = Tok()
                        for b in range(NB):
                            k.load(gTs[b][:], hgT[b, tok0:tok0 + L, :].rearrange("(tt p) c -> p tt c", p=128), [], [t_g], q="act")
                        Cf = [sb("y_Cf%d" % i, [128, NT, 128], BF16, st=st2) for i in range(2)]
                        Sf = [sb("y_Sf%d" % i, [128, NT, 128], BF16, st=st2) for i in range(2)]; t_cs = [Tok(), Tok()]
                        KA = sb("y_KA", [128, 512], st=st2); KB = sb("y_KB", [128, 512], st=st2); t_K = Tok()
                        Asb = sb("y_A", [128, 512], st=st2); t_A = Tok()
                        tq = [sb("y_t%d" % i, [128, 512], st=st2) for i in range(4)]; t_tq = Tok()
                        for ft in range(NT):
                            i2 = ft % 2
                            k.load(Cf[i2][:], din["dftC" + sfx][ft], [], [t_cs[i2]], q="sp")
                            k.load(Sf[i2][:], din["dftS" + sfx][ft], [], [t_cs[i2]], q="act")
                            pKA, pKB = nb(), nb()
                            for tt in range(NT):
                                k.mm(ps[pKA][:], Cf[i2][:, tt, :], FS[:, tt, :], tt == 0, tt == NT - 1, [t_cs[i2], t_F], [t_ps[pKA]])
                            for tt in range(NT):
                                k.mm(ps[pKB][:], Sf[i2][:, tt, :], FD[:, tt, :], tt == 0, tt == NT - 1, [t_cs[i2], t_F], [t_ps[pKB]])
                            k.act(KA[:], ps[pKA][:], AF.Copy, [t_ps[pKA]], [t_K])
                            k.act(KB[:], ps[pKB][:], AF.Copy, [t_ps[pKB]], [t_K])
                            for b in range(NB):
                                pA, pB = nb(), nb()
                                for tt in range(NT):
                                    k.mm(ps[pA][:], Cf[i2][:, tt, :], gTs[b][:, tt, :], tt == 0, tt == NT - 1, [t_cs[i2], t_g], [t_ps[pA]])
                                for tt in range(NT):
                                    k.mm(ps[pB][:], Sf[i2][:, tt, :], gTs[b][:, tt, :], tt == 0, tt == NT - 1, [t_cs[i2], t_g], [t_ps[pB]])
                                k.act(Asb[:], ps[pA][:], AF.Copy, [t_ps[pA]], [t_A])
                                k.v("dve", "tensor_tensor", [t_A, t_K], [t_tq], out=tq[0][:], in0=Asb[:], in1=KA[:], op=ALU.mult)
                                k.v("dve", "tensor_tensor", [t_ps[pB], t_K], [t_tq], out=tq[1][:], in0=ps[pB][:], in1=KB[:], op=ALU.mult)
                                k.v("pool", "tensor_tensor", [t_tq], [t_Z], out=Z1[b][:, ft, :], in0=tq[0][:], in1=tq[1][:], op=ALU.subtract)
                                k.v("dve", "tensor_tensor", [t_A, t_K], [t_tq], out=tq[2][:], in0=Asb[:], in1=KB[:], op=ALU.mult)
                                k.v("dve", "tensor_tensor", [t_ps[pB], t_K], [t_tq], out=tq[3][:], in0=ps[pB][:], in1=KA[:], op=ALU.mult)
                                k.v("pool", "tensor_tensor", [t_tq], [t_Z], out=Z2[b][:, ft, :], in0=tq[2][:], in1=tq[3][:], op=ALU.add)
                    P.barrier()
                    with contextlib.ExitStack() as st3:
                        CT = [sb("y_CT%d" % i, [128, NT, TC], BF16, st=st3) for i in range(2)]
                        ST = [sb("y_ST%d" % i, [128, NT, TC], BF16, st=st3) for i in range(2)]; t_ct = [Tok(), Tok()]
                        gc = [sb("y_gc%d" % i, [128, TC], st=st3) for i in range(2)]
                        x0 = [sb("y_x0%d" % i, [128, TC], st=st3) for i in range(2)]; t_in = [Tok(), Tok()]
                        yo = [sb("y_yo%d" % i, [128, TC], BF16, st=st3) for i in range(2)]; t_yo = [Tok(), Tok()]
                        hb = sb("y_hb", [128, 4], st=st3); t_hb = Tok()
                        k.load(hb[:], din["hy_bias_fm"][:, :], [], [t_hb])
                        n = 0
                        for ci, c0 in enumerate(range(0, L, TC)):
                            i2 = ci % 2
                            k.load(CT[i2][:], din["dftCT" + sfx][ci], [], [t_ct[i2]], q="sp")
                            k.load(ST[i2][:], din["dftST" + sfx][ci], [], [t_ct[i2]], q="act")
                            for b in range(NB):
                                for j in range(4):
                                    u2 = n % 2; n += 1
                                    k.load(gc[u2][:], hg[b, j, :, tok0 + c0:tok0 + c0 + TC], [], [t_in[u2]])
                                    k.load(x0[u2][:], hx0[b, j, :, tok0 + c0:tok0 + c0 + TC], [], [t_in[u2]], q="act")
                                    pY = nb()
                                    for ft in range(NT):
                                        k.mm(ps[pY][:, 0:TC], Z1[b][:, ft, j * 128:(j + 1) * 128], CT[i2][:, ft, :], ft == 0, False, [t_Z, t_ct[i2]], [t_ps[pY]])
                                    for ft in range(NT):
                                        k.mm(ps[pY][:, 0:TC], Z2[b][:, ft, j * 128:(j + 1) * 128], ST[i2][:, ft, :], False, ft == NT - 1, [t_Z, t_ct[i2]], [t_ps[pY]])
                                    k.v("dve", "tensor_scalar", [t_in[u2], t_hb], [t_in[u2]], out=gc[u2][:], in0=gc[u2][:], scalar1=hb[:, j:j + 1], scalar2=None, op0=ALU.mult)
                                    k.v("dve", "scalar_tensor_tensor", [t_ps[pY], t_in[u2]], [t_in[u2]], out=gc[u2][:], in0=ps[pY][:, 0:TC], scalar=2.0 / NN, in1=gc[u2][:],
                                        op0=ALU.mult, op1=ALU.add)
                                    k.v("dve", "tensor_tensor", [t_in[u2]], [t_yo[u2]], out=yo[u2][:], in0=gc[u2][:], in1=x0[u2][:], op=ALU.mult)
                                    k.load(mixD[b, j, :, tok0 + c0:tok0 + c0 + TC], yo[u2][:], [t_yo[u2]], [Tok()], q="act")
                    P.barrier()

            hyena_seg(LL, LC, "")
            hyena_seg(LC, 0, "c")
            if "mixC" in k.stop:
                return
            ffn_issue = ffn_prefetch(0) if "ffn0" not in skip else (lambda: None)
            with contextlib.ExitStack() as st:
                wo = sb("o_wo", [128, 8, D], BF16, st=st); t_w = Tok()
                k.load(wo[:], din["w_out_0"].rearrange("(r p) c -> p r c", p=128), [], [t_w], q="pool")
                ffn_issue()
                mx = [sb("o_mx%d" % i, [128, 8, 512], BF16, st=st) for i in range(2)]; t_mx = [Tok(), Tok()]
                xw = [sb("o_xw%d" % i, [128, KT, 512], st=st) for i in range(2)]; t_xw = [Tok(), Tok()]
                n = 0
                for b in range(NB):
                    for ci, (c0, W) in enumerate(CHUNKS):
                        v = b if c0 >= LC else 2
                        i2 = n % 2; n += 1
                        k.load(mx[i2][:, :, 0:W], mixD[b, :, :, c0:c0 + W].rearrange("r p t -> p r t"), [], [t_mx[i2]], q="act")
                        k.load(xw[i2][:, :, 0:W], xS[b, :, :, c0:c0 + W], [tS[b][ci]], [t_xw[i2]])
                        for kt in range(KT):
                            pY = nb()
                            for r in range(8):
                                k.mm(ps[pY][:, 0:W], wo[:, r, kt * 128:(kt + 1) * 128], mx[i2][:, r, 0:W], r == 0, r == 7, [t_w, t_mx[i2]], [t_ps[pY]])
                            k.v("dve", "scalar_tensor_tensor", [t_ps[pY], t_mods[l]], [t_xw[i2]], out=xw[i2][:, kt, 0:W], in0=ps[pY][:, 0:W],
                                scalar=mods[l][:, 16 + kt, v:v + 1], in1=xw[i2][:, kt, 0:W], op0=ALU.mult, op1=ALU.add)
                        k.load(xD[b, :, :, c0:c0 + W], xw[i2][:, :, 0:W], [t_xw[i2]], [tD[b][ci]])
                P.barrier()

        din = k.din
        for nm_, shp in (("ffn_up_%d", [D, 2 * DFF]), ("ffn_down_%d", [DFF, D]), ("ffn_conv_w_%d_fm", [128, FT, 3]), ("ffn_conv_b_%d_fm", [128, FT])):
            for l in range(2):
                k.inp(nm_ % l, shp)
        k.inp("mla_w_down_ext", [D, 768]); k.inp("mla_w_uq_ext", [384, 2048]); k.inp("mla_w_ukv", [256, 2048])
        k.inp("mla_w_o", [D, D]); k.inp("mla_q_norm_fm", [128, 3]); k.inp("mla_kv_norm_fm", [128, 2])
        k.inp("ropeC", [64, LL]); k.inp("ropeS", [64, LL])
        k.inp("w_in_0", [D, 3600]); k.inp("w_out_0", [D, D])
        k.inp("hy_conv_w_fm", [128, 12, 3]); k.inp("hy_conv_b_fm", [128, 12]); k.inp("ml_conv_w_fm", [128, 8, 3])
        k.inp("ml_gate_b_fm", [4, 4]); k.inp("ml_norm_g_fm", [128, 4]); k.inp("hy_bias_fm", [128, 4])
        k.inp("maskF", [128, 4, 512]); k.inp("maskB", [128, 4, 512]); k.inp("sel4", [4, 4, 128])
        k.inp("hy_fw1", [33, 64]); k.inp("hy_fw2", [64, 64]); k.inp("hy_fw3", [64, 1024]); k.inp("hy_fvec", [64, 3])
        k.inp("hy_absdelta", [128, 1024])
        for sfx_, L_ in (("", LL), ("c", LC)):
            k.inp("hy_z" + sfx_, [33, L_]); k.inp("hy_negt" + sfx_, [128, L_ // 128])
            nt_ = L_ // 128
            tc_ = min(512, L_)
            for nm_ in ("dftC", "dftS"):
                k.inp(nm_ + sfx_, [nt_, 128, nt_, 128], BF16)
            for nm_ in ("dftCT", "dftST"):
                k.inp(nm_ + sfx_, [L_ // tc_, 128, nt_, tc_], BF16)

        A = (xT, t_xT)
        B = (xT2, t_xT2)
        if "mix0" not in skip:
            mix0_phase(0, A[0], A[1], B[0], B[1])
        else:
            A, B = B, A
        if "ffn0" not in skip:
            ffn_phase(0, B[0], B[1], A[0], A[1], din["ffn_up_0"], din["ffn_down_0"], din["ffn_conv_w_0_fm"], din["ffn_conv_b_0_fm"])
        else:
            A, B = B, A
        if "mix1" not in skip:
            mla_phase(1, A[0], A[1], B[0], B[1])
        else:
            A, B = B, A
        if "ffn1" not in skip:
            ffn_phase(1, B[0], B[1], A[0], A[1], din["ffn_up_1"], din["ffn_down_1"], din["ffn_conv_w_1_fm"], din["ffn_conv_b_1_fm"])
        else:
            A, B = B, A
        xT, t_xT = A

        with contextlib.ExitStack() as st:
          if "final" not in skip:
              gb = sb("fin_g", [128, D], st=st)
              t_gb = Tok()
              k.load(gb[:], fnorm_in[0:1, :].to_broadcast([128, D]), [], [t_gb])
              xc = [sb("fin_xc%d" % i, [128, KT, 512], st=st) for i in range(2)]
              t_xc = [Tok(), Tok()]
              yo = [sb("fin_y%d" % i, [128, D], st=st) for i in range(2)]
              t_yo = [Tok(), Tok()]
              junk = sb("fin_junk", [128, D], st=st)
              stat = sb("fin_stat", [128, 4], st=st)
              t_stat = Tok()
              n = 0
              for b in range(NB):
                  for ci, (c0, cw) in enumerate(CHUNKS):
                      if c0 < LC:
                          continue
                      c2 = ci % 2
                      k.load(xc[c2][:], xT[b, :, :, c0:c0 + 512], [t_xT[b][ci]], [t_xc[c2]], q="sp" if c2 == 0 else "act")
                      for tt in range(cw // 128):
                          t0 = c0 + tt * 128
                          i2 = n % 2
                          pa, pb = (2 * n) % 8, (2 * n + 1) % 8
                          n += 1
                          for kt in range(KT):
                              pp = pa if kt < 4 else pb
                              k.tr(ps[pp][:, (kt % 4) * 128:(kt % 4 + 1) * 128], xc[c2][:, kt, tt * 128:(tt + 1) * 128], ident[:],
                                   [t_xc[c2], t_const], [t_ps[pp]])
                          k.act(junk[:, 0:512], ps[pa][:], AF.Square, [t_ps[pa]], [t_stat], accum_out=stat[:, 0:1])
                          k.act(junk[:, 512:1024], ps[pb][:], AF.Square, [t_ps[pb]], [t_stat], accum_out=stat[:, 1:2])
                          k.v("dve", "tensor_tensor", [t_stat], [t_stat], out=stat[:, 2:3], in0=stat[:, 0:1],
                              in1=stat[:, 1:2], op=ALU.add)
                          k.act(stat[:, 3:4], stat[:, 2:3], AF.Sqrt, [t_stat], [t_stat], scale=1.0 / D, bias=EPS)
                          k.v("dve", "reciprocal", [t_stat], [t_stat], out=stat[:, 3:4], in_=stat[:, 3:4])
                          for hh, pp in ((0, pa), (1, pb)):
                              k.v("dve", "scalar_tensor_tensor", [t_ps[pp], t_stat, t_gb], [t_yo[i2]],
                                  out=yo[i2][:, hh * 512:(hh + 1) * 512], in0=ps[pp][:], scalar=stat[:, 3:4],
                                  in1=gb[:, hh * 512:(hh + 1) * 512], op0=ALU.mult, op1=ALU.mult)
                          finals.append(k.load(out[b, t0 - LC:t0 - LC + 128, :], yo[i2][:], [t_yo[i2]], []))
              P.barrier()

        P.emit(finals)
    return k


def _fm(v, nt):
    return np.ascontiguousarray(np.asarray(v, np.float32).reshape(nt, 128).T)


def make_in_maps(inp):
    shared = {
        "ident": np.eye(128, dtype=np.float32),
        "final_norm": np.asarray(inp["final_norm"], np.float32).reshape(1, D),
    }
    per_layer = (
        (inp["ada_w_0"], inp["ada_b_0"], inp["norm_mix_0"], inp["norm_ffn_0"],
         inp["ffn_up_0"], inp["ffn_down_0"], inp["ffn_conv_w_0"], inp["ffn_conv_b_0"]),
        (inp["ada_w_1"], inp["ada_b_1"], inp["norm_mix_1"], inp["norm_ffn_1"],
         inp["ffn_up_1"], inp["ffn_down_1"], inp["ffn_conv_w_1"], inp["ffn_conv_b_1"]),
    )
    for l, (aw_, ab_, nmx_, nff_, fu_, fd_, fcw_, fcb_) in enumerate(per_layer):
        shared["ada_w_%d" % l] = np.ascontiguousarray(aw_, dtype=np.float32)
        shared["ada_b_%d_fm" % l] = _fm(ab_, 48)
        shared["norm_mix_%d_fm" % l] = _fm(nmx_, KT)
        shared["norm_ffn_%d_fm" % l] = _fm(nff_, KT)
        shared["ffn_up_%d" % l] = np.ascontiguousarray(fu_, dtype=np.float32)
        shared["ffn_down_%d" % l] = np.ascontiguousarray(fd_, dtype=np.float32)
        cwv = np.asarray(fcw_, np.float32)
        shared["ffn_conv_w_%d_fm" % l] = np.ascontiguousarray(cwv.reshape(3, FT, 128).transpose(2, 1, 0))
        shared["ffn_conv_b_%d_fm" % l] = _fm(fcb_, FT)
    perm = np.arange(64).reshape(2, 2, 16)[:, ::-1, :].reshape(64)
    wd = np.asarray(inp["mla_w_down"], np.float32)
    shared["mla_w_down_ext"] = np.ascontiguousarray(np.concatenate([wd, wd[:, 640:704][:, perm]], axis=1))
    wq = np.asarray(inp["mla_w_uq"], np.float32).reshape(384, 8, 192)
    shared["mla_w_uq_ext"] = np.ascontiguousarray(np.concatenate([wq, wq[:, :, 128:192][:, :, perm]], axis=2).reshape(384, 2048))
    shared["mla_w_ukv"] = np.ascontiguousarray(inp["mla_w_ukv"], dtype=np.float32)
    shared["mla_w_o"] = np.ascontiguousarray(inp["mla_w_o"], dtype=np.float32)
    shared["mla_q_norm_fm"] = _fm(inp["mla_q_norm"], 3)
    shared["mla_kv_norm_fm"] = _fm(inp["mla_kv_norm"], 2)
    tpos = np.arange(LL)
    inv = (10000.0 ** (-np.arange(0, 32, 2, dtype=np.float32) / 32.0)).astype(np.float32)
    ang = np.stack([(tpos // 64).astype(np.float32)[:, None] * inv, (tpos % 64).astype(np.float32)[:, None] * inv], axis=1)
    cosT = np.zeros((2, 2, 16, LL), np.float32)
    sinT = np.zeros((2, 2, 16, LL), np.float32)
    for a_ in range(2):
        for hf in range(2):
            cosT[a_, hf] = np.cos(ang[:, a_, :]).T
            sinT[a_, hf] = np.sin(ang[:, a_, :]).T * (-1.0 if hf == 0 else 1.0)
    shared["ropeC"] = np.ascontiguousarray(cosT.reshape(64, LL))
    shared["ropeS"] = np.ascontiguousarray(sinT.reshape(64, LL))
    shared["w_in_0"] = np.ascontiguousarray(inp["w_in_0"], dtype=np.float32)
    shared["w_out_0"] = np.ascontiguousarray(inp["w_out_0"], dtype=np.float32)
    shared["hy_conv_w_fm"] = np.ascontiguousarray(np.asarray(inp["hy_conv_w"], np.float32).reshape(3, 12, 128).transpose(2, 1, 0))
    shared["hy_conv_b_fm"] = _fm(inp["hy_conv_b"], 12)
    shared["ml_conv_w_fm"] = np.ascontiguousarray(np.asarray(inp["ml_conv_w"], np.float32).reshape(3, 8, 128).transpose(2, 1, 0))
    shared["ml_gate_b_fm"] = np.ascontiguousarray(np.asarray(inp["ml_gate_b"], np.float32).reshape(4, 4).T)
    shared["ml_norm_g_fm"] = _fm(inp["ml_norm_g"], 4)
    shared["hy_bias_fm"] = _fm(inp["hy_bias"], 4)
    sidx = np.arange(128)[:, None, None] + 128 * np.arange(4)[None, :, None]
    tidx = np.arange(512)[None, None, :]
    shared["maskF"] = np.where(sidx > tidx, -30000.0, 0.0).astype(np.float32)
    shared["maskB"] = np.where(sidx < tidx, -30000.0, 0.0).astype(np.float32)
    sel = np.zeros((4, 4, 128), np.float32)
    for h_ in range(4):
        sel[h_, h_, :] = 1.0
    shared["sel4"] = sel
    shared["hy_fw1"] = np.ascontiguousarray(inp["hy_fw1"], dtype=np.float32)
    shared["hy_fw2"] = np.ascontiguousarray(inp["hy_fw2"], dtype=np.float32)
    shared["hy_fw3"] = np.ascontiguousarray(inp["hy_fw3"], dtype=np.float32)
    shared["hy_fvec"] = np.ascontiguousarray(np.stack([inp["hy_freq"], inp["hy_fb1"], inp["hy_fb2"]], axis=1), dtype=np.float32)
    deltas = np.linspace(math.log(1e-2) / 0.3, math.log(1e-2) / 1.5, 512, dtype=np.float32)
    shared["hy_absdelta"] = np.ascontiguousarray(np.broadcast_to(np.abs(np.tile(deltas, 2))[None, :], (128, 1024)), dtype=np.float32)
    for sfx_, L_ in (("", LL), ("c", LC)):
        pos = np.arange(L_, dtype=np.float32)
        tg = (pos / np.float32(L_ - 1)).astype(np.float32)
        fb = np.linspace(1e-4, 15.0, 16, dtype=np.float32)
        angz = (np.float32(2.0 * math.pi / L_) * pos[:, None] * fb[None, :]).astype(np.float32)
        zz = np.concatenate([tg[:, None], np.cos(angz), -np.sin(angz)], axis=-1).astype(np.float32)
        shared["hy_z" + sfx_] = np.ascontiguousarray(zz.T)
        shared["hy_negt" + sfx_] = np.ascontiguousarray((-tg).reshape(L_ // 128, 128).T)
        th = 2.0 * np.pi * np.outer(np.arange(L_, dtype=np.float64), np.arange(L_, dtype=np.float64) + 0.5) / (2.0 * L_)
        Cm = np.cos(th).astype(np.float32); Sm = np.sin(th).astype(np.float32)
        bf = ml_dtypes.bfloat16
        nt_ = L_ // 128
        tc_ = min(512, L_)
        for nm_, M_ in (("dftC", Cm), ("dftS", Sm)):
            shared[nm_ + sfx_] = np.ascontiguousarray(M_.reshape(nt_, 128, nt_, 128).transpose(2, 1, 0, 3)).astype(bf)
            shared[nm_ + "T" + sfx_] = np.ascontiguousarray(M_.T.reshape(nt_, 128, L_ // tc_, tc_).transpose(2, 1, 0, 3)).astype(bf)
    maps = []
    x = np.asarray(inp["x"], np.float32)
    ctx = np.asarray(inp["ctx"], np.float32)
    c = np.asarray(inp["c"], np.float32)
    c_ctx = np.asarray(inp["c_ctx"], np.float32)
    for core in range(NCORES):
        b0 = core * NB
        cm = np.stack([_fm(c[b0], KT), _fm(c[b0 + 1], KT), _fm(c_ctx, KT)], axis=-1)
        m = dict(shared)
        m["x"] = np.ascontiguousarray(x[b0:b0 + NB])
        m["ctx"] = np.ascontiguousarray(ctx[b0:b0 + NB])
        m["cm"] = np.ascontiguousarray(cm)
        maps.append(m)
    return maps


_CACHE = {}


def kernel(**inputs):
    if "k" not in _CACHE:
        _CACHE["k"] = build()
    k = _CACHE["k"]
    maps = make_in_maps(inputs)
    maps = [{n: m[n] for n in k.din} for m in maps]
    res = run_bass_kernel_spmd(k.nc, maps, core_ids=list(range(NCORES)))
    return np.concatenate([np.asarray(r["out"]) for r in res.results], axis=0).astype(np.float32)
```
